# Optimizing a Trainium2 kernel written in Bass

```python
import jax, jax.numpy as jnp
from jax import lax
import numpy as np

D_MODEL = 1024
BATCH = 16
SEQ = 256
DEPTH = 2
DEC_BATCH = 2
DEC_SEQ = 2048
PAST_LEN = 512

GRID_W = 64
N_MIXERS = 2
N_POOL_LAYERS = (DEPTH + 1) // 2
N_SSD_LAYERS = DEPTH // 2
POOL_WINDOWS = (2, 4, 8, 16)
POOL_GROUPS = len(POOL_WINDOWS)
POOL_CH = D_MODEL // POOL_GROUPS
D_INNER = 2 * D_MODEL
HEAD_DIM = 64
SSD_HEADS = D_INNER // HEAD_DIM
SSD_GROUPS = 8
HEADS_PER_GROUP = SSD_HEADS // SSD_GROUPS
D_STATE = 128
D_CONV = 4
CHUNK = 128
CONV_CH = D_INNER + 2 * SSD_GROUPS * D_STATE
IN_PROJ_COLS = D_INNER + CONV_CH + 2 * SSD_HEADS
NORM_GROUP = D_INNER // SSD_GROUPS
FFN_HIDDEN = -(-8 * D_MODEL // (3 * 256)) * 256
ALPHA = float((2 * DEPTH) ** 0.25)
BETA = float((8 * DEPTH) ** -0.25)
LN_EPS = 1e-5
RMS_EPS = 1e-5

kernel_name = "hybrid_pool_ssd_diffusion_step"


def _layer_norm(x, g, b):
    xf = x.astype(jnp.float32)
    mu = jnp.mean(xf, axis=-1, keepdims=True)
    var = jnp.mean(jnp.square(xf - mu), axis=-1, keepdims=True)
    return ((xf - mu) * lax.rsqrt(var + LN_EPS) * g + b).astype(x.dtype)


def _modulation(cond, w_ada, b_ada):
    m = jax.nn.silu(cond) @ w_ada + b_ada
    return jnp.split(m[:, None, :], 6, axis=-1)


def _window_mean(x, k, axis):
    n = x.shape[axis]
    left = k // 2
    right = k - 1 - left
    pos = np.arange(n)
    lo = np.clip(pos - left, 0, n)
    hi = np.clip(pos + right + 1, 0, n)
    cshape = [1] * x.ndim
    cshape[axis] = n
    cnt = jnp.asarray((hi - lo).astype(np.float32).reshape(cshape))
    zshape = list(x.shape)
    zshape[axis] = 1
    csum = jnp.concatenate([jnp.zeros(zshape, x.dtype), jnp.cumsum(x, axis=axis)], axis=axis)
    return (jnp.take(csum, hi, axis=axis) - jnp.take(csum, lo, axis=axis)) / cnt


def _pool_mixer(h, w_pool, pool_scale, on_grid):
    b, l, d = h.shape
    hf = h.astype(jnp.float32)
    outs = []
    for g, k in enumerate(POOL_WINDOWS):
        xg = hf[..., g * POOL_CH:(g + 1) * POOL_CH]
        if on_grid:
            rows = l // GRID_W
            xgrid = xg.reshape(b, rows, GRID_W, POOL_CH)
            m = _window_mean(_window_mean(xgrid, k, 2), k, 1).reshape(b, l, POOL_CH)
        else:
            m = _window_mean(xg, k, 1)
        outs.append(m - xg)
    mixed = jnp.stack(outs, axis=2).astype(h.dtype)
    y = jnp.einsum('blgc,gcd->blgd', mixed, w_pool).reshape(b, l, d)
    return y * pool_scale


def _centred_conv(x, w, bias):
    l = x.shape[1]
    left = D_CONV // 2
    right = D_CONV - 1 - left
    xp = jnp.pad(x, ((0, 0), (left, right), (0, 0)))
    out = xp[:, 0:l] * w[0]
    for k in range(1, D_CONV):
        out = out + xp[:, k:k + l] * w[k]
    return out + bias


def _segsum_decay(a_cs):
    q = a_cs.shape[-1]
    mask = np.tril(np.ones((q, q), dtype=bool))
    diff = a_cs[..., :, None] - a_cs[..., None, :]
    return jnp.exp(jnp.where(mask, diff, -jnp.inf))


def _ssd_scan(x, dt, a, bm, cm, h0):
    b, l, nh, p = x.shape
    nc = l // CHUNK
    xs = (x * dt[..., None]).reshape(b, nc, CHUNK, SSD_GROUPS, HEADS_PER_GROUP, p)
    da = (dt * a).reshape(b, nc, CHUNK, SSD_GROUPS, HEADS_PER_GROUP)
    a_cs = jnp.moveaxis(jnp.cumsum(da, axis=2), 2, -1)
    bc = bm.reshape(b, nc, CHUNK, SSD_GROUPS, D_STATE)
    cc = cm.reshape(b, nc, CHUNK, SSD_GROUPS, D_STATE)
    decay = _segsum_decay(a_cs)
    cb = jnp.einsum('bcign,bcjgn->bcgij', cc, bc)
    y_diag = jnp.einsum('bcgij,bcgrij,bcjgrp->bcigrp', cb, decay, xs)
    decay_to_end = jnp.exp(a_cs[..., -1:] - a_cs)
    states = jnp.einsum('bcjgn,bcgrj,bcjgrp->bcgrpn', bc, decay_to_end, xs)
    chunk_decay = jnp.exp(a_cs[..., -1])

    def step(hc, inp):
        st, dec = inp
        return hc * dec[..., None, None] + st, hc

    h0g = h0.astype(jnp.float32).reshape(b, SSD_GROUPS, HEADS_PER_GROUP, p, D_STATE)
    h_final, h_starts = lax.scan(step, h0g, (jnp.moveaxis(states, 1, 0), jnp.moveaxis(chunk_decay, 1, 0)))
    h_starts = jnp.moveaxis(h_starts, 0, 1)
    y_off = jnp.einsum('bcign,bcgrpn,bcgri->bcigrp', cc, h_starts, jnp.exp(a_cs))
    y = (y_diag + y_off).reshape(b, l, nh, p)
    return y, h_final.reshape(b, nh, p, D_STATE)


def _ssd_mixer(h, w_in, conv_w, conv_b, dt_bias, a_log, d_skip, norm_w, w_out, h0_fwd, h0_bwd):
    b, l, _ = h.shape
    zxbcdt = h @ w_in
    z = zxbcdt[..., :D_INNER]
    xbc = zxbcdt[..., D_INNER:D_INNER + CONV_CH]
    dt_raw = zxbcdt[..., D_INNER + CONV_CH:]
    xbc = jax.nn.silu(_centred_conv(xbc, conv_w, conv_b)).astype(jnp.float32)
    xs = xbc[..., :D_INNER].reshape(b, l, SSD_HEADS, HEAD_DIM)
    bm = xbc[..., D_INNER:D_INNER + SSD_GROUPS * D_STATE].reshape(b, l, SSD_GROUPS, D_STATE)
    cm = xbc[..., D_INNER + SSD_GROUPS * D_STATE:].reshape(b, l, SSD_GROUPS, D_STATE)
    dt = jax.nn.softplus(dt_raw.astype(jnp.float32).reshape(b, l, 2, SSD_HEADS) + dt_bias.astype(jnp.float32))
    a = -jnp.exp(a_log.astype(jnp.float32))
    y_f, hf = _ssd_scan(xs, dt[:, :, 0], a[0], bm, cm, h0_fwd)
    y_b, hb = _ssd_scan(xs[:, ::-1], dt[:, ::-1, 1], a[1], bm[:, ::-1], cm[:, ::-1], h0_bwd)
    y = y_f + y_b[:, ::-1] + d_skip.astype(jnp.float32)[:, None] * xs
    y = y.reshape(b, l, D_INNER) * jax.nn.silu(z.astype(jnp.float32))
    yg = y.reshape(b, l, SSD_GROUPS, NORM_GROUP)
    yg = yg * lax.rsqrt(jnp.mean(jnp.square(yg), axis=-1, keepdims=True) + RMS_EPS)
    y = (yg.reshape(b, l, D_INNER) * norm_w).astype(h.dtype)
    return y @ w_out, hf, hb


def _swiglu(h, w_in, w_out):
    gu = h @ w_in
    return (jax.nn.silu(gu[..., :FFN_HIDDEN]) * gu[..., FFN_HIDDEN:]) @ w_out


def _trunk(x, cond, on_grid, ssm_init, w_ada, b_ada, w_pool, pool_scale, ssd_w_in, ssd_conv_w, ssd_conv_b,
           ssd_dt_bias, ssd_a_log, ssd_d, ssd_norm_w, ssd_w_out, ffn_w_in, ffn_w_out, ln1_g, ln1_b, ln2_g, ln2_b):
    new_states = []
    for i in range(DEPTH):
        sh1, sc1, g1, sh2, sc2, g2 = _modulation(cond, w_ada[i], b_ada[i])
        h = x * (1 + sc1) + sh1
        j = i // N_MIXERS
        if i % N_MIXERS == 0:
            mix = _pool_mixer(h, w_pool[j], pool_scale[j], on_grid)
        else:
            mix, hf, hb = _ssd_mixer(h, ssd_w_in[j], ssd_conv_w[j], ssd_conv_b[j], ssd_dt_bias[j], ssd_a_log[j],
                                     ssd_d[j], ssd_norm_w[j], ssd_w_out[j], ssm_init[:, j, 0], ssm_init[:, j, 1])
            new_states.append(jnp.stack([hf, hb], axis=1))
        x = _layer_norm(ALPHA * x + g1 * mix, ln1_g[i], ln1_b[i])
        h = x * (1 + sc2) + sh2
        x = _layer_norm(ALPHA * x + g2 * _swiglu(h, ffn_w_in[i], ffn_w_out[i]), ln2_g[i], ln2_b[i])
    return x, jnp.stack(new_states, axis=1)


def setup_inputs(seed: int = 0) -> dict:
    key = jax.random.key(seed)
    ks = jax.random.split(key, 24)
    f32 = jnp.float32
    nrm = lambda k, shape, s: jax.random.normal(k, shape, f32) * s
    dt0 = jnp.exp(jax.random.uniform(ks[11], (N_SSD_LAYERS, 2, SSD_HEADS), f32, np.log(1e-3), np.log(1e-1)))
    return {
        "x_prompt": nrm(ks[0], (BATCH, SEQ, D_MODEL), 1.0),
        "x_sample": nrm(ks[1], (DEC_BATCH, DEC_SEQ, D_MODEL), 1.0),
        "c": nrm(ks[2], (DEC_BATCH, D_MODEL), 1.0),
        "state_ssm": nrm(ks[3], (DEC_BATCH, N_SSD_LAYERS, 2, SSD_HEADS, HEAD_DIM, D_STATE), 0.5),
        "c_ctx": nrm(ks[4], (D_MODEL,), 1.0),
        "w_ada": nrm(ks[5], (DEPTH, D_MODEL, 6 * D_MODEL), 0.5 * D_MODEL ** -0.5),
        "b_ada": nrm(ks[6], (DEPTH, 6 * D_MODEL), 0.02),
        "w_pool": nrm(ks[7], (N_POOL_LAYERS, POOL_GROUPS, POOL_CH, POOL_CH), BETA * POOL_CH ** -0.5),
        "pool_scale": 1.0 + nrm(ks[8], (N_POOL_LAYERS, D_MODEL), 0.02),
        "ssd_w_in": nrm(ks[9], (N_SSD_LAYERS, D_MODEL, IN_PROJ_COLS), D_MODEL ** -0.5),
        "ssd_conv_w": nrm(ks[10], (N_SSD_LAYERS, D_CONV, CONV_CH), D_CONV ** -0.5),
        "ssd_conv_b": nrm(ks[12], (N_SSD_LAYERS, CONV_CH), 0.02),
        "ssd_dt_bias": dt0 + jnp.log(-jnp.expm1(-dt0)),
        "ssd_a_log": jnp.log(jax.random.uniform(ks[13], (N_SSD_LAYERS, 2, SSD_HEADS), f32, 1.0, 16.0)),
        "ssd_d": 1.0 + nrm(ks[14], (N_SSD_LAYERS, SSD_HEADS), 0.02),
        "ssd_norm_w": 1.0 + nrm(ks[15], (N_SSD_LAYERS, D_INNER), 0.02),
        "ssd_w_out": nrm(ks[16], (N_SSD_LAYERS, D_INNER, D_MODEL), BETA * D_INNER ** -0.5),
        "ffn_w_in": nrm(ks[17], (DEPTH, D_MODEL, 2 * FFN_HIDDEN), D_MODEL ** -0.5),
        "ffn_w_out": nrm(ks[18], (DEPTH, FFN_HIDDEN, D_MODEL), BETA * FFN_HIDDEN ** -0.5),
        "ln1_g": 1.0 + nrm(ks[19], (DEPTH, D_MODEL), 0.02),
        "ln1_b": nrm(ks[20], (DEPTH, D_MODEL), 0.02),
        "ln2_g": 1.0 + nrm(ks[21], (DEPTH, D_MODEL), 0.02),
        "ln2_b": nrm(ks[22], (DEPTH, D_MODEL), 0.02),
    }


def reference(x_prompt, x_sample, c, state_ssm, c_ctx, w_ada, b_ada, w_pool, pool_scale, ssd_w_in, ssd_conv_w,
              ssd_conv_b, ssd_dt_bias, ssd_a_log, ssd_d, ssd_norm_w, ssd_w_out, ffn_w_in, ffn_w_out,
              ln1_g, ln1_b, ln2_g, ln2_b):
    ctx_init = jnp.zeros((x_prompt.shape[0], N_SSD_LAYERS, 2, SSD_HEADS, HEAD_DIM, D_STATE), jnp.float32)
    y_prompt, new_state_ssm = _trunk(
        x_prompt, c_ctx[None, :], False, ctx_init, w_ada, b_ada, w_pool, pool_scale, ssd_w_in, ssd_conv_w,
        ssd_conv_b, ssd_dt_bias, ssd_a_log, ssd_d, ssd_norm_w, ssd_w_out, ffn_w_in, ffn_w_out,
        ln1_g, ln1_b, ln2_g, ln2_b)
    y_sample, _ = _trunk(
        x_sample, c, True, state_ssm, w_ada, b_ada, w_pool, pool_scale, ssd_w_in, ssd_conv_w,
        ssd_conv_b, ssd_dt_bias, ssd_a_log, ssd_d, ssd_norm_w, ssd_w_out, ffn_w_in, ffn_w_out,
        ln1_g, ln1_b, ln2_g, ln2_b)
    return (y_prompt, y_sample, new_state_ssm)
```

```python
import numpy as np
from contextlib import ExitStack
import concourse.bass as bass
import concourse.mybir as mybir
from concourse.bass_utils import run_bass_kernel_spmd

F32 = mybir.dt.float32
BF16 = mybir.dt.bfloat16
AF = mybir.ActivationFunctionType
ALU = mybir.AluOpType

D = 1024
NT = 8
FF = 2816
NFC = FF // 128
DI = 2048
CONV_CH = 4096
NCOLS_IN = 6208
ALPHA = float(4 ** 0.25)
LN_EPS = 1e-5
RMS_EPS = 1e-5
POOL_K = (2, 4, 8, 16)
P2_TILES = {0: (3, 8), 1: (3, 9), 2: (2, 10), 3: (0, 12)}
P2_BLK0 = {0: 0, 1: 5, 2: 11, 3: 19}
NP2 = 31
GROUPS4 = [[0, 1, 2, 3], [4, 5, 6, 7]]
SEQS = [([0, 1], False), ([2, 3], False), ([4, 5, 6, 7], True)]
NEG = -30000.0


class Sched:
    ENG = ("pe", "act", "dve", "pool", "sp")

    def __init__(self, nc):
        self.nc = nc
        self.ops = []
        self.per_eng = {e: [] for e in self.ENG}
        self.last_w = {}
        self.readers = {}

    def fence(self):
        self._add("dve", "c", "memset", dict(ap=self.fence_tile, constant=0.0), (), ("__fence__",), fence=True)

    def _add(self, eng, kind, name, kwargs, reads, writes, fence=False):
        if not fence:
            def _isps(n):
                return len(n) >= 3 and n[:2] == "ps" and n[2].isdigit()
            psr = tuple(r[:3] for r in reads if _isps(r))
            reads = tuple(r for r in reads if not _isps(r))
            writes = tuple(dict.fromkeys(tuple((w[:3] if _isps(w) else w) for w in writes) + psr))
            reads = tuple(reads) + ("__fence__",)
        oid = len(self.ops)
        deps = set()
        if fence:
            for e in self.ENG:
                cs = [o for o in self.per_eng[e] if o["kind"] == "c"]
                if cs:
                    deps.add(cs[-1]["id"])
                ds = [o for o in self.per_eng[e] if o["kind"] != "c"]
                for o in ds[-16:]:
                    deps.add(o["id"])
            reads, writes_ = (), writes
        for r in reads:
            if r in self.last_w:
                deps.add(self.last_w[r])
        for w in writes:
            if w in self.last_w:
                deps.add(self.last_w[w])
            if not fence:
                for rd in self.readers.get(w, ()):
                    deps.add(rd)
        deps.discard(oid)
        op = dict(id=oid, eng=eng, kind=kind, name=name, kw=kwargs, deps=deps, needed=False)
        self.ops.append(op)
        self.per_eng[eng].append(op)
        for r in reads:
            self.readers.setdefault(r, []).append(oid)
        for w in writes:
            self.last_w[w] = oid
            self.readers[w] = []
        return oid

    def op(self, eng, name, reads=(), writes=(), **kw):
        return self._add(eng, "c", name, kw, tuple(reads), tuple(writes))

    def dma(self, eng, out, in_, reads=(), writes=()):
        return self._add(eng, "d", "dma_start", dict(out=out, in_=in_), tuple(reads), tuple(writes))

    def cc(self, reads, writes, **kw):
        return self._add("pool", "cc", "collective_compute", kw, tuple(reads), tuple(writes))

    def emit(self, es):
        nc = self.nc
        ops = self.ops
        for op in ops:
            for d in op["deps"]:
                p = ops[d]
                if p["kind"] == "c":
                    if p["eng"] == "pe" and op["eng"] == "pe" and op["kind"] == "c":
                        continue
                    p["needed"] = True
        M = 16
        csem = {e: es.enter_context(nc.semaphore("c_" + e)) for e in ("pe", "act", "dve", "pool")}
        dsem = {e: [es.enter_context(nc.semaphore(f"d_{e}{i}")) for i in range(M)] for e in ("pool", "sp")}
        ccsem = [es.enter_context(nc.semaphore(f"ccs{i}")) for i in range(M)]
        ccnt = {e: 0 for e in csem}
        dcnt = {e: 0 for e in dsem}
        ncc = 0
        for e in self.ENG:
            for op in self.per_eng[e]:
                if op["kind"] == "c":
                    if op["needed"]:
                        ccnt[e] += 1
                        op["sem"] = ("c", e, ccnt[e])
                    else:
                        op["sem"] = None
                elif op["kind"] == "d":
                    i = dcnt[e]
                    dcnt[e] += 1
                    op["sem"] = ("d", (e, i % M), 16 * (i // M + 1))
                else:
                    i = ncc
                    ncc += 1
                    op["sem"] = ("cc", i % M, i // M + 1)
        import os as _os3
        if _os3.environ.get("KDEBUG"):
            print("SCHED counts", ccnt, dcnt, ncc, {e: len(v) for e, v in self.per_eng.items()})
        handles = {"pe": nc.tensor, "act": nc.scalar, "dve": nc.vector, "pool": nc.gpsimd, "sp": nc.sync}
        block = es.enter_context(nc.Block())

        def semof(key):
            if key[0] == "c":
                return csem[key[1]]
            if key[0] == "d":
                return dsem[key[1][0]][key[1][1]]
            return ccsem[key[1]]

        def run(e):
            h = handles[e]
            waited = {}
            for op in self.per_eng[e]:
                need = {}
                for d in op["deps"]:
                    p = ops[d]
                    if p["kind"] == "c" and p["eng"] == "pe" and e == "pe" and op["kind"] == "c":
                        continue
                    s = p["sem"]
                    if s is None:
                        continue
                    key = (s[0], s[1])
                    need[key] = max(need.get(key, 0), s[2])
                s = op["sem"]
                if s is not None and s[0] == "d" and s[2] > 16:
                    key = (s[0], s[1])
                    need[key] = max(need.get(key, 0), s[2] - 16)
                if s is not None and s[0] == "cc" and s[2] > 1:
                    key = (s[0], s[1])
                    need[key] = max(need.get(key, 0), s[2] - 1)
                for key, v in need.items():
                    if waited.get(key, 0) < v:
                        h.wait_ge(semof(key), v)
                        waited[key] = v
                ins = getattr(h, op["name"])(**op["kw"])
                if s is not None:
                    if s[0] == "c":
                        ins.then_inc(csem[s[1]], 1)
                    elif s[0] == "d":
                        ins.then_inc(semof((s[0], s[1])), 16)
                    else:
                        ins.then_inc(ccsem[s[1]])
            if e == "sp":
                for q in dsem:
                    for i in range(min(M, dcnt[q])):
                        n_i = (dcnt[q] - 1 - i) // M + 1
                        h.wait_ge(dsem[q][i], 16 * n_i)
                for i in range(min(M, ncc)):
                    h.wait_ge(ccsem[i], (ncc - 1 - i) // M + 1)

        @block.tensor
        def _(t):
            run("pe")

        @block.scalar
        def _(t):
            run("act")

        @block.vector
        def _(t):
            run("dve")

        @block.gpsimd
        def _(t):
            run("pool")

        @block.sync
        def _(t):
            run("sp")


class _Stop(Exception):
    pass


def build_program(stop=99):
    nc = bass.Bass("TRN2", target_bir_lowering=False)
    S = Sched(nc)

    def din(name, shape):
        return nc.dram_tensor(name, list(shape), F32, kind="ExternalInput").ap()

    xin = din("xin", [1024, D])
    xext = din("xext", [1536, D])
    cvec = din("cvecT", [128, 8, 2])
    h0T = din("h0T", [2, 128, DI])
    w_ada = din("w_ada", [2, D, 6 * D])
    b_ada = din("b_ada", [2, 6 * D])
    w_pool = din("w_pool", [1, 4, 256, 256])
    pool_scale = din("pool_scale", [1, D])
    w_in = din("ssd_w_in", [1, D, NCOLS_IN])
    dt_bias = din("ssd_dt_bias", [1, 2, 32])
    a_log = din("ssd_a_log", [1, 2, 32])
    ssd_d = din("ssd_d", [1, 32])
    norm_w = din("ssd_norm_w", [1, DI])
    w_out = din("ssd_w_out", [1, DI, D])
    ffn_w_in = din("ffn_w_in", [2, D, 2 * FF])
    ffn_w_out = din("ffn_w_out", [2, FF, D])
    ln_in = {k: din(k, [2, D]) for k in ("ln1_g", "ln1_b", "ln2_g", "ln2_b")}
    lncT = din("lncT", [128, 8, 8])
    cwl = din("cwl", [128, 32, 5])
    cm = din("cmats", [6, 128, 128])
    p1 = din("p1", [8, 128, 256])
    p2 = din("p2", [NP2, 128, 512])
    invc = din("invc", [1, 4 * 768])
    cmask = din("cmask", [1, 19])
    sel = din("sel", [12, 3])

    y_out = nc.dram_tensor("y_out", [1024, D], F32, kind="ExternalOutput").ap()
    st_out = nc.dram_tensor("st_out", [2, 2, 128, DI], F32, kind="ExternalOutput").ap()
    ag1_src = nc.dram_tensor("ag1_src", [3, D], F32).ap()
    ag1_dst = nc.dram_tensor("ag1_dst", [12, D], F32).ap()
    ag2_src = [nc.dram_tensor(f"ag2_src{g}", [256, 260], F32).ap() for g in range(8)]
    ag2_dst = [nc.dram_tensor(f"ag2_dst{g}", [1024, 260], F32).ap() for g in range(8)]

    with ExitStack() as es:
        def sb(name, shape, dt=F32):
            return es.enter_context(nc.sbuf_tensor(name, list(shape), dt))

        x = sb("x", [128, NT, D])
        hT = sb("hT", [128, 8, 1024], BF16)
        gate_bc = sb("gate_bc", [128, 2, 2, D])
        lnbc = sb("lnbc", [128, 2, D])
        cmf = sb("cmf", [128, 6, 128])
        cmb = sb("cmb", [128, 6, 128], BF16)
        modc = sb("modc", [128, 2, 4, 8, 2])
        lnc = sb("lnc", [128, 8, 8])
        AB = sb("AB", [128, 2, 2, 8])
        siluc = sb("siluc", [128, 8, 2], BF16)
        cT = sb("cT", [128, 8, 2])
        ones_row = sb("ones_row", [1, 128], BF16)
        epsc = sb("epsc", [128, 2])
        maskbc = sb("maskbc", [128, 19])
        stat = sb("stat", [128, 32])
        bnst = sb("bnst", [128, 2, 6])
        pre = sb("pre", [128, D])
        xnbf = sb("xnbf", [128, D], BF16)
        fence_t = sb("fence_t", [128, 2])
        S.fence_tile = fence_t[:]
        wab = [None, None]
        bab = [None, None]
        nwab = [1]
        ARENA = 122 * 1024 // 4
        arena = sb("arena", [128, ARENA])
        apos = [0]

        def carve(nbytes, dt, shape_str=None, **kw):
            n4 = (nbytes + 3) // 4
            a = arena[:, apos[0]:apos[0] + n4]
            apos[0] += n4
            assert apos[0] <= ARENA, ("arena overflow", apos[0] * 4)
            if dt == BF16:
                a = a.bitcast(BF16)
            if shape_str:
                a = a.rearrange(shape_str, **kw)
            return a

        ps = [es.enter_context(nc.psum_tensor(f"ps{i}", [128, 512], F32)) for i in range(8)]

        def psbf(i):
            return ps[i][:, :].bitcast(BF16)

        ident_f = cmf[:, 0, :]
        ident_b = cmb[:, 0, :]
        ones_f = cmf[:, 5, :]

        for t in range(NT):
            S.dma("sp", x[:, t, :], xin[t * 128:(t + 1) * 128, :], writes=[f"x{t}"])
        S.dma("sp", cmf[:], cm.rearrange("c p f -> p c f"), writes=["cmf"])
        S.dma("pool", cmb[:], cm.rearrange("c p f -> p c f"), writes=["cmb"])
        S.dma("sp", cT[:], cvec, writes=["cT"])
        S.dma("sp", lnc[:], lncT, writes=["lnc"])
        S.dma("sp", maskbc[:], cmask[0, :].partition_broadcast(128), writes=["maskbc"])
        S.op("dve", "memset", writes=["ones_row"], ap=ones_row[:], constant=1.0)
        S.op("dve", "memset", writes=["epsc"], ap=epsc[:, 0:1], constant=LN_EPS)
        S.op("dve", "memset", writes=["epsc"], ap=epsc[:, 1:2], constant=RMS_EPS)
        S.op("act", "activation", reads=["cT"], writes=["siluc"], out=siluc[:], in_=cT[:], func=AF.Silu)

        wab_i = [0]

        def mod_halfblock(l, nb, hf):
            i = wab_i[0] % nwab[0]
            wab_i[0] += 1
            c0 = nb * 1024 + hf * 512
            S.dma("pool", wab[i][:], w_ada[l, :, c0:c0 + 512].rearrange("(kc p) n -> p kc n", p=128),
                  writes=[f"wab{i}"])
            S.dma("pool", bab[i][0:1, :], b_ada[l:l + 1, c0:c0 + 512], writes=[f"bab{i}"])
            if nb in (2, 5):
                which = 0 if nb == 2 else 1
                for c in range(2):
                    for kc in range(8):
                        S.op("pe", "matmul", reads=["siluc", f"wab{i}"], writes=["ps6"],
                             out=ps[6][:, :], lhsT=siluc[:, kc, c:c + 1].to_broadcast([128, 128]),
                             rhs=wab[i][:, kc, :], start=(kc == 0), stop=False)
                    S.op("pe", "matmul", reads=["ones_row", f"bab{i}"], writes=["ps6"],
                         out=ps[6][:, :], lhsT=ones_row[0:1, :], rhs=bab[i][0:1, :], start=False, stop=True)
                    S.op("act", "activation", reads=["ps6"], writes=[f"gate{which}{c}"],
                         out=gate_bc[:, which, c, hf * 512:(hf + 1) * 512], in_=ps[6][:, :], func=AF.Identity)
            else:
                v = {0: 0, 1: 1, 3: 2, 4: 3}[nb]
                for cc in range(4):
                    kcx = hf * 4 + cc
                    o = ps[7][:, (v * 8 + kcx) * 2:(v * 8 + kcx) * 2 + 2]
                    for kc in range(8):
                        S.op("pe", "matmul", reads=["siluc", f"wab{i}"], writes=["ps7"],
                             out=o, lhsT=wab[i][:, kc, cc * 128:(cc + 1) * 128], rhs=siluc[:, kc, :],
                             start=(kc == 0), stop=False)
                    S.op("pe", "matmul", reads=["ones_row", f"bab{i}"], writes=["ps7"],
                         out=o, lhsT=bab[i][0:1, cc * 128:(cc + 1) * 128], rhs=ones_row[0:1, 0:2],
                         start=False, stop=True)

        def mod_finish(l):
            S.op("dve", "tensor_copy", reads=["ps7"], writes=[f"modc{l}"],
                 out=modc[:, l].rearrange("p v k c -> p (v k c)"), in_=ps[7][:, 0:64])

        def make_AB(l_ln, which_ln, l_mod, vsh, vsc):
            g_i = l_ln * 4 + (0 if which_ln == 1 else 2)
            for c in range(2):
                S.op("dve", "scalar_tensor_tensor", reads=[f"modc{l_mod}", "lnc"], writes=["AB"],
                     out=AB[:, 0, c, :], in0=modc[:, l_mod, vsc, :, c], scalar=1.0, in1=lnc[:, g_i, :],
                     op0=ALU.add, op1=ALU.mult)
                S.op("dve", "scalar_tensor_tensor", reads=[f"modc{l_mod}", "lnc"], writes=["AB"],
                     out=AB[:, 1, c, :], in0=modc[:, l_mod, vsc, :, c], scalar=1.0, in1=lnc[:, g_i + 1, :],
                     op0=ALU.add, op1=ALU.mult)
                S.op("dve", "tensor_tensor", reads=["AB", f"modc{l_mod}"], writes=["AB"],
                     out=AB[:, 1, c, :], in0=AB[:, 1, c, :], in1=modc[:, l_mod, vsh, :, c], op=ALU.add)

        def load_lnbc(l, which):
            kg = "ln1_g" if which == 1 else "ln2_g"
            kb = "ln1_b" if which == 1 else "ln2_b"
            S.dma("sp", lnbc[:, 0, :], ln_in[kg][l, :].partition_broadcast(128), writes=["lnbc"])
            S.dma("sp", lnbc[:, 1, :], ln_in[kb][l, :].partition_broadcast(128), writes=["lnbc"])

        def ln_epilogue(t, sub_reads, sub_parts, which_gate, do_T, tp_bank, out_dram=None):
            c = 0 if t < 4 else 1
            gname = f"gate{which_gate}{c}"
            for (src, lo, hi) in sub_parts:
                S.op("dve", "tensor_tensor", reads=list(sub_reads) + [gname], writes=["pre"],
                     out=pre[:, lo:hi], in0=src, in1=gate_bc[:, which_gate, c, lo:hi], op=ALU.mult)
            S.op("dve", "scalar_tensor_tensor", reads=["pre", f"x{t}"], writes=["pre"],
                 out=pre[:], in0=x[:, t, :], scalar=ALPHA, in1=pre[:], op0=ALU.mult, op1=ALU.add)
            for hh in range(2):
                S.op("dve", "bn_stats", reads=["pre"], writes=["bnst"], out=bnst[:, hh, :],
                     in_=pre[:, hh * 512:(hh + 1) * 512])
            S.op("dve", "bn_aggr", reads=["bnst"], writes=["stat"], out=stat[:, 0:2],
                 in_=bnst[:].rearrange("p a b -> p (a b)"))
            S.op("act", "activation", reads=["stat", "epsc"], writes=["stat_sd"], out=stat[:, 2:3],
                 in_=stat[:, 1:2], func=AF.Sqrt, bias=epsc[:, 0:1], scale=1.0)
            S.op("dve", "reciprocal", reads=["stat_sd"], writes=["stat_r"], out=stat[:, 3:4], in_=stat[:, 2:3])
            S.op("dve", "scalar_tensor_tensor", reads=["stat", "stat_r"], writes=["stat_n"],
                 out=stat[:, 4:5], in0=stat[:, 0:1], scalar=-1.0, in1=stat[:, 3:4], op0=ALU.mult, op1=ALU.mult)
            S.op("act", "activation", reads=["pre", "stat_r", "stat_n"], writes=["pre"], out=pre[:],
                 in_=pre[:], func=AF.Identity, bias=stat[:, 4:5], scale=stat[:, 3:4])
            S.op("pool", "tensor_tensor", reads=["pre", "lnbc"], writes=[f"x{t}"], out=x[:, t, :],
                 in0=pre[:], in1=lnbc[:, 0, :], op=ALU.mult)
            S.op("pool", "tensor_tensor", reads=[f"x{t}", "lnbc"], writes=[f"x{t}"], out=x[:, t, :],
                 in0=x[:, t, :], in1=lnbc[:, 1, :], op=ALU.add)
            if out_dram is not None:
                S.dma("sp", out_dram, x[:, t, :], reads=[f"x{t}"], writes=[f"yout{t}"])
            if do_T:
                S.op("act", "activation", reads=["pre"], writes=["xnbf"], out=xnbf[:], in_=pre[:],
                     func=AF.Identity)
                pb = psbf(tp_bank)
                for kc in range(8):
                    S.op("pe", "transpose", reads=["xnbf", "cmb"], writes=[f"ps{tp_bank}"],
                         out=pb[:, kc * 128:(kc + 1) * 128], in_=xnbf[:, kc * 128:(kc + 1) * 128],
                         identity=ident_b)
                for kc in range(8):
                    if kc % 2 == 0:
                        S.op("act", "activation", reads=[f"ps{tp_bank}", "AB"], writes=[f"hT{t}"],
                             out=hT[:, kc, t * 128:(t + 1) * 128], in_=pb[:, kc * 128:(kc + 1) * 128],
                             func=AF.Identity, scale=AB[:, 0, c, kc:kc + 1], bias=AB[:, 1, c, kc:kc + 1])
                    else:
                        S.op("dve", "tensor_scalar", reads=[f"ps{tp_bank}", "AB"], writes=[f"hT{t}"],
                             out=hT[:, kc, t * 128:(t + 1) * 128], in0=pb[:, kc * 128:(kc + 1) * 128],
                             scalar1=AB[:, 0, c, kc:kc + 1], scalar2=AB[:, 1, c, kc:kc + 1],
                             op0=ALU.mult, op1=ALU.add)

        if stop == 0:
            S.op("act", "activation", reads=["cmf"], writes=["x0"], out=x[:, 0, 0:128], in_=cmf[:, 1, :], func=AF.Identity)
            for t_ in range(NT):
                S.dma("sp", y_out[t_ * 128:(t_ + 1) * 128, :], x[:, t_, :], reads=[f"x{t_}"], writes=[f"yout{t_}"])
            S.emit(es)
            return nc
        apos[0] = 0
        for i2 in range(2):
            wab[i2] = carve(8 * 1024, BF16, "p (k n) -> p k n", k=8)
            bab[i2] = carve(1024, BF16)
        nwab[0] = 2
        S.op("dve", "memset", writes=["ps7"], ap=ps[7][:, 0:64], constant=0.0)
        for nb in range(1, 6):
            for hf in range(2):
                mod_halfblock(0, nb, hf)
        mod_finish(0)
        xbf = carve(16 * 2048, BF16, "p (t d) -> p t d", t=16)
        p1b = carve(8 * 512, BF16, "p (b n) -> p b n", b=8)
        p2b = carve(NP2 * 1024, BF16, "p (b n) -> p b n", b=NP2)
        wpb = carve(8 * 512, BF16, "p (b n) -> p b n", b=8)
        mixT = carve(8 * 2048, BF16, "p (c n) -> p c n", c=8)
        invbc = carve(4 * 768 * 4, F32, "p (g n) -> p g n", g=4)
        psbc = carve(4096, F32)
        for t in range(4):
            S.dma("pool", xbf[:, t, :], xin[t * 128:(t + 1) * 128, :], writes=[f"xbf{t}"])
        for t in range(12):
            S.dma("pool", xbf[:, 4 + t, :], xext[t * 128:(t + 1) * 128, :], writes=[f"xbf{4 + t}"])
        S.dma("pool", p1b[:], p1.rearrange("b p n -> p b n"), writes=["p1b"])
        S.dma("pool", p2b[:], p2.rearrange("b p n -> p b n"), writes=["p2b"])
        S.dma("pool", wpb[:], w_pool[0].rearrange("g (c p) n -> p (g c) n", p=128), writes=["wpb"])
        S.dma("sp", invbc[:].rearrange("p g n -> p (g n)"), invc[0, :].partition_broadcast(128), writes=["invbc"])
        S.dma("sp", psbc, pool_scale[0, :].partition_broadcast(128), writes=["psbc"])
        for c in range(2):
            S.op("dve", "tensor_tensor", reads=["gate0%d" % c, "psbc"], writes=["gate0%d" % c],
                 out=gate_bc[:, 0, c, :], in0=gate_bc[:, 0, c, :], in1=psbc, op=ALU.mult)
        sc1p = sb("sc1p", [128, 8, 2])
        S.op("dve", "tensor_scalar", reads=["modc0"], writes=["sc1p"], out=sc1p[:], in0=modc[:, 0, 1, :, :],
             scalar1=1.0, scalar2=None, op0=ALU.add)
        make_AB(0, 1, 0, 2, 3)
        load_lnbc(0, 1)
        pbank = [0]
        for si, (tiles, ext) in enumerate(SEQS):
            c = 1 if ext else 0
            nout = 512 if ext else 256
            o0 = tiles[0] * 128
            for g in range(4):
                if ext:
                    lo, hi = P2_TILES[g]
                    srcs = [(4 + tin, p2b[:, P2_BLK0[g] + tin - lo, :], "p2b") for tin in range(lo, hi)]
                    inv = invbc[:, g, 256:768]
                else:
                    srcs = [(tiles[0] + j, p1b[:, g * 2 + j, :], "p1b") for j in range(2)]
                    inv = invbc[:, g, 0:256]
                for cc in range(2):
                    ch = g * 2 + cc
                    bk = pbank[0] % 2
                    pbank[0] += 1
                    for n, (xt, blk, bn) in enumerate(srcs):
                        S.op("pe", "matmul", reads=[f"xbf{xt}", bn], writes=[f"ps{bk}"],
                             out=ps[bk][:, 0:nout], lhsT=xbf[:, xt, ch * 128:(ch + 1) * 128], rhs=blk,
                             start=(n == 0), stop=(n == len(srcs) - 1))
                    S.op("dve", "scalar_tensor_tensor", reads=[f"ps{bk}", "sc1p", "invbc"],
                         writes=[f"mixT{ch}_{si}"], out=mixT[:, ch, o0:o0 + nout], in0=ps[bk][:, 0:nout],
                         scalar=sc1p[:, ch, c:c + 1], in1=inv, op0=ALU.mult, op1=ALU.mult)
        for t in range(NT):
            si = 0 if t < 2 else (1 if t < 4 else 2)
            b0 = 2 + 2 * (t % 2)
            for g in range(4):
                for cc in range(2):
                    ch = g * 2 + cc
                    bk = b0 + (g // 2)
                    S.op("pe", "matmul", reads=[f"mixT{ch}_{si}", "wpb"], writes=[f"ps{bk}"],
                         out=ps[bk][:, (g % 2) * 256:(g % 2) * 256 + 256], lhsT=mixT[:, ch, t * 128:(t + 1) * 128],
                         rhs=wpb[:, ch, :], start=(cc == 0), stop=(cc == 1))
            ln_epilogue(t, [f"ps{b0}", f"ps{b0 + 1}"], [(ps[b0][:, :], 0, 512), (ps[b0 + 1][:, :], 512, 1024)],
                        0, True, 6)

        if stop == 1:
            for t_ in range(NT):
                S.dma("sp", y_out[t_ * 128:(t_ + 1) * 128, :], x[:, t_, :], reads=[f"x{t_}"], writes=[f"yout{t_}"])
            S.emit(es)
            return nc
        def ffn(l, mid_hook, post_AB, final, pre_hook=None):
            S.fence()
            apos[0] = 0
            hid = carve(NFC * 2048, BF16, "p (f n) -> p f n", f=NFC)
            wo = carve(2 * NFC * 1024, BF16, "p (h f n) -> p h f n", h=2, f=NFC)
            wi = [carve(8 * 512, BF16, "p (k n) -> p k n", k=8) for _ in range(3)]
            sg = [carve(2048, F32) for _ in range(2)]
            for i2 in range(2):
                wab[i2] = carve(8 * 1024, BF16, "p (k n) -> p k n", k=8)
                bab[i2] = carve(1024, BF16)
            nwab[0] = 2
            if pre_hook is not None:
                pre_hook()
            load_lnbc(l, 2)
            for fc in range(NFC):
                i = fc % 3
                S.dma("pool", wi[i][:, :, 0:128],
                      ffn_w_in[l, :, fc * 128:(fc + 1) * 128].rearrange("(kc p) n -> p kc n", p=128),
                      writes=[f"wi{i}"])
                S.dma("pool", wi[i][:, :, 128:256],
                      ffn_w_in[l, :, FF + fc * 128:FF + (fc + 1) * 128].rearrange("(kc p) n -> p kc n", p=128),
                      writes=[f"wi{i}"])
                S.dma("pool", wo[:, :, fc, :],
                      ffn_w_out[l, fc * 128:(fc + 1) * 128, :].rearrange("p (h n) -> p h n", h=2),
                      writes=[f"wo_{fc}"])
                for th in range(2):
                    pg = (fc * 2 + th) % 2
                    bg, bu = pg * 2, pg * 2 + 1
                    hreads = [f"hT{t}" for t in range(th * 4, th * 4 + 4)]
                    for kc in range(8):
                        S.op("pe", "matmul", reads=hreads + [f"wi{i}"], writes=[f"ps{bg}"],
                             out=ps[bg][:, :], lhsT=wi[i][:, kc, 0:128], rhs=hT[:, kc, th * 512:(th + 1) * 512],
                             start=(kc == 0), stop=(kc == 7))
                    for kc in range(8):
                        S.op("pe", "matmul", reads=hreads + [f"wi{i}"], writes=[f"ps{bu}"],
                             out=ps[bu][:, :], lhsT=wi[i][:, kc, 128:256], rhs=hT[:, kc, th * 512:(th + 1) * 512],
                             start=(kc == 0), stop=(kc == 7))
                    S.op("act", "activation", reads=[f"ps{bg}"], writes=[f"sg{pg}"], out=sg[pg], in_=ps[bg][:, :],
                         func=AF.Silu)
                    S.op("dve", "tensor_tensor", reads=[f"sg{pg}", f"ps{bu}"], writes=[f"hid{fc}_{th}"],
                         out=hid[:, fc, th * 512:(th + 1) * 512], in0=sg[pg], in1=ps[bu][:, :], op=ALU.mult)
                if mid_hook is not None:
                    mid_hook(fc)
            if post_AB is not None:
                post_AB()
            for t in range(NT):
                th = t // 4
                b0 = (t % 2) * 2
                for dh in range(2):
                    for fc in range(NFC):
                        S.op("pe", "matmul", reads=[f"hid{fc}_{th}", f"wo_{fc}"], writes=[f"ps{b0 + dh}"],
                             out=ps[b0 + dh][:, :], lhsT=hid[:, fc, t * 128:(t + 1) * 128], rhs=wo[:, dh, fc, :],
                             start=(fc == 0), stop=(fc == NFC - 1))
                ln_epilogue(t, [f"ps{b0}", f"ps{b0 + 1}"], [(ps[b0][:, :], 0, 512), (ps[b0 + 1][:, :], 512, 1024)],
                            1, not final, 4 + (t % 2),
                            out_dram=(y_out[t * 128:(t + 1) * 128, :] if final else None))

        l1_blocks = [(nb, hf) for nb in range(5) for hf in range(2)]

        def hook0(fc):
            if fc < 10:
                nb, hf = l1_blocks[fc]
                mod_halfblock(1, nb, hf)
            if fc == 10:
                mod_finish(1)

        ffn(0, hook0, lambda: make_AB(0, 2, 1, 0, 1), False)

        if stop == 2:
            for t_ in range(NT):
                S.dma("sp", y_out[t_ * 128:(t_ + 1) * 128, :], x[:, t_, :], reads=[f"x{t_}"], writes=[f"yout{t_}"])
            S.emit(es)
            return nc
        S.fence()
        apos[0] = 0
        ynT = carve(16 * 2048, BF16, "p (c n) -> p c n", c=16)
        mark_grp = apos[0]
        W = carve(8 * 776 * 2, BF16, "p (k n) -> p k n", k=8)
        wname = "wg"
        xTg = carve(2 * 2048, BF16, "p (c n) -> p c n", c=2)
        BCT = carve(2 * 2048, BF16, "p (c n) -> p c n", c=2)
        xs_tm = carve(NT * 512, BF16, "p (t n) -> p t n", t=NT)
        B_tm = carve(NT * 256, BF16, "p (t n) -> p t n", t=NT)
        siluz = carve(NT * 512, BF16, "p (t n) -> p t n", t=NT)
        cacc = carve(2048, F32)
        hTh = carve(8 * 4 * 2, BF16, "p (k n) -> p k n", k=8)
        x1g = pre[0:12, :]
        selb = sb("selb", [12, 3])
        cwb = carve(32 * 5 * 4, F32, "p (c k) -> p c k", c=32)
        dtb = carve(64 * 4, F32)
        abc = carve(64 * 4, F32)
        dbc = carve(32 * 4, F32)
        nwg = carve(256 * 4, F32)
        dtt = carve(NT * 8 * 4, F32, "p (t n) -> p t n", t=NT)
        smA = carve(NT * 16 * 4, F32, "p (t n) -> p t n", t=NT)
        nacsA, eaA, dteA, cdecA, w2A, daA = [carve(NT * 8 * 4, F32, "p (t n) -> p t n", t=NT) for _ in range(6)]
        dahl = carve(NT * 16 * 2, BF16, "p (t n) -> p t n", t=NT)
        spt = carve(4 * 64 * 4, F32)
        xsdt = [carve(512, BF16) for _ in range(2)]
        xsst = [carve(512, BF16) for _ in range(2)]
        GT = carve(512, F32)
        Lsum = [pre[:, 0:512], carve(2048, F32)]
        GL4 = [xnbf[:, 0:512], xnbf[:, 512:1024]]
        S_sb = carve(NT * 2 * 1024, F32, "p (t d n) -> p t d n", t=NT, d=2)
        ydiag = carve(NT * 1024, F32, "p (t n) -> p t n", t=NT)
        Hst = [carve(1024, F32) for _ in range(2)]
        Hbf = [carve(512, BF16) for _ in range(2)]
        Zst = carve(2 * 1040, F32, "p (d n) -> p d n", d=2)
        gat = carve(8 * 1040, F32, "p (j d n) -> p j d n", j=4, d=2)
        h0g = carve(2 * 1024, F32, "p (d n) -> p d n", d=2)
        aj = carve(32 * 4, F32, "p (j d r) -> p j d r", j=4, d=2)
        ysum = gate_bc[:, 1, 0, :]
        yz = gate_bc[:, 1, 1, :]
        ynb = lnbc[:, 0, :].bitcast(BF16)[:, 0:1024]
        sq = carve(1024, F32)
        ytmp = [carve(1024, F32) for _ in range(2)]
        lnbf = lnbc[:, 0, :].bitcast(BF16)
        GTb = [GT, sq[:, 0:128]]
        xsdtb = [xsdt, [carve(512, BF16) for _ in range(2)]]
        xsstb = [xsst, [carve(512, BF16) for _ in range(2)]]
        Lsumb = [Lsum, [lnbc[:, 1, 0:512], lnbc[:, 1, 512:1024]]]
        GL4b = [GL4, [lnbf[:, 1024:1536], lnbf[:, 1536:2048]]]

        print("SSD arena bytes used", apos[0] * 4, "of", ARENA * 4)
        S.dma("sp", cwb[:], cwl, writes=["cwb"])
        S.dma("sp", dtb, dt_bias[0].rearrange("d h -> (d h)").partition_broadcast(128), writes=["dtb"])
        S.dma("sp", abc, a_log[0].rearrange("d h -> (d h)").partition_broadcast(128), writes=["abc"])
        S.dma("sp", dbc, ssd_d[0, :].partition_broadcast(128), writes=["dbc"])
        S.dma("sp", selb[:], sel, writes=["selb"])
        S.op("act", "activation", reads=["abc"], writes=["abc"], out=abc, in_=abc, func=AF.Exp)
        S.op("dve", "tensor_scalar", reads=["abc"], writes=["abc"], out=abc, in0=abc, scalar1=-1.0, scalar2=None,
             op0=ALU.mult)

        def wsl(c0, n):
            return w_in[0, :, c0:c0 + n].rearrange("(kc p) n -> p kc n", p=128)

        def load_wg(g):
            S.dma("pool", W[:, :, 0:256], wsl(DI + g * 256, 256), writes=[wname])
            S.dma("pool", W[:, :, 256:384], wsl(2 * DI + g * 128, 128), writes=[wname])
            S.dma("pool", W[:, :, 384:512], wsl(2 * DI + 1024 + g * 128, 128), writes=[wname])
            S.dma("pool", W[:, :, 512:768], wsl(g * 256, 256), writes=[wname])
            for d in range(2):
                S.dma("pool", W[:, :, 768 + d * 4:772 + d * 4], wsl(DI + CONV_CH + d * 32 + g * 4, 4), writes=[wname])

        load_wg(0)
        S.dma("sp", ag1_src[0:1, :], x[0:1, 4, :], reads=["x4"], writes=["ag1_src"])
        S.dma("sp", ag1_src[1:3, :], x[126:128, 7, :], reads=["x7"], writes=["ag1_src"])
        S.cc(reads=["ag1_src"], writes=["ag1_dst"], kind="AllGather", op=ALU.bypass, replica_groups=GROUPS4,
             ins=[ag1_src.opt()], outs=[ag1_dst.opt()])
        def emit_halo():
            S.dma("sp", x1g, ag1_dst, reads=["ag1_dst"], writes=["pre"])
            sc1h = sb("sc1h", [128, 8])
            S.op("dve", "tensor_scalar", reads=["modc1"], writes=["sc1h"], out=sc1h[:], in0=modc[:, 1, 1, :, 1],
                 scalar1=1.0, scalar2=None, op0=ALU.add)
            for kc in range(8):
                S.op("pe", "matmul", reads=["pre", "selb"], writes=["ps7"], out=ps[7][:, 64 + kc * 4:64 + kc * 4 + 3],
                     lhsT=x1g[:, kc * 128:(kc + 1) * 128], rhs=selb[:], start=True, stop=True)
            for kc in range(8):
                S.op("dve", "tensor_scalar", reads=["ps7", "sc1h", "modc1"], writes=["hTh32"],
                     out=sq[:, kc * 4:kc * 4 + 3], in0=ps[7][:, 64 + kc * 4:64 + kc * 4 + 3],
                     scalar1=sc1h[:, kc:kc + 1], scalar2=modc[:, 1, 0, kc, 1:2], op0=ALU.mult, op1=ALU.add)
                S.op("dve", "tensor_tensor", reads=["hTh32", "maskbc"], writes=["hTh"], out=hTh[:, kc, 0:3],
                     in0=sq[:, kc * 4:kc * 4 + 3], in1=maskbc[:, 16:19], op=ALU.mult)

        import os as _os2
        KSUB = int(_os2.environ.get("KSUB", "0"))
        KSKIP = _os2.environ.get("KSKIP", "")
        if KSUB == 1:
            for t_ in range(NT):
                S.dma("sp", y_out[t_ * 128:(t_ + 1) * 128, :], x[:, t_, :], reads=[f"x{t_}"], writes=[f"yout{t_}"])
            S.emit(es)
            return nc
        make_AB(1, 1, 1, 2, 3)

        def bc4(ap4, n=64):
            return ap4.unsqueeze(2).to_broadcast([128, 4, n])

        def v3(ap, n=64):
            return ap.rearrange("p (r e) -> p r e", e=n)

        import os as _os
        NG = int(_os.environ.get("KGROUPS", "8"))
        for g in range(NG):
            S.dma("sp", nwg, norm_w[0, g * 256:(g + 1) * 256].partition_broadcast(128), writes=["nwg"])
            for th, ci in [(th_, ci_) for th_ in range(2) for ci_ in range(4)]:
                chg = (g * 2 + ci) if ci < 2 else (16 + g if ci == 2 else 24 + g)
                if g == 0 and th == 1 and ci == 0:
                    emit_halo()
                if True:
                    bk = ci % 2
                    hreads = [f"hT{t}" for t in range(th * 4, th * 4 + 4)]
                    for kc in range(8):
                        S.op("pe", "matmul", reads=hreads + [wname], writes=[f"ps{bk}"], out=ps[bk][:, :],
                             lhsT=W[:, kc, ci * 128:(ci + 1) * 128], rhs=hT[:, kc, th * 512:(th + 1) * 512],
                             start=(kc == 0), stop=(kc == 7))
                    if th == 1 and 'h' not in KSKIP:
                        for kc in range(8):
                            S.op("pe", "matmul", reads=["hTh", wname], writes=["ps7h"], out=ps[7][:, 128:131],
                                 lhsT=W[:, kc, ci * 128:(ci + 1) * 128], rhs=hTh[:, kc, 0:3],
                                 start=(kc == 0), stop=(kc == 7))
                    P = ps[bk]
                    cw = cwb[:, chg, :]
                    S.op("act", "activation", reads=[f"ps{bk}", "cwb"], writes=["cacc"], out=cacc, in_=P[:, :],
                         func=AF.Identity, scale=cw[:, 2:3])
                    segs = [(0, 256), (256, 256)] if th == 0 else [(0, 512)]
                    for (o, L) in segs:
                        for k, s_ in ((0, -2), (1, -1), (3, 1)):
                            lo = o + max(0, -s_)
                            hi = o + min(L, L - s_)
                            S.op("dve", "scalar_tensor_tensor", reads=[f"ps{bk}", "cwb", "cacc"], writes=["cacc"],
                                 out=cacc[:, lo:hi], in0=P[:, lo + s_:hi + s_], scalar=cw[:, k:k + 1], in1=cacc[:, lo:hi],
                                 op0=ALU.mult, op1=ALU.add)
                    if th == 1 and 'h' not in KSKIP:
                        Hh = ps[7]
                        for (dst, hc, k) in ((0, 128, 0), (1, 129, 0), (0, 129, 1), (511, 130, 3)):
                            S.op("dve", "scalar_tensor_tensor", reads=["ps7h", "cwb", "cacc"], writes=["cacc"],
                                 out=cacc[:, dst:dst + 1], in0=Hh[:, hc:hc + 1], scalar=cw[:, k:k + 1],
                                 in1=cacc[:, dst:dst + 1], op0=ALU.mult, op1=ALU.add)
                    if ci < 2:
                        dst, dname = xTg[:, ci, th * 512:(th + 1) * 512], f"xTg{ci}_{th}"
                    else:
                        dst, dname = BCT[:, ci - 2, th * 512:(th + 1) * 512], f"BCT{ci - 2}_{th}"
                    S.op("act", "activation", reads=["cacc", "cwb"], writes=[dname], out=dst, in_=cacc, func=AF.Silu,
                         bias=cw[:, 4:5], scale=1.0)
            for t in (range(NT) if 't' not in KSKIP else []):
                th = t // 4
                tb = 6 if t % 2 == 0 else 4
                pb = psbf(tb)
                for ci in range(2):
                    S.op("pe", "transpose", reads=[f"xTg{ci}_{th}", "cmb"], writes=[f"ps{tb}"],
                         out=pb[:, ci * 128:(ci + 1) * 128], in_=xTg[:, ci, t * 128:(t + 1) * 128], identity=ident_b)
                S.op("pe", "transpose", reads=[f"BCT0_{th}", "cmb"], writes=[f"ps{tb}"],
                     out=pb[:, 256:384], in_=BCT[:, 0, t * 128:(t + 1) * 128], identity=ident_b)
                S.op("act", "activation", reads=[f"ps{tb}"], writes=[f"xs_tm{t}"], out=xs_tm[:, t, :], in_=pb[:, 0:256],
                     func=AF.Identity)
                S.op("act", "activation", reads=[f"ps{tb}"], writes=[f"B_tm{t}"], out=B_tm[:, t, :], in_=pb[:, 256:384],
                     func=AF.Identity)
            vv, av, ev, dtraw = spt[:, 0:64], spt[:, 64:128], spt[:, 128:192], spt[:, 192:256]
            t8 = lambda ap: ap.rearrange("p (t n) -> p t n", t=NT)
            for t in range(NT):
                zb = 2 + t % 2
                for kc in range(8):
                    S.op("pe", "matmul", reads=[f"hT{t}", wname], writes=[f"ps{zb}"], out=ps[zb][:, 0:264],
                         lhsT=hT[:, kc, t * 128:(t + 1) * 128], rhs=W[:, kc, 512:776], start=(kc == 0), stop=(kc == 7))
                S.op("act", "activation", reads=[f"ps{zb}"], writes=[f"siluz{t}"], out=siluz[:, t, :],
                     in_=ps[zb][:, 0:256], func=AF.Silu)
                S.op("act", "activation", reads=[f"ps{zb}"], writes=["dtraw"], out=dtraw[:, t * 8:(t + 1) * 8],
                     in_=ps[zb][:, 256:264], func=AF.Identity)
            for d in range(2):
                S.op("dve", "tensor_tensor", reads=["dtraw", "dtb"], writes=["spt_v"], out=t8(vv)[:, :, d * 4:d * 4 + 4],
                     in0=t8(dtraw)[:, :, d * 4:d * 4 + 4],
                     in1=dtb[:, d * 32 + g * 4:d * 32 + g * 4 + 4].unsqueeze(1).to_broadcast([128, NT, 4]), op=ALU.add)
            S.op("dve", "scalar_tensor_tensor", reads=["spt_v"], writes=["spt_a"], out=av, in0=vv, scalar=-1.0,
                 in1=vv, op0=ALU.mult, op1=ALU.max)
            S.op("act", "activation", reads=["spt_a"], writes=["spt_e"], out=ev, in_=av, func=AF.Exp, scale=-1.0)
            S.op("act", "activation", reads=["spt_e"], writes=["spt_e"], out=ev, in_=ev, func=AF.Ln, bias=1.0, scale=1.0)
            S.op("dve", "scalar_tensor_tensor", reads=["spt_v", "spt_e"], writes=["dtt"],
                 out=dtt[:].rearrange("p t n -> p (t n)"), in0=vv, scalar=0.0, in1=ev, op0=ALU.max, op1=ALU.add)
            if KSUB == 2:
                for t_ in range(NT):
                    S.dma("sp", y_out[t_ * 128:(t_ + 1) * 128, :], x[:, t_, :], reads=[f"x{t_}"], writes=[f"yout{t_}"])
                S.emit(es)
                return nc
            if g + 1 < NG:
                load_wg(g + 1)
            fl = lambda ap: ap.rearrange("p t n -> p (t n)")
            for d in range(2):
                S.op("dve", "tensor_tensor", reads=["dtt", "abc"], writes=["da"], out=daA[:, :, d * 4:d * 4 + 4],
                     in0=dtt[:, :, d * 4:d * 4 + 4],
                     in1=abc[:, d * 32 + g * 4:d * 32 + g * 4 + 4].unsqueeze(1).to_broadcast([128, NT, 4]), op=ALU.mult)
            for t in range(NT):
                for d in range(2):
                    S.op("pe", "matmul", reads=["da", "cmf"], writes=["ps7a"],
                         out=ps[7][:, 256 + t * 16 + d * 4:256 + t * 16 + d * 4 + 4],
                         lhsT=cmf[:, 1 + d, :], rhs=daA[:, t, d * 4:d * 4 + 4], start=True, stop=True)
                S.op("pe", "matmul", reads=["da", "cmf"], writes=["ps7a"], out=ps[7][:, 256 + t * 16 + 8:256 + t * 16 + 16],
                     lhsT=ones_f, rhs=daA[:, t, :], start=True, stop=True)
            S.op("dve", "tensor_copy", reads=["ps7a"], writes=["smA"], out=fl(smA), in_=ps[7][:, 256:384])
            S.op("dve", "tensor_scalar", reads=["smA"], writes=["nacs"], out=nacsA[:], in0=smA[:, :, 0:8], scalar1=-1.0,
                 scalar2=None, op0=ALU.mult)
            S.op("act", "activation", reads=["smA"], writes=["ea"], out=eaA[:], in_=smA[:, :, 0:8], func=AF.Exp)
            S.op("dve", "tensor_tensor", reads=["smA"], writes=["dte"], out=dteA[:], in0=smA[:, :, 8:16], in1=smA[:, :, 0:8],
                 op=ALU.subtract)
            S.op("act", "activation", reads=["dte"], writes=["dte"], out=dteA[:], in_=dteA[:], func=AF.Exp)
            S.op("act", "activation", reads=["smA"], writes=["cdec"], out=cdecA[:], in_=smA[:, :, 8:16], func=AF.Exp)
            S.op("dve", "tensor_tensor", reads=["dte", "dtt"], writes=["w2"], out=w2A[:], in0=dteA[:], in1=dtt[:], op=ALU.mult)
            S.op("dve", "tensor_copy", reads=["da"], writes=["dahl"], out=dahl[:, :, 0:8], in_=daA[:])
            S.op("dve", "tensor_tensor", reads=["da", "dahl"], writes=["dahl"], out=dahl[:, :, 8:16], in0=daA[:],
                 in1=dahl[:, :, 0:8], op=ALU.subtract)
            def diag_steps(t, par):
                th = t // 4
                tsl = slice(t * 128, (t + 1) * 128)
                GTp, gtn = GTb[par], f"GT{par}"
                yb = 3 if par == 0 else 7
                gtps = ps[3][:, 256:384] if par == 0 else ps[7][:, 384:512]
                lbs = (4, 2) if par == 0 else (0, 6)
                xd, xs_ = xsdtb[par], xsstb[par]

                def stepA():
                    S.op("pe", "matmul", reads=[f"BCT0_{th}", f"BCT1_{th}"], writes=[f"ps{yb}"], out=gtps,
                         lhsT=BCT[:, 0, tsl], rhs=BCT[:, 1, tsl], start=True, stop=True)
                    S.op("act", "activation", reads=[f"ps{yb}"], writes=[gtn], out=GTp, in_=gtps, func=AF.Identity)
                    for d in range(2):
                        S.op("pool", "tensor_tensor", reads=[f"xs_tm{t}", "dtt"], writes=[f"xsdt{par}{d}"],
                             out=v3(xd[d]), in0=v3(xs_tm[:, t, :]), in1=bc4(dtt[:, t, d * 4:d * 4 + 4]), op=ALU.mult)
                        S.op("dve", "tensor_tensor", reads=[f"xs_tm{t}", "w2"], writes=[f"xsst{par}{d}"],
                             out=v3(xs_[d]), in0=v3(xs_tm[:, t, :]), in1=bc4(w2A[:, t, d * 4:d * 4 + 4]), op=ALU.mult)
                        sbk = 5 if d == 0 else 1
                        S.op("pe", "matmul", reads=[f"B_tm{t}", f"xsst{par}{d}"], writes=[f"ps{sbk}"],
                             out=ps[sbk][:, 0:256], lhsT=B_tm[:, t, :], rhs=xs_[d], start=True, stop=True)
                        S.op("act", "activation", reads=[f"ps{sbk}"], writes=[f"S{t}_{d}"], out=S_sb[:, t, d, :],
                             in_=ps[sbk][:, 0:256], func=AF.Identity)

                def stepB():
                    for d in range(2):
                        lb = lbs[d]
                        lpn = f"ps{lb}"
                        for r in range(4):
                            col = d * 4 + r
                            lp = ps[lb][:, r * 128:(r + 1) * 128]
                            S.op("pe", "matmul", reads=["dahl", "cmb"], writes=[lpn], out=lp,
                                 lhsT=dahl[:, t, col:col + 1].to_broadcast([128, 128]), rhs=cmb[:, 1 + d, :],
                                 start=True, stop=False)
                            S.op("pe", "matmul", reads=["dahl", "cmb"], writes=[lpn], out=lp,
                                 lhsT=dahl[:, t, 8 + col:9 + col].to_broadcast([128, 128]), rhs=cmb[:, 1 + d, :],
                                 start=False, stop=False)
                            S.op("pe", "matmul", reads=["cmb"], writes=[lpn], out=lp, lhsT=ident_b, rhs=cmb[:, 3 + d, :],
                                 start=False, stop=True)

                def stepC(d):
                    def f():
                        lb = lbs[d]
                        lpn = f"ps{lb}"
                        Ls, lsn = Lsumb[par][d], f"Lsum{par}{d}"
                        G4, gln = GL4b[par][d], f"GL{par}{d}"
                        S.op("dve", "tensor_tensor", reads=[lpn, "nacs"], writes=[lsn], out=v3(Ls, 128),
                             in0=v3(ps[lb][:, :], 128), in1=bc4(nacsA[:, t, d * 4:d * 4 + 4], 128), op=ALU.add)
                        S.op("act", "activation", reads=[lsn], writes=[lsn], out=Ls, in_=Ls, func=AF.Exp)
                        S.op("dve", "tensor_tensor", reads=[lsn, gtn], writes=[gln], out=v3(G4, 128),
                             in0=v3(Ls, 128), in1=GTp.unsqueeze(1).to_broadcast([128, 4, 128]), op=ALU.mult)
                        for r in range(4):
                            S.op("pe", "matmul", reads=[gln, f"xsdt{par}{d}"], writes=[f"ps{yb}"],
                                 out=ps[yb][:, r * 64:(r + 1) * 64], lhsT=G4[:, r * 128:(r + 1) * 128],
                                 rhs=xd[d][:, r * 64:(r + 1) * 64],
                                 start=(d == 0 and r == 0), stop=(d == 1 and r == 3), skip_group_check=True)
                    return f

                def stepE():
                    S.op("act", "activation", reads=[f"ps{yb}"], writes=[f"ydiag{t}"], out=ydiag[:, t, :],
                         in_=ps[yb][:, 0:256], func=AF.Identity)

                return [stepA, stepB, stepC(0), stepC(1), stepE]

            def diag_pair(ta, tb):
                sa, sb_ = diag_steps(ta, 0), diag_steps(tb, 1)
                for fa, fb in zip(sa, sb_):
                    fa()
                    fb()

            def chain_seq(si, tiles, ext):
                orders = [tiles, tiles[::-1]]
                have = [False, False]
                if ext:
                    for d in range(2):
                        S.op("dve", "tensor_copy", reads=["h0g"], writes=[f"Hst{d}"], out=Hst[d], in_=h0g[:, d, :])
                    for step in range(4):
                        for d in range(2):
                            j = step if d == 0 else 3 - step
                            Hs, hn = Hst[d], f"Hst{d}"
                            mcol = maskbc[:, d * 4 + j:d * 4 + j + 1]
                            S.op("dve", "tensor_tensor", reads=[hn, "aj"], writes=[hn], out=v3(Hs), in0=v3(Hs),
                                 in1=bc4(aj[:, j, d, :]), op=ALU.mult)
                            S.op("dve", "scalar_tensor_tensor", reads=["gat", "maskbc", hn], writes=[hn], out=Hs,
                                 in0=gat[:, j, d, 0:256], scalar=mcol, in1=Hs, op0=ALU.mult, op1=ALU.add)
                    have = [True, True]
                for n in range(len(tiles)):
                    for d in range(2):
                        t = orders[d][n]
                        th = t // 4
                        Hs, hn = Hst[d], f"Hst{d}"
                        if have[d]:
                            S.op("act", "activation", reads=[hn], writes=[f"Hbf{d}"], out=Hbf[d], in_=Hs, func=AF.Identity)
                            sbk = 5 if d == 0 else 1
                            S.op("pe", "matmul", reads=[f"BCT1_{th}", f"Hbf{d}"], writes=[f"ps{sbk}"],
                                 out=ps[sbk][:, 0:256], lhsT=BCT[:, 1, t * 128:(t + 1) * 128],
                                 rhs=Hbf[d], start=True, stop=True)
                            S.op("dve", "tensor_tensor", reads=[f"ps{sbk}", "ea"], writes=[f"ytmp{d}"], out=v3(ytmp[d]),
                                 in0=v3(ps[sbk][:, 0:256]), in1=bc4(eaA[:, t, d * 4:d * 4 + 4]),
                                 op=ALU.mult)
                            S.op("dve", "tensor_tensor", reads=[f"ytmp{d}", f"ydiag{t}"], writes=[f"ydiag{t}"],
                                 out=ydiag[:, t, :], in0=ydiag[:, t, :], in1=ytmp[d], op=ALU.add)
                            if n < len(tiles) - 1 or not ext:
                                S.op("dve", "tensor_tensor", reads=[hn, "cdec"], writes=[hn], out=v3(Hs), in0=v3(Hs),
                                     in1=bc4(cdecA[:, t, d * 4:d * 4 + 4]), op=ALU.mult)
                                S.op("dve", "tensor_tensor", reads=[hn, f"S{t}_{d}"], writes=[hn], out=Hs, in0=Hs,
                                     in1=S_sb[:, t, d, :], op=ALU.add)
                        else:
                            S.op("dve", "tensor_copy", reads=[f"S{t}_{d}"], writes=[hn], out=Hs, in_=S_sb[:, t, d, :])
                            have[d] = True
                if not ext:
                    for d in range(2):
                        S.dma("sp", st_out[si, d, :, g * 256:(g + 1) * 256], Hst[d], reads=[f"Hst{d}"],
                              writes=[f"st{si}{d}{g}"])
                nt = len(tiles)
                t0 = tiles[0]
                W_ = nt * 256
                yd = ydiag[:, t0:t0 + nt, :]
                ydn = [f"ydiag{t}" for t in tiles]
                ys_, yz_ = ysum[:, 0:W_], yz[:, 0:W_]
                r4 = lambda ap: ap.rearrange("p (t r e) -> p t r e", t=nt, e=64)
                S.op("dve", "tensor_tensor", reads=[f"xs_tm{t}" for t in tiles] + ["dbc"], writes=["ysum"],
                     out=r4(ys_), in0=xs_tm[:, t0:t0 + nt, :].rearrange("p t (r e) -> p t r e", e=64),
                     in1=dbc[:, g * 4:g * 4 + 4].unsqueeze(1).unsqueeze(3).to_broadcast([128, nt, 4, 64]), op=ALU.mult)
                S.op("dve", "tensor_tensor", reads=["ysum"] + ydn, writes=["ysum"], out=ys_, in0=ys_,
                     in1=yd.rearrange("p t n -> p (t n)"), op=ALU.add)
                S.op("dve", "tensor_tensor", reads=["ysum"] + [f"siluz{t}" for t in tiles], writes=["yz"], out=yz_, in0=ys_,
                     in1=siluz[:, t0:t0 + nt, :].rearrange("p t n -> p (t n)"), op=ALU.mult)
                for n in range(nt):
                    S.op("act", "activation", reads=["yz"], writes=["sq", "stat_ss"], out=sq[:, 0:256],
                         in_=yz[:, n * 256:(n + 1) * 256], func=AF.Square, accum_out=stat[:, 8 + n:9 + n])
                S.op("act", "activation", reads=["stat_ss", "epsc"], writes=["stat_rs"], out=stat[:, 12:12 + nt],
                     in_=stat[:, 8:8 + nt], func=AF.Sqrt, bias=epsc[:, 1:2], scale=1.0 / 256.0)
                S.op("dve", "reciprocal", reads=["stat_rs"], writes=["stat_rr"], out=stat[:, 16:16 + nt], in_=stat[:, 12:12 + nt])
                y3 = lambda ap: ap.rearrange("p (t n) -> p t n", t=nt)
                S.op("dve", "tensor_tensor", reads=["yz", "stat_rr"], writes=["yz"], out=y3(yz_), in0=y3(yz_),
                     in1=stat[:, 16:16 + nt].unsqueeze(2).to_broadcast([128, nt, 256]), op=ALU.mult)
                S.op("dve", "tensor_tensor", reads=["yz", "nwg"], writes=["ynb"], out=y3(ynb[:, 0:W_]), in0=y3(yz_),
                     in1=nwg.unsqueeze(1).to_broadcast([128, nt, 256]), op=ALU.mult)
                pb = psbf(6)
                for n in range(nt):
                    for cc in range(2):
                        S.op("pe", "transpose", reads=["ynb", "cmb"], writes=["ps6"],
                             out=pb[:, (n * 2 + cc) * 128:(n * 2 + cc + 1) * 128],
                             in_=ynb[:, n * 256 + cc * 128:n * 256 + (cc + 1) * 128], identity=ident_b)
                for cc in range(2):
                    S.op("act", "activation", reads=["ps6"], writes=[f"ynT{g}_{t}" for t in tiles],
                         out=ynT[:, 2 * g + cc, t0 * 128:(t0 + nt) * 128].rearrange("p (t n) -> p t n", t=nt),
                         in_=pb[:, 0:W_].rearrange("p (t c n) -> p t c n", t=nt, c=2)[:, :, cc, :], func=AF.Identity)

            stiles = SEQS[2][0]
            diag_pair(4, 5)
            diag_pair(6, 7)
            if KSUB == 3:
                for t_ in range(NT):
                    S.dma("sp", y_out[t_ * 128:(t_ + 1) * 128, :], x[:, t_, :], reads=[f"x{t_}"], writes=[f"yout{t_}"])
                S.emit(es)
                return nc
            for d in range(2):
                order = stiles if d == 0 else stiles[::-1]
                Z = Zst[:, d, 0:256]
                LD = Zst[:, d, 256:260]
                for n, t in enumerate(order):
                    if n == 0:
                        S.op("dve", "tensor_copy", reads=[f"S{t}_{d}"], writes=["Zst"], out=Z, in_=S_sb[:, t, d, :])
                        S.op("dve", "tensor_copy", reads=["smA"], writes=["Zst"], out=LD,
                             in_=smA[:, t, 8 + d * 4:12 + d * 4])
                    else:
                        S.op("dve", "tensor_tensor", reads=["Zst", "cdec"], writes=["Zst"], out=v3(Z), in0=v3(Z),
                             in1=bc4(cdecA[:, t, d * 4:d * 4 + 4]), op=ALU.mult)
                        S.op("dve", "tensor_tensor", reads=["Zst", f"S{t}_{d}"], writes=["Zst"], out=Z, in0=Z,
                             in1=S_sb[:, t, d, :], op=ALU.add)
                        S.op("dve", "tensor_tensor", reads=["Zst", "smA"], writes=["Zst"], out=LD, in0=LD,
                             in1=smA[:, t, 8 + d * 4:12 + d * 4], op=ALU.add)
            S.dma("sp", ag2_src[g].rearrange("(d n) c -> n d c", d=2), Zst, reads=["Zst"], writes=[f"ag2s{g}"])
            S.cc(reads=[f"ag2s{g}"], writes=[f"ag2d{g}"], kind="AllGather", op=ALU.bypass, replica_groups=GROUPS4,
                 ins=[ag2_src[g].opt()], outs=[ag2_dst[g].opt()])
            S.dma("sp", gat, ag2_dst[g].rearrange("(j d n) c -> n j d c", j=4, d=2), reads=[f"ag2d{g}"], writes=["gat"])
            S.dma("sp", h0g, h0T[:, :, g * 256:(g + 1) * 256].rearrange("d n c -> n d c"), writes=["h0g"])
            for si in range(2):
                diag_pair(*SEQS[si][0])
                chain_seq(si, SEQS[si][0], False)
            if KSUB == 5:
                for t_ in range(NT):
                    S.dma("sp", y_out[t_ * 128:(t_ + 1) * 128, :], x[:, t_, :], reads=[f"x{t_}"], writes=[f"yout{t_}"])
                S.emit(es)
                return nc
            mjd = lambda o: maskbc[:, o:o + 8].rearrange("p (d j) -> p j d", d=2).unsqueeze(3).to_broadcast([128, 4, 2, 4])
            S.op("act", "activation", reads=["gat"], writes=["aj"], out=aj, in_=gat[:, :, :, 256:260], func=AF.Exp)
            S.op("dve", "tensor_tensor", reads=["aj", "maskbc"], writes=["aj"], out=aj, in0=aj, in1=mjd(0), op=ALU.mult)
            S.op("dve", "tensor_tensor", reads=["aj", "maskbc"], writes=["aj"], out=aj, in0=aj, in1=mjd(8), op=ALU.add)
            chain_seq(2, stiles, True)

        S.fence()
        apos[0] = mark_grp
        wob = carve(16 * 2048, BF16, "p (c n) -> p c n", c=16)
        S.dma("pool", wob, w_out[0].rearrange("(c p) n -> p c n", p=128), writes=["wob"])
        load_lnbc(1, 1)
        for t in range(NT):
            b0 = (t % 2) * 2
            for dh in range(2):
                for c16 in range(16):
                    S.op("pe", "matmul", reads=[f"ynT{c16 // 2}_{t}", "wob"], writes=[f"ps{b0 + dh}"],
                         out=ps[b0 + dh][:, :], lhsT=ynT[:, c16, t * 128:(t + 1) * 128],
                         rhs=wob[:, c16, dh * 512:(dh + 1) * 512], start=(c16 == 0), stop=(c16 == 15))
            ln_epilogue(t, [f"ps{b0}", f"ps{b0 + 1}"], [(ps[b0][:, :], 0, 512), (ps[b0 + 1][:, :], 512, 1024)],
                        0, True, 4 + (t % 2))

        if stop == 3:
            for t_ in range(NT):
                S.dma("sp", y_out[t_ * 128:(t_ + 1) * 128, :], x[:, t_, :], reads=[f"x{t_}"], writes=[f"yout{t_}"])
            S.emit(es)
            return nc
        def pre1():
            for hf in range(2):
                mod_halfblock(1, 5, hf)

        ffn(1, None, None, True, pre_hook=pre1)

        S.emit(es)
    return nc


def _win(n, k):
    left = k // 2
    right = k - 1 - left
    pos = np.arange(n)
    lo = np.clip(pos - left, 0, n)
    hi = np.clip(pos + right + 1, 0, n)
    return lo, hi


def _host_consts():
    ar = np.arange(128)
    ident = np.eye(128, dtype=np.float32)
    tri_f = (ar[:, None] <= ar[None, :]).astype(np.float32)
    tri_b = (ar[:, None] >= ar[None, :]).astype(np.float32)
    mneg_f = np.where(ar[:, None] <= ar[None, :], 0.0, NEG).astype(np.float32)
    mneg_b = np.where(ar[:, None] >= ar[None, :], 0.0, NEG).astype(np.float32)
    ones = np.ones((128, 128), np.float32)
    cm = np.stack([ident, tri_f, tri_b, mneg_f, mneg_b, ones]).astype(np.float32)
    p1 = np.zeros((4, 256, 256), np.float32)
    inv1 = np.zeros((4, 256), np.float32)
    for g, k in enumerate(POOL_K):
        lo, hi = _win(256, k)
        for o in range(256):
            p1[g, lo[o]:hi[o], o] = 1.0
            p1[g, o, o] -= float(hi[o] - lo[o])
            inv1[g, o] = 1.0 / float(hi[o] - lo[o])
    p1 = p1.reshape(4, 2, 128, 256).reshape(8, 128, 256)
    return cm, p1, inv1


def _host_pool2d(q):
    r0 = q * 8 - 8
    blocks = np.zeros((NP2, 128, 512), np.float32)
    inv2 = np.zeros((4, 512), np.float32)
    for g, k in enumerate(POOL_K):
        rlo, rhi = _win(32, k)
        clo, chi = _win(64, k)
        P = np.zeros((1536, 512), np.float32)
        for rr in range(8):
            r = q * 8 + rr
            for c in range(64):
                o = rr * 64 + c
                cnt = float((rhi[r] - rlo[r]) * (chi[c] - clo[c]))
                for r2 in range(rlo[r], rhi[r]):
                    e0 = (r2 - r0) * 64
                    P[e0 + clo[c]:e0 + chi[c], o] = 1.0
                P[(r - r0) * 64 + c, o] -= cnt
                inv2[g, o] = 1.0 / cnt
        lo, hi = P2_TILES[g]
        nz = np.abs(P).reshape(12, 128, 512).sum(axis=(1, 2))
        for tin in range(12):
            if tin < lo or tin >= hi:
                assert nz[tin] == 0.0
        blocks[P2_BLK0[g]:P2_BLK0[g] + hi - lo] = P.reshape(12, 128, 512)[lo:hi]
    return blocks, inv2


_NC_CACHE = {}


def kernel(x_prompt, x_sample, c, state_ssm, c_ctx, w_ada, b_ada, w_pool, pool_scale, ssd_w_in, ssd_conv_w,
           ssd_conv_b, ssd_dt_bias, ssd_a_log, ssd_d, ssd_norm_w, ssd_w_out, ffn_w_in, ffn_w_out,
           ln1_g, ln1_b, ln2_g, ln2_b):
    f = lambda a: np.ascontiguousarray(np.asarray(a, dtype=np.float32))
    x_prompt, x_sample, c, state_ssm, c_ctx = map(f, (x_prompt, x_sample, c, state_ssm, c_ctx))
    shared = dict(w_ada=f(w_ada), b_ada=f(b_ada), w_pool=f(w_pool), pool_scale=f(pool_scale), ssd_w_in=f(ssd_w_in),
                  ssd_dt_bias=f(ssd_dt_bias),
                  ssd_a_log=f(ssd_a_log), ssd_d=f(ssd_d), ssd_norm_w=f(ssd_norm_w), ssd_w_out=f(ssd_w_out),
                  ffn_w_in=f(ffn_w_in), ffn_w_out=f(ffn_w_out), ln1_g=f(ln1_g), ln1_b=f(ln1_b), ln2_g=f(ln2_g),
                  ln2_b=f(ln2_b))
    lnl = [shared[k][l] for l in range(2) for k in ("ln1_g", "ln1_b", "ln2_g", "ln2_b")]
    shared["lncT"] = np.ascontiguousarray(np.stack(lnl).reshape(8, 8, 128).transpose(2, 0, 1))
    cw5 = np.concatenate([f(ssd_conv_w)[0], f(ssd_conv_b)], axis=0)
    shared["cwl"] = np.ascontiguousarray(cw5.reshape(5, 32, 128).transpose(2, 1, 0))
    cm, p1, inv1 = _host_consts()
    pool2 = [_host_pool2d(q) for q in range(4)]
    in_maps = []
    for core in range(8):
        b, q = core // 4, core % 4
        xin = np.concatenate([x_prompt[2 * core].reshape(256, D), x_prompt[2 * core + 1].reshape(256, D),
                              x_sample[b, q * 512:(q + 1) * 512]], axis=0)
        xext = np.zeros((1536, D), np.float32)
        lo = q * 512 - 512
        hi = q * 512 + 1024
        slo, shi = max(lo, 0), min(hi, 2048)
        xext[slo - lo:shi - lo] = x_sample[b, slo:shi]
        cvec = np.ascontiguousarray(np.stack([c_ctx, c[b]]).astype(np.float32).reshape(2, 8, 128).transpose(2, 1, 0))
        h0T = np.ascontiguousarray(state_ssm[b, 0].transpose(0, 3, 1, 2).reshape(2, 128, DI))
        blocks, inv2 = pool2[q]
        invc = np.concatenate([inv1, inv2], axis=1).reshape(1, 4 * 768).astype(np.float32)
        mf = np.array([1.0 if j < q else 0.0 for j in range(4)], np.float32)
        mb = np.array([1.0 if j > q else 0.0 for j in range(4)], np.float32)
        valid = np.array([1.0 if q > 0 else 0.0, 1.0 if q > 0 else 0.0, 1.0 if q < 3 else 0.0], np.float32)
        cmask = np.concatenate([mf, mb, 1 - mf, 1 - mb, valid]).reshape(1, 19).astype(np.float32)
        sel = np.zeros((12, 3), np.float32)
        if q > 0:
            sel[3 * (q - 1) + 1, 0] = 1.0
            sel[3 * (q - 1) + 2, 1] = 1.0
        if q < 3:
            sel[3 * (q + 1) + 0, 2] = 1.0
        m = dict(shared)
        m.update(xin=np.ascontiguousarray(xin), xext=xext, cvecT=cvec, h0T=h0T, cmats=cm, p1=p1, p2=blocks, invc=invc,
                 cmask=cmask, sel=sel)
        in_maps.append(m)
    import os
    stop = int(os.environ.get("KSTOP", "99"))
    if stop not in _NC_CACHE:
        _NC_CACHE[stop] = build_program(stop)
    nc = _NC_CACHE[stop]
    res = run_bass_kernel_spmd(nc, in_maps, core_ids=list(range(8)))
    y_prompt = np.zeros((16, 256, D), np.float32)
    y_sample = np.zeros((2, 2048, D), np.float32)
    new_state = np.zeros((16, 1, 2, 32, 64, 128), np.float32)
    for core in range(8):
        r = res.results[core]
        b, q = core // 4, core % 4
        yo = np.asarray(r["y_out"])
        y_prompt[2 * core] = yo[0:256]
        y_prompt[2 * core + 1] = yo[256:512]
        y_sample[b, q * 512:(q + 1) * 512] = yo[512:1024]
        st = np.asarray(r["st_out"]).reshape(2, 2, 128, 32, 64)
        new_state[2 * core:2 * core + 2, 0] = st.transpose(0, 1, 3, 4, 2)
    return (y_prompt, y_sample, new_state)
```

```python
import numpy as np
from contextlib import ExitStack
import concourse.bass as bass
import concourse.mybir as mybir
from concourse.bass_utils import run_bass_kernel_spmd

F32 = mybir.dt.float32
BF16 = mybir.dt.bfloat16
AF = mybir.ActivationFunctionType
ALU = mybir.AluOpType

D = 1024
NT = 8
FF = 2816
NFC = FF // 128
DI = 2048
CONV_CH = 4096
NCOLS_IN = 6208
ALPHA = float(4 ** 0.25)
LN_EPS = 1e-5
RMS_EPS = 1e-5
POOL_K = (2, 4, 8, 16)
P2_TILES = {0: (3, 8), 1: (3, 9), 2: (2, 10), 3: (0, 12)}
P2_BLK0 = {0: 0, 1: 5, 2: 11, 3: 19}
NP2 = 31
GROUPS4 = [[0, 1, 2, 3], [4, 5, 6, 7]]
SEQS = [([0, 1], False), ([2, 3], False), ([4, 5, 6, 7], True)]
NEG = -30000.0


class Sched:
    ENG = ("pe", "act", "dve", "pool", "sp")

    def __init__(self, nc):
        self.nc = nc
        self.ops = []
        self.per_eng = {e: [] for e in self.ENG}
        self.last_w = {}
        self.readers = {}

    def fence(self):
        self._add("dve", "c", "memset", dict(ap=self.fence_tile, constant=0.0), (), ("__fence__",), fence=True)

    def _add(self, eng, kind, name, kwargs, reads, writes, fence=False):
        if not fence:
            def _isps(n):
                return len(n) >= 3 and n[:2] == "ps" and n[2].isdigit()
            psr = tuple(r[:3] for r in reads if _isps(r))
            reads = tuple(r for r in reads if not _isps(r))
            writes = tuple(dict.fromkeys(tuple((w[:3] if _isps(w) else w) for w in writes) + psr))
            reads = tuple(reads) + ("__fence__",)
        oid = len(self.ops)
        deps = set()
        if fence:
            for e in self.ENG:
                cs = [o for o in self.per_eng[e] if o["kind"] == "c"]
                if cs:
                    deps.add(cs[-1]["id"])
                ds = [o for o in self.per_eng[e] if o["kind"] != "c"]
                for o in ds[-8:]:
                    deps.add(o["id"])
            reads, writes_ = (), writes
        for r in reads:
            if r in self.last_w:
                deps.add(self.last_w[r])
        for w in writes:
            if w in self.last_w:
                deps.add(self.last_w[w])
            if not fence:
                for rd in self.readers.get(w, ()):
                    deps.add(rd)
        deps.discard(oid)
        op = dict(id=oid, eng=eng, kind=kind, name=name, kw=kwargs, deps=deps, needed=False)
        self.ops.append(op)
        self.per_eng[eng].append(op)
        for r in reads:
            self.readers.setdefault(r, []).append(oid)
        for w in writes:
            self.last_w[w] = oid
            self.readers[w] = []
        return oid

    def op(self, eng, name, reads=(), writes=(), **kw):
        return self._add(eng, "c", name, kw, tuple(reads), tuple(writes))

    def dma(self, eng, out, in_, reads=(), writes=()):
        return self._add(eng, "d", "dma_start", dict(out=out, in_=in_), tuple(reads), tuple(writes))

    def cc(self, reads, writes, **kw):
        return self._add("pool", "cc", "collective_compute", kw, tuple(reads), tuple(writes))

    def emit(self, es):
        nc = self.nc
        ops = self.ops
        for op in ops:
            for d in op["deps"]:
                p = ops[d]
                if p["kind"] == "c":
                    if p["eng"] == "pe" and op["eng"] == "pe" and op["kind"] == "c":
                        continue
                    p["needed"] = True
        M = 8
        csem = {e: es.enter_context(nc.semaphore("c_" + e)) for e in ("pe", "act", "dve", "pool")}
        dsem = {e: [es.enter_context(nc.semaphore(f"d_{e}{i}")) for i in range(M)] for e in ("pool", "sp")}
        ccsem = [es.enter_context(nc.semaphore(f"ccs{i}")) for i in range(M)]
        ccnt = {e: 0 for e in csem}
        dcnt = {e: 0 for e in dsem}
        ncc = 0
        for e in self.ENG:
            for op in self.per_eng[e]:
                if op["kind"] == "c":
                    if op["needed"]:
                        ccnt[e] += 1
                        op["sem"] = ("c", e, ccnt[e])
                    else:
                        op["sem"] = None
                elif op["kind"] == "d":
                    i = dcnt[e]
                    dcnt[e] += 1
                    op["sem"] = ("d", (e, i % M), 16 * (i // M + 1))
                else:
                    i = ncc
                    ncc += 1
                    op["sem"] = ("cc", i % M, i // M + 1)
        import os as _os3
        if _os3.environ.get("KDEBUG"):
            print("SCHED counts", ccnt, dcnt, ncc, {e: len(v) for e, v in self.per_eng.items()})
        handles = {"pe": nc.tensor, "act": nc.scalar, "dve": nc.vector, "pool": nc.gpsimd, "sp": nc.sync}
        block = es.enter_context(nc.Block())

        def semof(key):
            if key[0] == "c":
                return csem[key[1]]
            if key[0] == "d":
                return dsem[key[1][0]][key[1][1]]
            return ccsem[key[1]]

        def run(e):
            h = handles[e]
            waited = {}
            for op in self.per_eng[e]:
                need = {}
                for d in op["deps"]:
                    p = ops[d]
                    if p["kind"] == "c" and p["eng"] == "pe" and e == "pe" and op["kind"] == "c":
                        continue
                    s = p["sem"]
                    if s is None:
                        continue
                    key = (s[0], s[1])
                    need[key] = max(need.get(key, 0), s[2])
                s = op["sem"]
                if s is not None and s[0] == "d" and s[2] > 16:
                    key = (s[0], s[1])
                    need[key] = max(need.get(key, 0), s[2] - 16)
                if s is not None and s[0] == "cc" and s[2] > 1:
                    key = (s[0], s[1])
                    need[key] = max(need.get(key, 0), s[2] - 1)
                for key, v in need.items():
                    if waited.get(key, 0) < v:
                        h.wait_ge(semof(key), v)
                        waited[key] = v
                ins = getattr(h, op["name"])(**op["kw"])
                if s is not None:
                    if s[0] == "c":
                        ins.then_inc(csem[s[1]], 1)
                    elif s[0] == "d":
                        ins.then_inc(semof((s[0], s[1])), 16)
                    else:
                        ins.then_inc(ccsem[s[1]])
            if e == "sp":
                for q in dsem:
                    for i in range(min(M, dcnt[q])):
                        n_i = (dcnt[q] - 1 - i) // M + 1
                        h.wait_ge(dsem[q][i], 16 * n_i)
                for i in range(min(M, ncc)):
                    h.wait_ge(ccsem[i], (ncc - 1 - i) // M + 1)

        @block.tensor
        def _(t):
            run("pe")

        @block.scalar
        def _(t):
            run("act")

        @block.vector
        def _(t):
            run("dve")

        @block.gpsimd
        def _(t):
            run("pool")

        @block.sync
        def _(t):
            run("sp")


class _Stop(Exception):
    pass


def build_program(stop=99):
    nc = bass.Bass("TRN2", target_bir_lowering=False)
    S = Sched(nc)

    def din(name, shape):
        return nc.dram_tensor(name, list(shape), F32, kind="ExternalInput").ap()

    xin = din("xin", [1024, D])
    xext = din("xext", [1536, D])
    cvec = din("cvecT", [128, 8, 2])
    h0T = din("h0T", [2, 128, DI])
    w_ada = din("w_ada", [2, D, 6 * D])
    b_ada = din("b_ada", [2, 6 * D])
    w_pool = din("w_pool", [1, 4, 256, 256])
    pool_scale = din("pool_scale", [1, D])
    w_in = din("ssd_w_in", [1, D, NCOLS_IN])
    dt_bias = din("ssd_dt_bias", [1, 2, 32])
    a_log = din("ssd_a_log", [1, 2, 32])
    ssd_d = din("ssd_d", [1, 32])
    norm_w = din("ssd_norm_w", [1, DI])
    w_out = din("ssd_w_out", [1, DI, D])
    ffn_w_in = din("ffn_w_in", [2, D, 2 * FF])
    ffn_w_out = din("ffn_w_out", [2, FF, D])
    ln_in = {k: din(k, [2, D]) for k in ("ln1_g", "ln1_b", "ln2_g", "ln2_b")}
    lncT = din("lncT", [128, 8, 8])
    cwl = din("cwl", [128, 32, 5])
    cm = din("cmats", [6, 128, 128])
    p1 = din("p1", [8, 128, 256])
    p2 = din("p2", [NP2, 128, 512])
    invc = din("invc", [1, 4 * 768])
    cmask = din("cmask", [1, 19])
    sel = din("sel", [12, 3])

    y_out = nc.dram_tensor("y_out", [1024, D], F32, kind="ExternalOutput").ap()
    st_out = nc.dram_tensor("st_out", [2, 2, 128, DI], F32, kind="ExternalOutput").ap()
    ag1_src = nc.dram_tensor("ag1_src", [3, D], F32).ap()
    ag1_dst = nc.dram_tensor("ag1_dst", [12, D], F32).ap()
    ag2_src = [nc.dram_tensor(f"ag2_src{g}", [256, 260], F32).ap() for g in range(8)]
    ag2_dst = [nc.dram_tensor(f"ag2_dst{g}", [1024, 260], F32).ap() for g in range(8)]

    with ExitStack() as es:
        def sb(name, shape, dt=F32):
            return es.enter_context(nc.sbuf_tensor(name, list(shape), dt))

        x = sb("x", [128, NT, D])
        hT = sb("hT", [128, 8, 1024], BF16)
        gate_bc = sb("gate_bc", [128, 2, 2, D])
        lnbc = sb("lnbc", [128, 2, D])
        cmf = sb("cmf", [128, 6, 128])
        cmb = sb("cmb", [128, 6, 128], BF16)
        modc = sb("modc", [128, 2, 4, 8, 2])
        lnc = sb("lnc", [128, 8, 8])
        AB = sb("AB", [128, 2, 2, 8])
        siluc = sb("siluc", [128, 8, 2], BF16)
        cT = sb("cT", [128, 8, 2])
        ones_row = sb("ones_row", [1, 128], BF16)
        epsc = sb("epsc", [128, 2])
        maskbc = sb("maskbc", [128, 19])
        stat = sb("stat", [128, 32])
        bnst = sb("bnst", [128, 2, 6])
        pre = sb("pre", [128, D])
        xnbf = sb("xnbf", [128, D], BF16)
        fence_t = sb("fence_t", [128, 2])
        S.fence_tile = fence_t[:]
        wab = [None, None]
        bab = [None, None]
        nwab = [1]
        ARENA = 122 * 1024 // 4
        arena = sb("arena", [128, ARENA])
        apos = [0]

        def carve(nbytes, dt, shape_str=None, **kw):
            n4 = (nbytes + 3) // 4
            a = arena[:, apos[0]:apos[0] + n4]
            apos[0] += n4
            assert apos[0] <= ARENA, ("arena overflow", apos[0] * 4)
            if dt == BF16:
                a = a.bitcast(BF16)
            if shape_str:
                a = a.rearrange(shape_str, **kw)
            return a

        ps = [es.enter_context(nc.psum_tensor(f"ps{i}", [128, 512], F32)) for i in range(8)]

        def psbf(i):
            return ps[i][:, :].bitcast(BF16)

        ident_f = cmf[:, 0, :]
        ident_b = cmb[:, 0, :]
        ones_f = cmf[:, 5, :]

        for t in range(NT):
            S.dma("sp", x[:, t, :], xin[t * 128:(t + 1) * 128, :], writes=[f"x{t}"])
        S.dma("sp", cmf[:], cm.rearrange("c p f -> p c f"), writes=["cmf"])
        S.dma("pool", cmb[:], cm.rearrange("c p f -> p c f"), writes=["cmb"])
        S.dma("sp", cT[:], cvec, writes=["cT"])
        S.dma("sp", lnc[:], lncT, writes=["lnc"])
        S.dma("sp", maskbc[:], cmask[0, :].partition_broadcast(128), writes=["maskbc"])
        S.op("dve", "memset", writes=["ones_row"], ap=ones_row[:], constant=1.0)
        S.op("dve", "memset", writes=["epsc"], ap=epsc[:, 0:1], constant=LN_EPS)
        S.op("dve", "memset", writes=["epsc"], ap=epsc[:, 1:2], constant=RMS_EPS)
        S.op("act", "activation", reads=["cT"], writes=["siluc"], out=siluc[:], in_=cT[:], func=AF.Silu)

        wab_i = [0]

        def mod_halfblock(l, nb, hf):
            i = wab_i[0] % nwab[0]
            wab_i[0] += 1
            c0 = nb * 1024 + hf * 512
            S.dma("pool", wab[i][:], w_ada[l, :, c0:c0 + 512].rearrange("(kc p) n -> p kc n", p=128),
                  writes=[f"wab{i}"])
            S.dma("pool", bab[i][0:1, :], b_ada[l:l + 1, c0:c0 + 512], writes=[f"bab{i}"])
            if nb in (2, 5):
                which = 0 if nb == 2 else 1
                for c in range(2):
                    for kc in range(8):
                        S.op("pe", "matmul", reads=["siluc", f"wab{i}"], writes=["ps6"],
                             out=ps[6][:, :], lhsT=siluc[:, kc, c:c + 1].to_broadcast([128, 128]),
                             rhs=wab[i][:, kc, :], start=(kc == 0), stop=False)
                    S.op("pe", "matmul", reads=["ones_row", f"bab{i}"], writes=["ps6"],
                         out=ps[6][:, :], lhsT=ones_row[0:1, :], rhs=bab[i][0:1, :], start=False, stop=True)
                    S.op("act", "activation", reads=["ps6"], writes=[f"gate{which}{c}"],
                         out=gate_bc[:, which, c, hf * 512:(hf + 1) * 512], in_=ps[6][:, :], func=AF.Identity)
            else:
                v = {0: 0, 1: 1, 3: 2, 4: 3}[nb]
                for cc in range(4):
                    kcx = hf * 4 + cc
                    o = ps[7][:, (v * 8 + kcx) * 2:(v * 8 + kcx) * 2 + 2]
                    for kc in range(8):
                        S.op("pe", "matmul", reads=["siluc", f"wab{i}"], writes=["ps7"],
                             out=o, lhsT=wab[i][:, kc, cc * 128:(cc + 1) * 128], rhs=siluc[:, kc, :],
                             start=(kc == 0), stop=False)
                    S.op("pe", "matmul", reads=["ones_row", f"bab{i}"], writes=["ps7"],
                         out=o, lhsT=bab[i][0:1, cc * 128:(cc + 1) * 128], rhs=ones_row[0:1, 0:2],
                         start=False, stop=True)

        def mod_finish(l):
            S.op("dve", "tensor_copy", reads=["ps7"], writes=[f"modc{l}"],
                 out=modc[:, l].rearrange("p v k c -> p (v k c)"), in_=ps[7][:, 0:64])

        def make_AB(l_ln, which_ln, l_mod, vsh, vsc):
            g_i = l_ln * 4 + (0 if which_ln == 1 else 2)
            for c in range(2):
                S.op("dve", "scalar_tensor_tensor", reads=[f"modc{l_mod}", "lnc"], writes=["AB"],
                     out=AB[:, 0, c, :], in0=modc[:, l_mod, vsc, :, c], scalar=1.0, in1=lnc[:, g_i, :],
                     op0=ALU.add, op1=ALU.mult)
                S.op("dve", "scalar_tensor_tensor", reads=[f"modc{l_mod}", "lnc"], writes=["AB"],
                     out=AB[:, 1, c, :], in0=modc[:, l_mod, vsc, :, c], scalar=1.0, in1=lnc[:, g_i + 1, :],
                     op0=ALU.add, op1=ALU.mult)
                S.op("dve", "tensor_tensor", reads=["AB", f"modc{l_mod}"], writes=["AB"],
                     out=AB[:, 1, c, :], in0=AB[:, 1, c, :], in1=modc[:, l_mod, vsh, :, c], op=ALU.add)

        def load_lnbc(l, which):
            kg = "ln1_g" if which == 1 else "ln2_g"
            kb = "ln1_b" if which == 1 else "ln2_b"
            S.dma("sp", lnbc[:, 0, :], ln_in[kg][l, :].partition_broadcast(128), writes=["lnbc"])
            S.dma("sp", lnbc[:, 1, :], ln_in[kb][l, :].partition_broadcast(128), writes=["lnbc"])

        def ln_epilogue(t, sub_reads, sub_parts, which_gate, do_T, tp_bank, out_dram=None):
            c = 0 if t < 4 else 1
            gname = f"gate{which_gate}{c}"
            for (src, lo, hi) in sub_parts:
                S.op("dve", "tensor_tensor", reads=list(sub_reads) + [gname], writes=["pre"],
                     out=pre[:, lo:hi], in0=src, in1=gate_bc[:, which_gate, c, lo:hi], op=ALU.mult)
            S.op("dve", "scalar_tensor_tensor", reads=["pre", f"x{t}"], writes=["pre"],
                 out=pre[:], in0=x[:, t, :], scalar=ALPHA, in1=pre[:], op0=ALU.mult, op1=ALU.add)
            for hh in range(2):
                S.op("dve", "bn_stats", reads=["pre"], writes=["bnst"], out=bnst[:, hh, :],
                     in_=pre[:, hh * 512:(hh + 1) * 512])
            S.op("dve", "bn_aggr", reads=["bnst"], writes=["stat"], out=stat[:, 0:2],
                 in_=bnst[:].rearrange("p a b -> p (a b)"))
            S.op("act", "activation", reads=["stat", "epsc"], writes=["stat_sd"], out=stat[:, 2:3],
                 in_=stat[:, 1:2], func=AF.Sqrt, bias=epsc[:, 0:1], scale=1.0)
            S.op("dve", "reciprocal", reads=["stat_sd"], writes=["stat_r"], out=stat[:, 3:4], in_=stat[:, 2:3])
            S.op("dve", "scalar_tensor_tensor", reads=["stat", "stat_r"], writes=["stat_n"],
                 out=stat[:, 4:5], in0=stat[:, 0:1], scalar=-1.0, in1=stat[:, 3:4], op0=ALU.mult, op1=ALU.mult)
            S.op("act", "activation", reads=["pre", "stat_r", "stat_n"], writes=["pre"], out=pre[:],
                 in_=pre[:], func=AF.Identity, bias=stat[:, 4:5], scale=stat[:, 3:4])
            S.op("pool", "tensor_tensor", reads=["pre", "lnbc"], writes=[f"x{t}"], out=x[:, t, :],
                 in0=pre[:], in1=lnbc[:, 0, :], op=ALU.mult)
            S.op("pool", "tensor_tensor", reads=[f"x{t}", "lnbc"], writes=[f"x{t}"], out=x[:, t, :],
                 in0=x[:, t, :], in1=lnbc[:, 1, :], op=ALU.add)
            if out_dram is not None:
                S.dma("sp", out_dram, x[:, t, :], reads=[f"x{t}"], writes=[f"yout{t}"])
            if do_T:
                S.op("act", "activation", reads=["pre"], writes=["xnbf"], out=xnbf[:], in_=pre[:],
                     func=AF.Identity)
                pb = psbf(tp_bank)
                for kc in range(8):
                    S.op("pe", "transpose", reads=["xnbf", "cmb"], writes=[f"ps{tp_bank}"],
                         out=pb[:, kc * 128:(kc + 1) * 128], in_=xnbf[:, kc * 128:(kc + 1) * 128],
                         identity=ident_b)
                for kc in range(8):
                    if kc % 2 == 0:
                        S.op("act", "activation", reads=[f"ps{tp_bank}", "AB"], writes=[f"hT{t}"],
                             out=hT[:, kc, t * 128:(t + 1) * 128], in_=pb[:, kc * 128:(kc + 1) * 128],
                             func=AF.Identity, scale=AB[:, 0, c, kc:kc + 1], bias=AB[:, 1, c, kc:kc + 1])
                    else:
                        S.op("dve", "tensor_scalar", reads=[f"ps{tp_bank}", "AB"], writes=[f"hT{t}"],
                             out=hT[:, kc, t * 128:(t + 1) * 128], in0=pb[:, kc * 128:(kc + 1) * 128],
                             scalar1=AB[:, 0, c, kc:kc + 1], scalar2=AB[:, 1, c, kc:kc + 1],
                             op0=ALU.mult, op1=ALU.add)

        if stop == 0:
            S.op("act", "activation", reads=["cmf"], writes=["x0"], out=x[:, 0, 0:128], in_=cmf[:, 1, :], func=AF.Identity)
            for t_ in range(NT):
                S.dma("sp", y_out[t_ * 128:(t_ + 1) * 128, :], x[:, t_, :], reads=[f"x{t_}"], writes=[f"yout{t_}"])
            S.emit(es)
            return nc
        apos[0] = 0
        for i2 in range(2):
            wab[i2] = carve(8 * 1024, BF16, "p (k n) -> p k n", k=8)
            bab[i2] = carve(1024, BF16)
        nwab[0] = 2
        S.op("dve", "memset", writes=["ps7"], ap=ps[7][:, 0:64], constant=0.0)
        for nb in range(1, 6):
            for hf in range(2):
                mod_halfblock(0, nb, hf)
        mod_finish(0)
        xbf = carve(16 * 2048, BF16, "p (t d) -> p t d", t=16)
        p1b = carve(8 * 512, BF16, "p (b n) -> p b n", b=8)
        p2b = carve(NP2 * 1024, BF16, "p (b n) -> p b n", b=NP2)
        wpb = carve(8 * 512, BF16, "p (b n) -> p b n", b=8)
        mixT = carve(8 * 2048, BF16, "p (c n) -> p c n", c=8)
        invbc = carve(4 * 768 * 4, F32, "p (g n) -> p g n", g=4)
        psbc = carve(4096, F32)
        for t in range(4):
            S.dma("pool", xbf[:, t, :], xin[t * 128:(t + 1) * 128, :], writes=[f"xbf{t}"])
        for t in range(12):
            S.dma("pool", xbf[:, 4 + t, :], xext[t * 128:(t + 1) * 128, :], writes=[f"xbf{4 + t}"])
        S.dma("pool", p1b[:], p1.rearrange("b p n -> p b n"), writes=["p1b"])
        S.dma("pool", p2b[:], p2.rearrange("b p n -> p b n"), writes=["p2b"])
        S.dma("pool", wpb[:], w_pool[0].rearrange("g (c p) n -> p (g c) n", p=128), writes=["wpb"])
        S.dma("sp", invbc[:].rearrange("p g n -> p (g n)"), invc[0, :].partition_broadcast(128), writes=["invbc"])
        S.dma("sp", psbc, pool_scale[0, :].partition_broadcast(128), writes=["psbc"])
        for c in range(2):
            S.op("dve", "tensor_tensor", reads=["gate0%d" % c, "psbc"], writes=["gate0%d" % c],
                 out=gate_bc[:, 0, c, :], in0=gate_bc[:, 0, c, :], in1=psbc, op=ALU.mult)
        sc1p = sb("sc1p", [128, 8, 2])
        S.op("dve", "tensor_scalar", reads=["modc0"], writes=["sc1p"], out=sc1p[:], in0=modc[:, 0, 1, :, :],
             scalar1=1.0, scalar2=None, op0=ALU.add)
        make_AB(0, 1, 0, 2, 3)
        load_lnbc(0, 1)
        pbank = [0]
        for si, (tiles, ext) in enumerate(SEQS):
            c = 1 if ext else 0
            nout = 512 if ext else 256
            o0 = tiles[0] * 128
            for g in range(4):
                if ext:
                    lo, hi = P2_TILES[g]
                    srcs = [(4 + tin, p2b[:, P2_BLK0[g] + tin - lo, :], "p2b") for tin in range(lo, hi)]
                    inv = invbc[:, g, 256:768]
                else:
                    srcs = [(tiles[0] + j, p1b[:, g * 2 + j, :], "p1b") for j in range(2)]
                    inv = invbc[:, g, 0:256]
                for cc in range(2):
                    ch = g * 2 + cc
                    bk = pbank[0] % 2
                    pbank[0] += 1
                    for n, (xt, blk, bn) in enumerate(srcs):
                        S.op("pe", "matmul", reads=[f"xbf{xt}", bn], writes=[f"ps{bk}"],
                             out=ps[bk][:, 0:nout], lhsT=xbf[:, xt, ch * 128:(ch + 1) * 128], rhs=blk,
                             start=(n == 0), stop=(n == len(srcs) - 1))
                    S.op("dve", "scalar_tensor_tensor", reads=[f"ps{bk}", "sc1p", "invbc"],
                         writes=[f"mixT{ch}_{si}"], out=mixT[:, ch, o0:o0 + nout], in0=ps[bk][:, 0:nout],
                         scalar=sc1p[:, ch, c:c + 1], in1=inv, op0=ALU.mult, op1=ALU.mult)
        for t in range(NT):
            si = 0 if t < 2 else (1 if t < 4 else 2)
            b0 = 2 + 2 * (t % 2)
            for g in range(4):
                for cc in range(2):
                    ch = g * 2 + cc
                    bk = b0 + (g // 2)
                    S.op("pe", "matmul", reads=[f"mixT{ch}_{si}", "wpb"], writes=[f"ps{bk}"],
                         out=ps[bk][:, (g % 2) * 256:(g % 2) * 256 + 256], lhsT=mixT[:, ch, t * 128:(t + 1) * 128],
                         rhs=wpb[:, ch, :], start=(cc == 0), stop=(cc == 1))
            ln_epilogue(t, [f"ps{b0}", f"ps{b0 + 1}"], [(ps[b0][:, :], 0, 512), (ps[b0 + 1][:, :], 512, 1024)],
                        0, True, 6)

        if stop == 1:
            for t_ in range(NT):
                S.dma("sp", y_out[t_ * 128:(t_ + 1) * 128, :], x[:, t_, :], reads=[f"x{t_}"], writes=[f"yout{t_}"])
            S.emit(es)
            return nc
        def ffn(l, mid_hook, post_AB, final, pre_hook=None):
            S.fence()
            apos[0] = 0
            hid = carve(NFC * 2048, BF16, "p (f n) -> p f n", f=NFC)
            wo = carve(2 * NFC * 1024, BF16, "p (h f n) -> p h f n", h=2, f=NFC)
            wi = [carve(8 * 512, BF16, "p (k n) -> p k n", k=8) for _ in range(3)]
            sg = [carve(2048, F32) for _ in range(2)]
            for i2 in range(2):
                wab[i2] = carve(8 * 1024, BF16, "p (k n) -> p k n", k=8)
                bab[i2] = carve(1024, BF16)
            nwab[0] = 2
            if pre_hook is not None:
                pre_hook()
            load_lnbc(l, 2)
            for fc in range(NFC):
                i = fc % 3
                S.dma("pool", wi[i][:, :, 0:128],
                      ffn_w_in[l, :, fc * 128:(fc + 1) * 128].rearrange("(kc p) n -> p kc n", p=128),
                      writes=[f"wi{i}"])
                S.dma("pool", wi[i][:, :, 128:256],
                      ffn_w_in[l, :, FF + fc * 128:FF + (fc + 1) * 128].rearrange("(kc p) n -> p kc n", p=128),
                      writes=[f"wi{i}"])
                S.dma("pool", wo[:, :, fc, :],
                      ffn_w_out[l, fc * 128:(fc + 1) * 128, :].rearrange("p (h n) -> p h n", h=2),
                      writes=[f"wo_{fc}"])
                for th in range(2):
                    pg = (fc * 2 + th) % 2
                    bg, bu = pg * 2, pg * 2 + 1
                    hreads = [f"hT{t}" for t in range(th * 4, th * 4 + 4)]
                    for kc in range(8):
                        S.op("pe", "matmul", reads=hreads + [f"wi{i}"], writes=[f"ps{bg}"],
                             out=ps[bg][:, :], lhsT=wi[i][:, kc, 0:128], rhs=hT[:, kc, th * 512:(th + 1) * 512],
                             start=(kc == 0), stop=(kc == 7))
                    for kc in range(8):
                        S.op("pe", "matmul", reads=hreads + [f"wi{i}"], writes=[f"ps{bu}"],
                             out=ps[bu][:, :], lhsT=wi[i][:, kc, 128:256], rhs=hT[:, kc, th * 512:(th + 1) * 512],
                             start=(kc == 0), stop=(kc == 7))
                    S.op("act", "activation", reads=[f"ps{bg}"], writes=[f"sg{pg}"], out=sg[pg], in_=ps[bg][:, :],
                         func=AF.Silu)
                    S.op("dve", "tensor_tensor", reads=[f"sg{pg}", f"ps{bu}"], writes=[f"hid{fc}_{th}"],
                         out=hid[:, fc, th * 512:(th + 1) * 512], in0=sg[pg], in1=ps[bu][:, :], op=ALU.mult)
                if mid_hook is not None:
                    mid_hook(fc)
            if post_AB is not None:
                post_AB()
            for t in range(NT):
                th = t // 4
                b0 = (t % 2) * 2
                for dh in range(2):
                    for fc in range(NFC):
                        S.op("pe", "matmul", reads=[f"hid{fc}_{th}", f"wo_{fc}"], writes=[f"ps{b0 + dh}"],
                             out=ps[b0 + dh][:, :], lhsT=hid[:, fc, t * 128:(t + 1) * 128], rhs=wo[:, dh, fc, :],
                             start=(fc == 0), stop=(fc == NFC - 1))
                ln_epilogue(t, [f"ps{b0}", f"ps{b0 + 1}"], [(ps[b0][:, :], 0, 512), (ps[b0 + 1][:, :], 512, 1024)],
                            1, not final, 4 + (t % 2),
                            out_dram=(y_out[t * 128:(t + 1) * 128, :] if final else None))

        l1_blocks = [(nb, hf) for nb in range(5) for hf in range(2)]

        def hook0(fc):
            if fc < 10:
                nb, hf = l1_blocks[fc]
                mod_halfblock(1, nb, hf)
            if fc == 10:
                mod_finish(1)

        ffn(0, hook0, lambda: make_AB(0, 2, 1, 0, 1), False)

        if stop == 2:
            for t_ in range(NT):
                S.dma("sp", y_out[t_ * 128:(t_ + 1) * 128, :], x[:, t_, :], reads=[f"x{t_}"], writes=[f"yout{t_}"])
            S.emit(es)
            return nc
        S.fence()
        apos[0] = 0
        ynT = carve(16 * 2048, BF16, "p (c n) -> p c n", c=16)
        mark_grp = apos[0]
        wob_early = arena[:, mark_grp:mark_grp + 8192].bitcast(BF16).rearrange("p (c n) -> p c n", c=16)
        W = carve(8 * 776 * 2, BF16, "p (k n) -> p k n", k=8)
        wname = "wg"
        xTg = carve(2 * 2048, BF16, "p (c n) -> p c n", c=2)
        BCT = carve(2 * 2048, BF16, "p (c n) -> p c n", c=2)
        xs_tm = carve(NT * 512, BF16, "p (t n) -> p t n", t=NT)
        B_tm = carve(NT * 256, BF16, "p (t n) -> p t n", t=NT)
        siluz = carve(NT * 512, BF16, "p (t n) -> p t n", t=NT)
        cacc = carve(2048, F32)
        hTh = carve(8 * 4 * 2, BF16, "p (k n) -> p k n", k=8)
        x1g = pre[0:12, :]
        selb = sb("selb", [12, 3])
        cwb = carve(32 * 5 * 4, F32, "p (c k) -> p c k", c=32)
        dtb = carve(64 * 4, F32)
        abc = carve(64 * 4, F32)
        dbc = carve(32 * 4, F32)
        nwg = carve(256 * 4, F32)
        dtt = carve(NT * 8 * 4, F32, "p (t n) -> p t n", t=NT)
        smA = carve(NT * 16 * 4, F32, "p (t n) -> p t n", t=NT)
        nacsA, eaA, dteA, cdecA, w2A, daA = [carve(NT * 8 * 4, F32, "p (t n) -> p t n", t=NT) for _ in range(6)]
        dahl = carve(NT * 16 * 2, BF16, "p (t n) -> p t n", t=NT)
        spt = carve(4 * 64 * 4, F32)
        xsdt = [carve(512, BF16) for _ in range(2)]
        xsst = [carve(512, BF16) for _ in range(2)]
        GT = carve(512, F32)
        Lsum = [pre[:, 0:512], carve(2048, F32)]
        GL4 = [xnbf[:, 0:512], xnbf[:, 512:1024]]
        S_sb = carve(NT * 2 * 1024, F32, "p (t d n) -> p t d n", t=NT, d=2)
        ydiag = carve(NT * 1024, F32, "p (t n) -> p t n", t=NT)
        Hst = [carve(1024, F32) for _ in range(2)]
        Hbf = [carve(512, BF16) for _ in range(2)]
        Zst = carve(2 * 1040, F32, "p (d n) -> p d n", d=2)
        gat = carve(8 * 1040, F32, "p (j d n) -> p j d n", j=4, d=2)
        h0g = carve(2 * 1024, F32, "p (d n) -> p d n", d=2)
        aj = carve(32 * 4, F32, "p (j d r) -> p j d r", j=4, d=2)
        ysum = gate_bc[:, 1, 0, :]
        yz = gate_bc[:, 1, 1, :]
        ynb = lnbc[:, 0, :].bitcast(BF16)[:, 0:1024]
        sq = carve(1024, F32)
        ytmp = [carve(1024, F32) for _ in range(2)]
        lnbf = lnbc[:, 0, :].bitcast(BF16)
        GTb = [GT, sq[:, 0:128]]
        xsdtb = [xsdt, [carve(512, BF16) for _ in range(2)]]
        xsstb = [xsst, [carve(512, BF16) for _ in range(2)]]
        Lsumb = [Lsum, [lnbc[:, 1, 0:512], lnbc[:, 1, 512:1024]]]
        GL4b = [GL4, [lnbf[:, 1024:1536], lnbf[:, 1536:2048]]]

        print("SSD arena bytes used", apos[0] * 4, "of", ARENA * 4)
        S.dma("sp", cwb[:], cwl, writes=["cwb"])
        S.dma("sp", dtb, dt_bias[0].rearrange("d h -> (d h)").partition_broadcast(128), writes=["dtb"])
        S.dma("sp", abc, a_log[0].rearrange("d h -> (d h)").partition_broadcast(128), writes=["abc"])
        S.dma("sp", dbc, ssd_d[0, :].partition_broadcast(128), writes=["dbc"])
        S.dma("sp", selb[:], sel, writes=["selb"])
        S.op("act", "activation", reads=["abc"], writes=["abc"], out=abc, in_=abc, func=AF.Exp)
        S.op("dve", "tensor_scalar", reads=["abc"], writes=["abc"], out=abc, in0=abc, scalar1=-1.0, scalar2=None,
             op0=ALU.mult)

        def wsl(c0, n):
            return w_in[0, :, c0:c0 + n].rearrange("(kc p) n -> p kc n", p=128)

        def load_wg(g):
            S.dma("pool", W[:, :, 0:256], wsl(DI + g * 256, 256), writes=[wname])
            S.dma("pool", W[:, :, 256:384], wsl(2 * DI + g * 128, 128), writes=[wname])
            S.dma("pool", W[:, :, 384:512], wsl(2 * DI + 1024 + g * 128, 128), writes=[wname])
            S.dma("pool", W[:, :, 512:768], wsl(g * 256, 256), writes=[wname])
            for d in range(2):
                S.dma("pool", W[:, :, 768 + d * 4:772 + d * 4], wsl(DI + CONV_CH + d * 32 + g * 4, 4), writes=[wname])

        load_wg(0)
        S.dma("sp", ag1_src[0:1, :], x[0:1, 4, :], reads=["x4"], writes=["ag1_src"])
        S.dma("sp", ag1_src[1:3, :], x[126:128, 7, :], reads=["x7"], writes=["ag1_src"])
        S.cc(reads=["ag1_src"], writes=["ag1_dst"], kind="AllGather", op=ALU.bypass, replica_groups=GROUPS4,
             ins=[ag1_src.opt()], outs=[ag1_dst.opt()])
        def emit_halo():
            S.dma("sp", x1g, ag1_dst, reads=["ag1_dst"], writes=["pre"])
            sc1h = sb("sc1h", [128, 8])
            S.op("dve", "tensor_scalar", reads=["modc1"], writes=["sc1h"], out=sc1h[:], in0=modc[:, 1, 1, :, 1],
                 scalar1=1.0, scalar2=None, op0=ALU.add)
            for kc in range(8):
                S.op("pe", "matmul", reads=["pre", "selb"], writes=["ps7"], out=ps[7][:, 64 + kc * 4:64 + kc * 4 + 3],
                     lhsT=x1g[:, kc * 128:(kc + 1) * 128], rhs=selb[:], start=True, stop=True)
            for kc in range(8):
                S.op("dve", "tensor_scalar", reads=["ps7", "sc1h", "modc1"], writes=["hTh32"],
                     out=sq[:, kc * 4:kc * 4 + 3], in0=ps[7][:, 64 + kc * 4:64 + kc * 4 + 3],
                     scalar1=sc1h[:, kc:kc + 1], scalar2=modc[:, 1, 0, kc, 1:2], op0=ALU.mult, op1=ALU.add)
                S.op("dve", "tensor_tensor", reads=["hTh32", "maskbc"], writes=["hTh"], out=hTh[:, kc, 0:3],
                     in0=sq[:, kc * 4:kc * 4 + 3], in1=maskbc[:, 16:19], op=ALU.mult)

        import os as _os2
        KSUB = int(_os2.environ.get("KSUB", "0"))
        KSKIP = _os2.environ.get("KSKIP", "")
        if KSUB == 1:
            for t_ in range(NT):
                S.dma("sp", y_out[t_ * 128:(t_ + 1) * 128, :], x[:, t_, :], reads=[f"x{t_}"], writes=[f"yout{t_}"])
            S.emit(es)
            return nc
        make_AB(1, 1, 1, 2, 3)

        def bc4(ap4, n=64):
            return ap4.unsqueeze(2).to_broadcast([128, 4, n])

        def v3(ap, n=64):
            return ap.rearrange("p (r e) -> p r e", e=n)

        import os as _os
        NG = int(_os.environ.get("KGROUPS", "8"))
        for g in range(NG):
            S.dma("sp", nwg, norm_w[0, g * 256:(g + 1) * 256].partition_broadcast(128), writes=["nwg"])
            for th, ci in [(th_, ci_) for th_ in range(2) for ci_ in range(4)]:
                chg = (g * 2 + ci) if ci < 2 else (16 + g if ci == 2 else 24 + g)
                if g == 0 and th == 1 and ci == 0:
                    emit_halo()
                if True:
                    bk = ci % 2
                    hreads = [f"hT{t}" for t in range(th * 4, th * 4 + 4)]
                    for kc in range(8):
                        S.op("pe", "matmul", reads=hreads + [wname], writes=[f"ps{bk}"], out=ps[bk][:, :],
                             lhsT=W[:, kc, ci * 128:(ci + 1) * 128], rhs=hT[:, kc, th * 512:(th + 1) * 512],
                             start=(kc == 0), stop=(kc == 7))
                    if th == 1 and 'h' not in KSKIP:
                        for kc in range(8):
                            S.op("pe", "matmul", reads=["hTh", wname], writes=["ps7h"], out=ps[7][:, 128:131],
                                 lhsT=W[:, kc, ci * 128:(ci + 1) * 128], rhs=hTh[:, kc, 0:3],
                                 start=(kc == 0), stop=(kc == 7))
                    P = ps[bk]
                    cw = cwb[:, chg, :]
                    S.op("act", "activation", reads=[f"ps{bk}", "cwb"], writes=["cacc"], out=cacc, in_=P[:, :],
                         func=AF.Identity, scale=cw[:, 2:3])
                    segs = [(0, 256), (256, 256)] if th == 0 else [(0, 512)]
                    for (o, L) in segs:
                        for k, s_ in ((0, -2), (1, -1), (3, 1)):
                            lo = o + max(0, -s_)
                            hi = o + min(L, L - s_)
                            S.op("dve", "scalar_tensor_tensor", reads=[f"ps{bk}", "cwb", "cacc"], writes=["cacc"],
                                 out=cacc[:, lo:hi], in0=P[:, lo + s_:hi + s_], scalar=cw[:, k:k + 1], in1=cacc[:, lo:hi],
                                 op0=ALU.mult, op1=ALU.add)
                    if th == 1 and 'h' not in KSKIP:
                        Hh = ps[7]
                        for (dst, hc, k) in ((0, 128, 0), (1, 129, 0), (0, 129, 1), (511, 130, 3)):
                            S.op("dve", "scalar_tensor_tensor", reads=["ps7h", "cwb", "cacc"], writes=["cacc"],
                                 out=cacc[:, dst:dst + 1], in0=Hh[:, hc:hc + 1], scalar=cw[:, k:k + 1],
                                 in1=cacc[:, dst:dst + 1], op0=ALU.mult, op1=ALU.add)
                    if ci < 2:
                        dst, dname = xTg[:, ci, th * 512:(th + 1) * 512], f"xTg{ci}_{th}"
                    else:
                        dst, dname = BCT[:, ci - 2, th * 512:(th + 1) * 512], f"BCT{ci - 2}_{th}"
                    S.op("act", "activation", reads=["cacc", "cwb"], writes=[dname], out=dst, in_=cacc, func=AF.Silu,
                         bias=cw[:, 4:5], scale=1.0)
            for t in (range(NT) if 't' not in KSKIP else []):
                th = t // 4
                tb = 6 if t % 2 == 0 else 4
                pb = psbf(tb)
                for ci in range(2):
                    S.op("pe", "transpose", reads=[f"xTg{ci}_{th}", "cmb"], writes=[f"ps{tb}"],
                         out=pb[:, ci * 128:(ci + 1) * 128], in_=xTg[:, ci, t * 128:(t + 1) * 128], identity=ident_b)
                S.op("pe", "transpose", reads=[f"BCT0_{th}", "cmb"], writes=[f"ps{tb}"],
                     out=pb[:, 256:384], in_=BCT[:, 0, t * 128:(t + 1) * 128], identity=ident_b)
                S.op("act", "activation", reads=[f"ps{tb}"], writes=[f"xs_tm{t}"], out=xs_tm[:, t, :], in_=pb[:, 0:256],
                     func=AF.Identity)
                S.op("act", "activation", reads=[f"ps{tb}"], writes=[f"B_tm{t}"], out=B_tm[:, t, :], in_=pb[:, 256:384],
                     func=AF.Identity)
            vv, av, ev, dtraw = spt[:, 0:64], spt[:, 64:128], spt[:, 128:192], spt[:, 192:256]
            t8 = lambda ap: ap.rearrange("p (t n) -> p t n", t=NT)
            for t in range(NT):
                zb = 2 + t % 2
                for kc in range(8):
                    S.op("pe", "matmul", reads=[f"hT{t}", wname], writes=[f"ps{zb}"], out=ps[zb][:, 0:264],
                         lhsT=hT[:, kc, t * 128:(t + 1) * 128], rhs=W[:, kc, 512:776], start=(kc == 0), stop=(kc == 7))
                S.op("act", "activation", reads=[f"ps{zb}"], writes=[f"siluz{t}"], out=siluz[:, t, :],
                     in_=ps[zb][:, 0:256], func=AF.Silu)
                S.op("act", "activation", reads=[f"ps{zb}"], writes=["dtraw"], out=dtraw[:, t * 8:(t + 1) * 8],
                     in_=ps[zb][:, 256:264], func=AF.Identity)
            for d in range(2):
                S.op("dve", "tensor_tensor", reads=["dtraw", "dtb"], writes=["spt_v"], out=t8(vv)[:, :, d * 4:d * 4 + 4],
                     in0=t8(dtraw)[:, :, d * 4:d * 4 + 4],
                     in1=dtb[:, d * 32 + g * 4:d * 32 + g * 4 + 4].unsqueeze(1).to_broadcast([128, NT, 4]), op=ALU.add)
            S.op("dve", "scalar_tensor_tensor", reads=["spt_v"], writes=["spt_a"], out=av, in0=vv, scalar=-1.0,
                 in1=vv, op0=ALU.mult, op1=ALU.max)
            S.op("act", "activation", reads=["spt_a"], writes=["spt_e"], out=ev, in_=av, func=AF.Exp, scale=-1.0)
            S.op("act", "activation", reads=["spt_e"], writes=["spt_e"], out=ev, in_=ev, func=AF.Ln, bias=1.0, scale=1.0)
            S.op("dve", "scalar_tensor_tensor", reads=["spt_v", "spt_e"], writes=["dtt"],
                 out=dtt[:].rearrange("p t n -> p (t n)"), in0=vv, scalar=0.0, in1=ev, op0=ALU.max, op1=ALU.add)
            if KSUB == 2:
                for t_ in range(NT):
                    S.dma("sp", y_out[t_ * 128:(t_ + 1) * 128, :], x[:, t_, :], reads=[f"x{t_}"], writes=[f"yout{t_}"])
                S.emit(es)
                return nc
            if g + 1 < NG:
                load_wg(g + 1)
            else:
                S.dma("pool", wob_early[:, 0:8, :], w_out[0, 0:1024, :].rearrange("(c p) n -> p c n", p=128),
                      writes=["wg", "xTg0_0", "xTg0_1", "xTg1_0", "xTg1_1", "wob_lo"])
            fl = lambda ap: ap.rearrange("p t n -> p (t n)")
            for d in range(2):
                S.op("dve", "tensor_tensor", reads=["dtt", "abc"], writes=["da"], out=daA[:, :, d * 4:d * 4 + 4],
                     in0=dtt[:, :, d * 4:d * 4 + 4],
                     in1=abc[:, d * 32 + g * 4:d * 32 + g * 4 + 4].unsqueeze(1).to_broadcast([128, NT, 4]), op=ALU.mult)
            for t in range(NT):
                for d in range(2):
                    S.op("pe", "matmul", reads=["da", "cmf"], writes=["ps7a"],
                         out=ps[7][:, 256 + t * 16 + d * 4:256 + t * 16 + d * 4 + 4],
                         lhsT=cmf[:, 1 + d, :], rhs=daA[:, t, d * 4:d * 4 + 4], start=True, stop=True)
                S.op("pe", "matmul", reads=["da", "cmf"], writes=["ps7a"], out=ps[7][:, 256 + t * 16 + 8:256 + t * 16 + 16],
                     lhsT=ones_f, rhs=daA[:, t, :], start=True, stop=True)
            S.op("dve", "tensor_copy", reads=["ps7a"], writes=["smA"], out=fl(smA), in_=ps[7][:, 256:384])
            S.op("dve", "tensor_scalar", reads=["smA"], writes=["nacs"], out=nacsA[:], in0=smA[:, :, 0:8], scalar1=-1.0,
                 scalar2=None, op0=ALU.mult)
            S.op("act", "activation", reads=["smA"], writes=["ea"], out=eaA[:], in_=smA[:, :, 0:8], func=AF.Exp)
            S.op("dve", "tensor_tensor", reads=["smA"], writes=["dte"], out=dteA[:], in0=smA[:, :, 8:16], in1=smA[:, :, 0:8],
                 op=ALU.subtract)
            S.op("act", "activation", reads=["dte"], writes=["dte"], out=dteA[:], in_=dteA[:], func=AF.Exp)
            S.op("act", "activation", reads=["smA"], writes=["cdec"], out=cdecA[:], in_=smA[:, :, 8:16], func=AF.Exp)
            S.op("dve", "tensor_tensor", reads=["dte", "dtt"], writes=["w2"], out=w2A[:], in0=dteA[:], in1=dtt[:], op=ALU.mult)
            S.op("dve", "tensor_copy", reads=["da"], writes=["dahl"], out=dahl[:, :, 0:8], in_=daA[:])
            S.op("dve", "tensor_tensor", reads=["da", "dahl"], writes=["dahl"], out=dahl[:, :, 8:16], in0=daA[:],
                 in1=dahl[:, :, 0:8], op=ALU.subtract)
            def diag_steps(t, par):
                th = t // 4
                tsl = slice(t * 128, (t + 1) * 128)
                GTp, gtn = GTb[par], f"GT{par}"
                yb = 3 if par == 0 else 7
                gtps = ps[3][:, 256:384] if par == 0 else ps[7][:, 384:512]
                lbs = (4, 2) if par == 0 else (0, 6)
                xd, xs_ = xsdtb[par], xsstb[par]

                def stepA():
                    S.op("pe", "matmul", reads=[f"BCT0_{th}", f"BCT1_{th}"], writes=[f"ps{yb}"], out=gtps,
                         lhsT=BCT[:, 0, tsl], rhs=BCT[:, 1, tsl], start=True, stop=True)
                    S.op("act", "activation", reads=[f"ps{yb}"], writes=[gtn], out=GTp, in_=gtps, func=AF.Identity)
                    for d in range(2):
                        S.op("pool", "tensor_tensor", reads=[f"xs_tm{t}", "dtt"], writes=[f"xsdt{par}{d}"],
                             out=v3(xd[d]), in0=v3(xs_tm[:, t, :]), in1=bc4(dtt[:, t, d * 4:d * 4 + 4]), op=ALU.mult)
                        S.op("dve", "tensor_tensor", reads=[f"xs_tm{t}", "w2"], writes=[f"xsst{par}{d}"],
                             out=v3(xs_[d]), in0=v3(xs_tm[:, t, :]), in1=bc4(w2A[:, t, d * 4:d * 4 + 4]), op=ALU.mult)
                        sbk = 5 if d == 0 else 1
                        S.op("pe", "matmul", reads=[f"B_tm{t}", f"xsst{par}{d}"], writes=[f"ps{sbk}"],
                             out=ps[sbk][:, 0:256], lhsT=B_tm[:, t, :], rhs=xs_[d], start=True, stop=True)
                        S.op("act", "activation", reads=[f"ps{sbk}"], writes=[f"S{t}_{d}"], out=S_sb[:, t, d, :],
                             in_=ps[sbk][:, 0:256], func=AF.Identity)

                def stepB():
                    for d in range(2):
                        lb = lbs[d]
                        lpn = f"ps{lb}"
                        for r in range(4):
                            col = d * 4 + r
                            lp = ps[lb][:, r * 128:(r + 1) * 128]
                            S.op("pe", "matmul", reads=["dahl", "cmb"], writes=[lpn], out=lp,
                                 lhsT=dahl[:, t, col:col + 1].to_broadcast([128, 128]), rhs=cmb[:, 1 + d, :],
                                 start=True, stop=False)
                            S.op("pe", "matmul", reads=["dahl", "cmb"], writes=[lpn], out=lp,
                                 lhsT=dahl[:, t, 8 + col:9 + col].to_broadcast([128, 128]), rhs=cmb[:, 1 + d, :],
                                 start=False, stop=False)
                            S.op("pe", "matmul", reads=["cmb"], writes=[lpn], out=lp, lhsT=ident_b, rhs=cmb[:, 3 + d, :],
                                 start=False, stop=True)

                def stepC(d):
                    def f():
                        lb = lbs[d]
                        lpn = f"ps{lb}"
                        Ls, lsn = Lsumb[par][d], f"Lsum{par}{d}"
                        G4, gln = GL4b[par][d], f"GL{par}{d}"
                        S.op("dve", "tensor_tensor", reads=[lpn, "nacs"], writes=[lsn], out=v3(Ls, 128),
                             in0=v3(ps[lb][:, :], 128), in1=bc4(nacsA[:, t, d * 4:d * 4 + 4], 128), op=ALU.add)
                        S.op("act", "activation", reads=[lsn], writes=[lsn], out=Ls, in_=Ls, func=AF.Exp)
                        S.op("dve", "tensor_tensor", reads=[lsn, gtn], writes=[gln], out=v3(G4, 128),
                             in0=v3(Ls, 128), in1=GTp.unsqueeze(1).to_broadcast([128, 4, 128]), op=ALU.mult)
                        for r in range(4):
                            S.op("pe", "matmul", reads=[gln, f"xsdt{par}{d}"], writes=[f"ps{yb}"],
                                 out=ps[yb][:, r * 64:(r + 1) * 64], lhsT=G4[:, r * 128:(r + 1) * 128],
                                 rhs=xd[d][:, r * 64:(r + 1) * 64],
                                 start=(d == 0 and r == 0), stop=(d == 1 and r == 3), skip_group_check=True)
                    return f

                def stepE():
                    S.op("act", "activation", reads=[f"ps{yb}"], writes=[f"ydiag{t}"], out=ydiag[:, t, :],
                         in_=ps[yb][:, 0:256], func=AF.Identity)

                return [stepA, stepB, stepC(0), stepC(1), stepE]

            def diag_pair(ta, tb):
                sa, sb_ = diag_steps(ta, 0), diag_steps(tb, 1)
                for fa, fb in zip(sa, sb_):
                    fa()
                    fb()

            def chain_seq(si, tiles, ext):
                orders = [tiles, tiles[::-1]]
                have = [False, False]
                if ext:
                    for d in range(2):
                        S.op("dve", "tensor_copy", reads=["h0g"], writes=[f"Hst{d}"], out=Hst[d], in_=h0g[:, d, :])
                    for step in range(4):
                        for d in range(2):
                            j = step if d == 0 else 3 - step
                            Hs, hn = Hst[d], f"Hst{d}"
                            mcol = maskbc[:, d * 4 + j:d * 4 + j + 1]
                            S.op("dve", "tensor_tensor", reads=[hn, "aj"], writes=[hn], out=v3(Hs), in0=v3(Hs),
                                 in1=bc4(aj[:, j, d, :]), op=ALU.mult)
                            S.op("dve", "scalar_tensor_tensor", reads=["gat", "maskbc", hn], writes=[hn], out=Hs,
                                 in0=gat[:, j, d, 0:256], scalar=mcol, in1=Hs, op0=ALU.mult, op1=ALU.add)
                    have = [True, True]
                for n in range(len(tiles)):
                    for d in range(2):
                        t = orders[d][n]
                        th = t // 4
                        Hs, hn = Hst[d], f"Hst{d}"
                        if have[d]:
                            S.op("act", "activation", reads=[hn], writes=[f"Hbf{d}"], out=Hbf[d], in_=Hs, func=AF.Identity)
                            sbk = 5 if d == 0 else 1
                            S.op("pe", "matmul", reads=[f"BCT1_{th}", f"Hbf{d}"], writes=[f"ps{sbk}"],
                                 out=ps[sbk][:, 0:256], lhsT=BCT[:, 1, t * 128:(t + 1) * 128],
                                 rhs=Hbf[d], start=True, stop=True)
                            S.op("dve", "tensor_tensor", reads=[f"ps{sbk}", "ea"], writes=[f"ytmp{d}"], out=v3(ytmp[d]),
                                 in0=v3(ps[sbk][:, 0:256]), in1=bc4(eaA[:, t, d * 4:d * 4 + 4]),
                                 op=ALU.mult)
                            S.op("dve", "tensor_tensor", reads=[f"ytmp{d}", f"ydiag{t}"], writes=[f"ydiag{t}"],
                                 out=ydiag[:, t, :], in0=ydiag[:, t, :], in1=ytmp[d], op=ALU.add)
                            if n < len(tiles) - 1 or not ext:
                                S.op("dve", "tensor_tensor", reads=[hn, "cdec"], writes=[hn], out=v3(Hs), in0=v3(Hs),
                                     in1=bc4(cdecA[:, t, d * 4:d * 4 + 4]), op=ALU.mult)
                                S.op("dve", "tensor_tensor", reads=[hn, f"S{t}_{d}"], writes=[hn], out=Hs, in0=Hs,
                                     in1=S_sb[:, t, d, :], op=ALU.add)
                        else:
                            S.op("dve", "tensor_copy", reads=[f"S{t}_{d}"], writes=[hn], out=Hs, in_=S_sb[:, t, d, :])
                            have[d] = True
                if not ext:
                    for d in range(2):
                        S.dma("sp", st_out[si, d, :, g * 256:(g + 1) * 256], Hst[d], reads=[f"Hst{d}"],
                              writes=[f"st{si}{d}{g}"])
                nt = len(tiles)
                t0 = tiles[0]
                W_ = nt * 256
                yd = ydiag[:, t0:t0 + nt, :]
                ydn = [f"ydiag{t}" for t in tiles]
                ys_, yz_ = ysum[:, 0:W_], yz[:, 0:W_]
                r4 = lambda ap: ap.rearrange("p (t r e) -> p t r e", t=nt, e=64)
                S.op("dve", "tensor_tensor", reads=[f"xs_tm{t}" for t in tiles] + ["dbc"], writes=["ysum"],
                     out=r4(ys_), in0=xs_tm[:, t0:t0 + nt, :].rearrange("p t (r e) -> p t r e", e=64),
                     in1=dbc[:, g * 4:g * 4 + 4].unsqueeze(1).unsqueeze(3).to_broadcast([128, nt, 4, 64]), op=ALU.mult)
                S.op("dve", "tensor_tensor", reads=["ysum"] + ydn, writes=["ysum"], out=ys_, in0=ys_,
                     in1=yd.rearrange("p t n -> p (t n)"), op=ALU.add)
                S.op("dve", "tensor_tensor", reads=["ysum"] + [f"siluz{t}" for t in tiles], writes=["yz"], out=yz_, in0=ys_,
                     in1=siluz[:, t0:t0 + nt, :].rearrange("p t n -> p (t n)"), op=ALU.mult)
                for n in range(nt):
                    S.op("act", "activation", reads=["yz"], writes=["sq", "stat_ss"], out=sq[:, 0:256],
                         in_=yz[:, n * 256:(n + 1) * 256], func=AF.Square, accum_out=stat[:, 8 + n:9 + n])
                S.op("act", "activation", reads=["stat_ss", "epsc"], writes=["stat_rs"], out=stat[:, 12:12 + nt],
                     in_=stat[:, 8:8 + nt], func=AF.Sqrt, bias=epsc[:, 1:2], scale=1.0 / 256.0)
                S.op("dve", "reciprocal", reads=["stat_rs"], writes=["stat_rr"], out=stat[:, 16:16 + nt], in_=stat[:, 12:12 + nt])
                y3 = lambda ap: ap.rearrange("p (t n) -> p t n", t=nt)
                S.op("dve", "tensor_tensor", reads=["yz", "stat_rr"], writes=["yz"], out=y3(yz_), in0=y3(yz_),
                     in1=stat[:, 16:16 + nt].unsqueeze(2).to_broadcast([128, nt, 256]), op=ALU.mult)
                S.op("dve", "tensor_tensor", reads=["yz", "nwg"], writes=["ynb"], out=y3(ynb[:, 0:W_]), in0=y3(yz_),
                     in1=nwg.unsqueeze(1).to_broadcast([128, nt, 256]), op=ALU.mult)
                pb = psbf(6)
                for n in range(nt):
                    for cc in range(2):
                        S.op("pe", "transpose", reads=["ynb", "cmb"], writes=["ps6"],
                             out=pb[:, (n * 2 + cc) * 128:(n * 2 + cc + 1) * 128],
                             in_=ynb[:, n * 256 + cc * 128:n * 256 + (cc + 1) * 128], identity=ident_b)
                for cc in range(2):
                    S.op("act", "activation", reads=["ps6"], writes=[f"ynT{g}_{t}" for t in tiles],
                         out=ynT[:, 2 * g + cc, t0 * 128:(t0 + nt) * 128].rearrange("p (t n) -> p t n", t=nt),
                         in_=pb[:, 0:W_].rearrange("p (t c n) -> p t c n", t=nt, c=2)[:, :, cc, :], func=AF.Identity)

            stiles = SEQS[2][0]
            diag_pair(4, 5)
            diag_pair(6, 7)
            if KSUB == 3:
                for t_ in range(NT):
                    S.dma("sp", y_out[t_ * 128:(t_ + 1) * 128, :], x[:, t_, :], reads=[f"x{t_}"], writes=[f"yout{t_}"])
                S.emit(es)
                return nc
            for d in range(2):
                order = stiles if d == 0 else stiles[::-1]
                Z = Zst[:, d, 0:256]
                LD = Zst[:, d, 256:260]
                for n, t in enumerate(order):
                    if n == 0:
                        S.op("dve", "tensor_copy", reads=[f"S{t}_{d}"], writes=["Zst"], out=Z, in_=S_sb[:, t, d, :])
                        S.op("dve", "tensor_copy", reads=["smA"], writes=["Zst"], out=LD,
                             in_=smA[:, t, 8 + d * 4:12 + d * 4])
                    else:
                        S.op("dve", "tensor_tensor", reads=["Zst", "cdec"], writes=["Zst"], out=v3(Z), in0=v3(Z),
                             in1=bc4(cdecA[:, t, d * 4:d * 4 + 4]), op=ALU.mult)
                        S.op("dve", "tensor_tensor", reads=["Zst", f"S{t}_{d}"], writes=["Zst"], out=Z, in0=Z,
                             in1=S_sb[:, t, d, :], op=ALU.add)
                        S.op("dve", "tensor_tensor", reads=["Zst", "smA"], writes=["Zst"], out=LD, in0=LD,
                             in1=smA[:, t, 8 + d * 4:12 + d * 4], op=ALU.add)
            S.dma("sp", ag2_src[g].rearrange("(d n) c -> n d c", d=2), Zst, reads=["Zst"], writes=[f"ag2s{g}"])
            S.cc(reads=[f"ag2s{g}"], writes=[f"ag2d{g}"], kind="AllGather", op=ALU.bypass, replica_groups=GROUPS4,
                 ins=[ag2_src[g].opt()], outs=[ag2_dst[g].opt()])
            S.dma("sp", gat, ag2_dst[g].rearrange("(j d n) c -> n j d c", j=4, d=2), reads=[f"ag2d{g}"], writes=["gat"])
            S.dma("sp", h0g, h0T[:, :, g * 256:(g + 1) * 256].rearrange("d n c -> n d c"), writes=["h0g"])
            for si in range(2):
                diag_pair(*SEQS[si][0])
                chain_seq(si, SEQS[si][0], False)
            if KSUB == 5:
                for t_ in range(NT):
                    S.dma("sp", y_out[t_ * 128:(t_ + 1) * 128, :], x[:, t_, :], reads=[f"x{t_}"], writes=[f"yout{t_}"])
                S.emit(es)
                return nc
            mjd = lambda o: maskbc[:, o:o + 8].rearrange("p (d j) -> p j d", d=2).unsqueeze(3).to_broadcast([128, 4, 2, 4])
            S.op("act", "activation", reads=["gat"], writes=["aj"], out=aj, in_=gat[:, :, :, 256:260], func=AF.Exp)
            S.op("dve", "tensor_tensor", reads=["aj", "maskbc"], writes=["aj"], out=aj, in0=aj, in1=mjd(0), op=ALU.mult)
            S.op("dve", "tensor_tensor", reads=["aj", "maskbc"], writes=["aj"], out=aj, in0=aj, in1=mjd(8), op=ALU.add)
            chain_seq(2, stiles, True)

        S.fence()
        apos[0] = mark_grp
        wob = carve(16 * 2048, BF16, "p (c n) -> p c n", c=16)
        S.dma("pool", wob[:, 8:16, :], w_out[0, 1024:2048, :].rearrange("(c p) n -> p c n", p=128), writes=["wob_hi"])
        load_lnbc(1, 1)
        for t in range(NT):
            b0 = (t % 2) * 2
            for dh in range(2):
                for c16 in range(16):
                    S.op("pe", "matmul", reads=[f"ynT{c16 // 2}_{t}", "wob_lo" if c16 < 8 else "wob_hi"], writes=[f"ps{b0 + dh}"],
                         out=ps[b0 + dh][:, :], lhsT=ynT[:, c16, t * 128:(t + 1) * 128],
                         rhs=wob[:, c16, dh * 512:(dh + 1) * 512], start=(c16 == 0), stop=(c16 == 15))
            ln_epilogue(t, [f"ps{b0}", f"ps{b0 + 1}"], [(ps[b0][:, :], 0, 512), (ps[b0 + 1][:, :], 512, 1024)],
                        0, True, 4 + (t % 2))

        if stop == 3:
            for t_ in range(NT):
                S.dma("sp", y_out[t_ * 128:(t_ + 1) * 128, :], x[:, t_, :], reads=[f"x{t_}"], writes=[f"yout{t_}"])
            S.emit(es)
            return nc
        def pre1():
            for hf in range(2):
                mod_halfblock(1, 5, hf)

        ffn(1, None, None, True, pre_hook=pre1)

        S.emit(es)
    return nc


def _win(n, k):
    left = k // 2
    right = k - 1 - left
    pos = np.arange(n)
    lo = np.clip(pos - left, 0, n)
    hi = np.clip(pos + right + 1, 0, n)
    return lo, hi


def _host_consts():
    ar = np.arange(128)
    ident = np.eye(128, dtype=np.float32)
    tri_f = (ar[:, None] <= ar[None, :]).astype(np.float32)
    tri_b = (ar[:, None] >= ar[None, :]).astype(np.float32)
    mneg_f = np.where(ar[:, None] <= ar[None, :], 0.0, NEG).astype(np.float32)
    mneg_b = np.where(ar[:, None] >= ar[None, :], 0.0, NEG).astype(np.float32)
    ones = np.ones((128, 128), np.float32)
    cm = np.stack([ident, tri_f, tri_b, mneg_f, mneg_b, ones]).astype(np.float32)
    p1 = np.zeros((4, 256, 256), np.float32)
    inv1 = np.zeros((4, 256), np.float32)
    for g, k in enumerate(POOL_K):
        lo, hi = _win(256, k)
        for o in range(256):
            p1[g, lo[o]:hi[o], o] = 1.0
            p1[g, o, o] -= float(hi[o] - lo[o])
            inv1[g, o] = 1.0 / float(hi[o] - lo[o])
    p1 = p1.reshape(4, 2, 128, 256).reshape(8, 128, 256)
    return cm, p1, inv1


def _host_pool2d(q):
    r0 = q * 8 - 8
    blocks = np.zeros((NP2, 128, 512), np.float32)
    inv2 = np.zeros((4, 512), np.float32)
    for g, k in enumerate(POOL_K):
        rlo, rhi = _win(32, k)
        clo, chi = _win(64, k)
        P = np.zeros((1536, 512), np.float32)
        for rr in range(8):
            r = q * 8 + rr
            for c in range(64):
                o = rr * 64 + c
                cnt = float((rhi[r] - rlo[r]) * (chi[c] - clo[c]))
                for r2 in range(rlo[r], rhi[r]):
                    e0 = (r2 - r0) * 64
                    P[e0 + clo[c]:e0 + chi[c], o] = 1.0
                P[(r - r0) * 64 + c, o] -= cnt
                inv2[g, o] = 1.0 / cnt
        lo, hi = P2_TILES[g]
        nz = np.abs(P).reshape(12, 128, 512).sum(axis=(1, 2))
        for tin in range(12):
            if tin < lo or tin >= hi:
                assert nz[tin] == 0.0
        blocks[P2_BLK0[g]:P2_BLK0[g] + hi - lo] = P.reshape(12, 128, 512)[lo:hi]
    return blocks, inv2


_NC_CACHE = {}


def kernel(x_prompt, x_sample, c, state_ssm, c_ctx, w_ada, b_ada, w_pool, pool_scale, ssd_w_in, ssd_conv_w,
           ssd_conv_b, ssd_dt_bias, ssd_a_log, ssd_d, ssd_norm_w, ssd_w_out, ffn_w_in, ffn_w_out,
           ln1_g, ln1_b, ln2_g, ln2_b):
    f = lambda a: np.ascontiguousarray(np.asarray(a, dtype=np.float32))
    x_prompt, x_sample, c, state_ssm, c_ctx = map(f, (x_prompt, x_sample, c, state_ssm, c_ctx))
    shared = dict(w_ada=f(w_ada), b_ada=f(b_ada), w_pool=f(w_pool), pool_scale=f(pool_scale), ssd_w_in=f(ssd_w_in),
                  ssd_dt_bias=f(ssd_dt_bias),
                  ssd_a_log=f(ssd_a_log), ssd_d=f(ssd_d), ssd_norm_w=f(ssd_norm_w), ssd_w_out=f(ssd_w_out),
                  ffn_w_in=f(ffn_w_in), ffn_w_out=f(ffn_w_out), ln1_g=f(ln1_g), ln1_b=f(ln1_b), ln2_g=f(ln2_g),
                  ln2_b=f(ln2_b))
    lnl = [shared[k][l] for l in range(2) for k in ("ln1_g", "ln1_b", "ln2_g", "ln2_b")]
    shared["lncT"] = np.ascontiguousarray(np.stack(lnl).reshape(8, 8, 128).transpose(2, 0, 1))
    cw5 = np.concatenate([f(ssd_conv_w)[0], f(ssd_conv_b)], axis=0)
    shared["cwl"] = np.ascontiguousarray(cw5.reshape(5, 32, 128).transpose(2, 1, 0))
    cm, p1, inv1 = _host_consts()
    pool2 = [_host_pool2d(q) for q in range(4)]
    in_maps = []
    for core in range(8):
        b, q = core // 4, core % 4
        xin = np.concatenate([x_prompt[2 * core].reshape(256, D), x_prompt[2 * core + 1].reshape(256, D),
                              x_sample[b, q * 512:(q + 1) * 512]], axis=0)
        xext = np.zeros((1536, D), np.float32)
        lo = q * 512 - 512
        hi = q * 512 + 1024
        slo, shi = max(lo, 0), min(hi, 2048)
        xext[slo - lo:shi - lo] = x_sample[b, slo:shi]
        cvec = np.ascontiguousarray(np.stack([c_ctx, c[b]]).astype(np.float32).reshape(2, 8, 128).transpose(2, 1, 0))
        h0T = np.ascontiguousarray(state_ssm[b, 0].transpose(0, 3, 1, 2).reshape(2, 128, DI))
        blocks, inv2 = pool2[q]
        invc = np.concatenate([inv1, inv2], axis=1).reshape(1, 4 * 768).astype(np.float32)
        mf = np.array([1.0 if j < q else 0.0 for j in range(4)], np.float32)
        mb = np.array([1.0 if j > q else 0.0 for j in range(4)], np.float32)
        valid = np.array([1.0 if q > 0 else 0.0, 1.0 if q > 0 else 0.0, 1.0 if q < 3 else 0.0], np.float32)
        cmask = np.concatenate([mf, mb, 1 - mf, 1 - mb, valid]).reshape(1, 19).astype(np.float32)
        sel = np.zeros((12, 3), np.float32)
        if q > 0:
            sel[3 * (q - 1) + 1, 0] = 1.0
            sel[3 * (q - 1) + 2, 1] = 1.0
        if q < 3:
            sel[3 * (q + 1) + 0, 2] = 1.0
        m = dict(shared)
        m.update(xin=np.ascontiguousarray(xin), xext=xext, cvecT=cvec, h0T=h0T, cmats=cm, p1=p1, p2=blocks, invc=invc,
                 cmask=cmask, sel=sel)
        in_maps.append(m)
    import os
    stop = int(os.environ.get("KSTOP", "99"))
    if stop not in _NC_CACHE:
        _NC_CACHE[stop] = build_program(stop)
    nc = _NC_CACHE[stop]
    res = run_bass_kernel_spmd(nc, in_maps, core_ids=list(range(8)))
    y_prompt = np.zeros((16, 256, D), np.float32)
    y_sample = np.zeros((2, 2048, D), np.float32)
    new_state = np.zeros((16, 1, 2, 32, 64, 128), np.float32)
    for core in range(8):
        r = res.results[core]
        b, q = core // 4, core % 4
        yo = np.asarray(r["y_out"])
        y_prompt[2 * core] = yo[0:256]
        y_prompt[2 * core + 1] = yo[256:512]
        y_sample[b, q * 512:(q + 1) * 512] = yo[512:1024]
        st = np.asarray(r["st_out"]).reshape(2, 2, 128, 32, 64)
        new_state[2 * core:2 * core + 2, 0] = st.transpose(0, 1, 3, 4, 2)
    return (y_prompt, y_sample, new_state)
```

```python
import numpy as np
from contextlib import ExitStack
import concourse.bass as bass
import concourse.mybir as mybir
from concourse.bass_utils import run_bass_kernel_spmd

F32 = mybir.dt.float32
BF16 = mybir.dt.bfloat16
AF = mybir.ActivationFunctionType
ALU = mybir.AluOpType

D = 1024
NT = 8
FF = 2816
NFC = FF // 128
DI = 2048
CONV_CH = 4096
NCOLS_IN = 6208
ALPHA = float(4 ** 0.25)
LN_EPS = 1e-5
RMS_EPS = 1e-5
POOL_K = (2, 4, 8, 16)
P2_TILES = {0: (3, 8), 1: (3, 9), 2: (2, 10), 3: (0, 12)}
P2_BLK0 = {0: 0, 1: 5, 2: 11, 3: 19}
NP2 = 31
GROUPS4 = [[0, 1, 2, 3], [4, 5, 6, 7]]
SEQS = [([0, 1], False), ([2, 3], False), ([4, 5, 6, 7], True)]
NEG = -30000.0


class Sched:
    ENG = ("pe", "act", "dve", "pool", "sp")

    def __init__(self, nc):
        self.nc = nc
        self.ops = []
        self.per_eng = {e: [] for e in self.ENG}
        self.last_w = {}
        self.readers = {}

    def fence(self):
        self._add("dve", "c", "memset", dict(ap=self.fence_tile, constant=0.0), (), ("__fence__",), fence=True)

    def _add(self, eng, kind, name, kwargs, reads, writes, fence=False):
        if not fence:
            def _isps(n):
                return len(n) >= 3 and n[:2] == "ps" and n[2].isdigit()
            psr = tuple(r[:3] for r in reads if _isps(r))
            reads = tuple(r for r in reads if not _isps(r))
            writes = tuple(dict.fromkeys(tuple((w[:3] if _isps(w) else w) for w in writes) + psr))
            reads = tuple(reads) + ("__fence__",)
        oid = len(self.ops)
        deps = set()
        if fence:
            for e in self.ENG:
                cs = [o for o in self.per_eng[e] if o["kind"] == "c"]
                if cs:
                    deps.add(cs[-1]["id"])
                ds = [o for o in self.per_eng[e] if o["kind"] != "c"]
                for o in ds[-16:]:
                    deps.add(o["id"])
            reads, writes_ = (), writes
        for r in reads:
            if r in self.last_w:
                deps.add(self.last_w[r])
        for w in writes:
            if w in self.last_w:
                deps.add(self.last_w[w])
            if not fence:
                for rd in self.readers.get(w, ()):
                    deps.add(rd)
        deps.discard(oid)
        op = dict(id=oid, eng=eng, kind=kind, name=name, kw=kwargs, deps=deps, needed=False)
        self.ops.append(op)
        self.per_eng[eng].append(op)
        for r in reads:
            self.readers.setdefault(r, []).append(oid)
        for w in writes:
            self.last_w[w] = oid
            self.readers[w] = []
        return oid

    def op(self, eng, name, reads=(), writes=(), **kw):
        return self._add(eng, "c", name, kw, tuple(reads), tuple(writes))

    def dma(self, eng, out, in_, reads=(), writes=()):
        return self._add(eng, "d", "dma_start", dict(out=out, in_=in_), tuple(reads), tuple(writes))

    def cc(self, reads, writes, **kw):
        return self._add("pool", "cc", "collective_compute", kw, tuple(reads), tuple(writes))

    def emit(self, es):
        nc = self.nc
        ops = self.ops
        for op in ops:
            for d in op["deps"]:
                p = ops[d]
                if p["kind"] == "c":
                    if p["eng"] == "pe" and op["eng"] == "pe" and op["kind"] == "c":
                        continue
                    p["needed"] = True
        M = 16
        csem = {e: es.enter_context(nc.semaphore("c_" + e)) for e in ("pe", "act", "dve", "pool")}
        dsem = {e: [es.enter_context(nc.semaphore(f"d_{e}{i}")) for i in range(M)] for e in ("pool", "sp")}
        ccsem = [es.enter_context(nc.semaphore(f"ccs{i}")) for i in range(M)]
        ccnt = {e: 0 for e in csem}
        dcnt = {e: 0 for e in dsem}
        ncc = 0
        for e in self.ENG:
            for op in self.per_eng[e]:
                if op["kind"] == "c":
                    if op["needed"]:
                        ccnt[e] += 1
                        op["sem"] = ("c", e, ccnt[e])
                    else:
                        op["sem"] = None
                elif op["kind"] == "d":
                    i = dcnt[e]
                    dcnt[e] += 1
                    op["sem"] = ("d", (e, i % M), 16 * (i // M + 1))
                else:
                    i = ncc
                    ncc += 1
                    op["sem"] = ("cc", i % M, i // M + 1)
        import os as _os3
        if _os3.environ.get("KDEBUG"):
            print("SCHED counts", ccnt, dcnt, ncc, {e: len(v) for e, v in self.per_eng.items()})
        handles = {"pe": nc.tensor, "act": nc.scalar, "dve": nc.vector, "pool": nc.gpsimd, "sp": nc.sync}
        block = es.enter_context(nc.Block())

        def semof(key):
            if key[0] == "c":
                return csem[key[1]]
            if key[0] == "d":
                return dsem[key[1][0]][key[1][1]]
            return ccsem[key[1]]

        def run(e):
            h = handles[e]
            waited = {}
            for op in self.per_eng[e]:
                need = {}
                for d in op["deps"]:
                    p = ops[d]
                    if p["kind"] == "c" and p["eng"] == "pe" and e == "pe" and op["kind"] == "c":
                        continue
                    s = p["sem"]
                    if s is None:
                        continue
                    key = (s[0], s[1])
                    need[key] = max(need.get(key, 0), s[2])
                s = op["sem"]
                if s is not None and s[0] == "d" and s[2] > 16:
                    key = (s[0], s[1])
                    need[key] = max(need.get(key, 0), s[2] - 16)
                if s is not None and s[0] == "cc" and s[2] > 1:
                    key = (s[0], s[1])
                    need[key] = max(need.get(key, 0), s[2] - 1)
                for key, v in need.items():
                    if waited.get(key, 0) < v:
                        h.wait_ge(semof(key), v)
                        waited[key] = v
                ins = getattr(h, op["name"])(**op["kw"])
                if s is not None:
                    if s[0] == "c":
                        ins.then_inc(csem[s[1]], 1)
                    elif s[0] == "d":
                        ins.then_inc(semof((s[0], s[1])), 16)
                    else:
                        ins.then_inc(ccsem[s[1]])
            if e == "sp":
                for q in dsem:
                    for i in range(min(M, dcnt[q])):
                        n_i = (dcnt[q] - 1 - i) // M + 1
                        h.wait_ge(dsem[q][i], 16 * n_i)
                for i in range(min(M, ncc)):
                    h.wait_ge(ccsem[i], (ncc - 1 - i) // M + 1)

        @block.tensor
        def _(t):
            run("pe")

        @block.scalar
        def _(t):
            run("act")

        @block.vector
        def _(t):
            run("dve")

        @block.gpsimd
        def _(t):
            run("pool")

        @block.sync
        def _(t):
            run("sp")


class _Stop(Exception):
    pass


def build_program(stop=99):
    nc = bass.Bass("TRN2", target_bir_lowering=False)
    S = Sched(nc)

    def din(name, shape):
        return nc.dram_tensor(name, list(shape), F32, kind="ExternalInput").ap()

    xin = din("xin", [1024, D])
    xext = din("xext", [1536, D])
    cvec = din("cvecT", [128, 8, 2])
    h0T = din("h0T", [2, 128, DI])
    w_ada = din("w_ada", [2, D, 6 * D])
    b_ada = din("b_ada", [2, 6 * D])
    w_pool = din("w_pool", [1, 4, 256, 256])
    pool_scale = din("pool_scale", [1, D])
    w_in = din("ssd_w_in", [1, D, NCOLS_IN])
    dt_bias = din("ssd_dt_bias", [1, 2, 32])
    a_log = din("ssd_a_log", [1, 2, 32])
    ssd_d = din("ssd_d", [1, 32])
    norm_w = din("ssd_norm_w", [1, DI])
    w_out = din("ssd_w_out", [1, DI, D])
    ffn_w_in = din("ffn_w_in", [2, D, 2 * FF])
    ffn_w_out = din("ffn_w_out", [2, FF, D])
    ln_in = {k: din(k, [2, D]) for k in ("ln1_g", "ln1_b", "ln2_g", "ln2_b")}
    lncT = din("lncT", [128, 8, 8])
    cwl = din("cwl", [128, 32, 5])
    cm = din("cmats", [6, 128, 128])
    p1 = din("p1", [8, 128, 256])
    p2 = din("p2", [NP2, 128, 512])
    invc = din("invc", [1, 4 * 768])
    cmask = din("cmask", [1, 19])
    sel = din("sel", [12, 3])

    y_out = nc.dram_tensor("y_out", [1024, D], F32, kind="ExternalOutput").ap()
    st_out = nc.dram_tensor("st_out", [2, 2, 128, DI], F32, kind="ExternalOutput").ap()
    ag1_src = nc.dram_tensor("ag1_src", [3, D], F32).ap()
    ag1_dst = nc.dram_tensor("ag1_dst", [12, D], F32).ap()
    ag2_src = [nc.dram_tensor(f"ag2_src{g}", [256, 260], F32).ap() for g in range(8)]
    ag2_dst = [nc.dram_tensor(f"ag2_dst{g}", [1024, 260], F32).ap() for g in range(8)]

    with ExitStack() as es:
        def sb(name, shape, dt=F32):
            return es.enter_context(nc.sbuf_tensor(name, list(shape), dt))

        x = sb("x", [128, NT, D])
        hT = sb("hT", [128, 8, 1024], BF16)
        gate_bc = sb("gate_bc", [128, 2, 2, D])
        lnbc = sb("lnbc", [128, 2, D])
        cmf = sb("cmf", [128, 6, 128])
        cmb = sb("cmb", [128, 6, 128], BF16)
        modc = sb("modc", [128, 2, 4, 8, 2])
        lnc = sb("lnc", [128, 8, 8])
        AB = sb("AB", [128, 2, 2, 8])
        siluc = sb("siluc", [128, 8, 2], BF16)
        cT = sb("cT", [128, 8, 2])
        ones_row = sb("ones_row", [1, 128], BF16)
        epsc = sb("epsc", [128, 2])
        maskbc = sb("maskbc", [128, 19])
        stat = sb("stat", [128, 32])
        bnst = sb("bnst", [128, 2, 6])
        pre = sb("pre", [128, D])
        xnbf = sb("xnbf", [128, D], BF16)
        fence_t = sb("fence_t", [128, 2])
        S.fence_tile = fence_t[:]
        wab = [None, None]
        bab = [None, None]
        nwab = [1]
        ARENA = 122 * 1024 // 4
        arena = sb("arena", [128, ARENA])
        apos = [0]

        def carve(nbytes, dt, shape_str=None, **kw):
            n4 = (nbytes + 3) // 4
            a = arena[:, apos[0]:apos[0] + n4]
            apos[0] += n4
            assert apos[0] <= ARENA, ("arena overflow", apos[0] * 4)
            if dt == BF16:
                a = a.bitcast(BF16)
            if shape_str:
                a = a.rearrange(shape_str, **kw)
            return a

        ps = [es.enter_context(nc.psum_tensor(f"ps{i}", [128, 512], F32)) for i in range(8)]

        def psbf(i):
            return ps[i][:, :].bitcast(BF16)

        ident_f = cmf[:, 0, :]
        ident_b = cmb[:, 0, :]
        ones_f = cmf[:, 5, :]

        for t in range(NT):
            S.dma("sp", x[:, t, :], xin[t * 128:(t + 1) * 128, :], writes=[f"x{t}"])
        S.dma("sp", cmf[:], cm.rearrange("c p f -> p c f"), writes=["cmf"])
        S.dma("pool", cmb[:], cm.rearrange("c p f -> p c f"), writes=["cmb"])
        S.dma("sp", cT[:], cvec, writes=["cT"])
        S.dma("sp", lnc[:], lncT, writes=["lnc"])
        S.dma("sp", maskbc[:], cmask[0, :].partition_broadcast(128), writes=["maskbc"])
        S.op("dve", "memset", writes=["ones_row"], ap=ones_row[:], constant=1.0)
        S.op("dve", "memset", writes=["epsc"], ap=epsc[:, 0:1], constant=LN_EPS)
        S.op("dve", "memset", writes=["epsc"], ap=epsc[:, 1:2], constant=RMS_EPS)
        S.op("act", "activation", reads=["cT"], writes=["siluc"], out=siluc[:], in_=cT[:], func=AF.Silu)

        wab_i = [0]

        def mod_halfblock(l, nb, hf):
            i = wab_i[0] % nwab[0]
            wab_i[0] += 1
            c0 = nb * 1024 + hf * 512
            S.dma("pool", wab[i][:], w_ada[l, :, c0:c0 + 512].rearrange("(kc p) n -> p kc n", p=128),
                  writes=[f"wab{i}"])
            S.dma("pool", bab[i][0:1, :], b_ada[l:l + 1, c0:c0 + 512], writes=[f"bab{i}"])
            if nb in (2, 5):
                which = 0 if nb == 2 else 1
                for c in range(2):
                    for kc in range(8):
                        S.op("pe", "matmul", reads=["siluc", f"wab{i}"], writes=["ps6"],
                             out=ps[6][:, :], lhsT=siluc[:, kc, c:c + 1].to_broadcast([128, 128]),
                             rhs=wab[i][:, kc, :], start=(kc == 0), stop=False)
                    S.op("pe", "matmul", reads=["ones_row", f"bab{i}"], writes=["ps6"],
                         out=ps[6][:, :], lhsT=ones_row[0:1, :], rhs=bab[i][0:1, :], start=False, stop=True)
                    S.op("act", "activation", reads=["ps6"], writes=[f"gate{which}{c}"],
                         out=gate_bc[:, which, c, hf * 512:(hf + 1) * 512], in_=ps[6][:, :], func=AF.Identity)
            else:
                v = {0: 0, 1: 1, 3: 2, 4: 3}[nb]
                for cc in range(4):
                    kcx = hf * 4 + cc
                    o = ps[7][:, (v * 8 + kcx) * 2:(v * 8 + kcx) * 2 + 2]
                    for kc in range(8):
                        S.op("pe", "matmul", reads=["siluc", f"wab{i}"], writes=["ps7"],
                             out=o, lhsT=wab[i][:, kc, cc * 128:(cc + 1) * 128], rhs=siluc[:, kc, :],
                             start=(kc == 0), stop=False)
                    S.op("pe", "matmul", reads=["ones_row", f"bab{i}"], writes=["ps7"],
                         out=o, lhsT=bab[i][0:1, cc * 128:(cc + 1) * 128], rhs=ones_row[0:1, 0:2],
                         start=False, stop=True)

        def mod_finish(l):
            S.op("dve", "tensor_copy", reads=["ps7"], writes=[f"modc{l}"],
                 out=modc[:, l].rearrange("p v k c -> p (v k c)"), in_=ps[7][:, 0:64])

        def make_AB(l_ln, which_ln, l_mod, vsh, vsc):
            g_i = l_ln * 4 + (0 if which_ln == 1 else 2)
            for c in range(2):
                S.op("dve", "scalar_tensor_tensor", reads=[f"modc{l_mod}", "lnc"], writes=["AB"],
                     out=AB[:, 0, c, :], in0=modc[:, l_mod, vsc, :, c], scalar=1.0, in1=lnc[:, g_i, :],
                     op0=ALU.add, op1=ALU.mult)
                S.op("dve", "scalar_tensor_tensor", reads=[f"modc{l_mod}", "lnc"], writes=["AB"],
                     out=AB[:, 1, c, :], in0=modc[:, l_mod, vsc, :, c], scalar=1.0, in1=lnc[:, g_i + 1, :],
                     op0=ALU.add, op1=ALU.mult)
                S.op("dve", "tensor_tensor", reads=["AB", f"modc{l_mod}"], writes=["AB"],
                     out=AB[:, 1, c, :], in0=AB[:, 1, c, :], in1=modc[:, l_mod, vsh, :, c], op=ALU.add)

        def load_lnbc(l, which):
            kg = "ln1_g" if which == 1 else "ln2_g"
            kb = "ln1_b" if which == 1 else "ln2_b"
            S.dma("sp", lnbc[:, 0, :], ln_in[kg][l, :].partition_broadcast(128), writes=["lnbc"])
            S.dma("sp", lnbc[:, 1, :], ln_in[kb][l, :].partition_broadcast(128), writes=["lnbc"])

        def ln_epilogue(t, sub_reads, sub_parts, which_gate, do_T, tp_bank, out_dram=None):
            c = 0 if t < 4 else 1
            gname = f"gate{which_gate}{c}"
            for (src, lo, hi) in sub_parts:
                S.op("dve", "tensor_tensor", reads=list(sub_reads) + [gname], writes=["pre"],
                     out=pre[:, lo:hi], in0=src, in1=gate_bc[:, which_gate, c, lo:hi], op=ALU.mult)
            S.op("dve", "scalar_tensor_tensor", reads=["pre", f"x{t}"], writes=["pre"],
                 out=pre[:], in0=x[:, t, :], scalar=ALPHA, in1=pre[:], op0=ALU.mult, op1=ALU.add)
            for hh in range(2):
                S.op("dve", "bn_stats", reads=["pre"], writes=["bnst"], out=bnst[:, hh, :],
                     in_=pre[:, hh * 512:(hh + 1) * 512])
            S.op("dve", "bn_aggr", reads=["bnst"], writes=["stat"], out=stat[:, 0:2],
                 in_=bnst[:].rearrange("p a b -> p (a b)"))
            S.op("act", "activation", reads=["stat", "epsc"], writes=["stat_sd"], out=stat[:, 2:3],
                 in_=stat[:, 1:2], func=AF.Sqrt, bias=epsc[:, 0:1], scale=1.0)
            S.op("dve", "reciprocal", reads=["stat_sd"], writes=["stat_r"], out=stat[:, 3:4], in_=stat[:, 2:3])
            S.op("dve", "scalar_tensor_tensor", reads=["stat", "stat_r"], writes=["stat_n"],
                 out=stat[:, 4:5], in0=stat[:, 0:1], scalar=-1.0, in1=stat[:, 3:4], op0=ALU.mult, op1=ALU.mult)
            S.op("act", "activation", reads=["pre", "stat_r", "stat_n"], writes=["pre"], out=pre[:],
                 in_=pre[:], func=AF.Identity, bias=stat[:, 4:5], scale=stat[:, 3:4])
            S.op("pool", "tensor_tensor", reads=["pre", "lnbc"], writes=[f"x{t}"], out=x[:, t, :],
                 in0=pre[:], in1=lnbc[:, 0, :], op=ALU.mult)
            S.op("pool", "tensor_tensor", reads=[f"x{t}", "lnbc"], writes=[f"x{t}"], out=x[:, t, :],
                 in0=x[:, t, :], in1=lnbc[:, 1, :], op=ALU.add)
            if out_dram is not None:
                S.dma("sp", out_dram, x[:, t, :], reads=[f"x{t}"], writes=[f"yout{t}"])
            if do_T:
                S.op("act", "activation", reads=["pre"], writes=["xnbf"], out=xnbf[:], in_=pre[:],
                     func=AF.Identity)
                pb = psbf(tp_bank)
                for kc in range(8):
                    S.op("pe", "transpose", reads=["xnbf", "cmb"], writes=[f"ps{tp_bank}"],
                         out=pb[:, kc * 128:(kc + 1) * 128], in_=xnbf[:, kc * 128:(kc + 1) * 128],
                         identity=ident_b)
                for kc in range(8):
                    if kc % 2 == 0:
                        S.op("act", "activation", reads=[f"ps{tp_bank}", "AB"], writes=[f"hT{t}"],
                             out=hT[:, kc, t * 128:(t + 1) * 128], in_=pb[:, kc * 128:(kc + 1) * 128],
                             func=AF.Identity, scale=AB[:, 0, c, kc:kc + 1], bias=AB[:, 1, c, kc:kc + 1])
                    else:
                        S.op("dve", "tensor_scalar", reads=[f"ps{tp_bank}", "AB"], writes=[f"hT{t}"],
                             out=hT[:, kc, t * 128:(t + 1) * 128], in0=pb[:, kc * 128:(kc + 1) * 128],
                             scalar1=AB[:, 0, c, kc:kc + 1], scalar2=AB[:, 1, c, kc:kc + 1],
                             op0=ALU.mult, op1=ALU.add)

        if stop == 0:
            S.op("act", "activation", reads=["cmf"], writes=["x0"], out=x[:, 0, 0:128], in_=cmf[:, 1, :], func=AF.Identity)
            for t_ in range(NT):
                S.dma("sp", y_out[t_ * 128:(t_ + 1) * 128, :], x[:, t_, :], reads=[f"x{t_}"], writes=[f"yout{t_}"])
            S.emit(es)
            return nc
        apos[0] = 0
        for i2 in range(2):
            wab[i2] = carve(8 * 1024, BF16, "p (k n) -> p k n", k=8)
            bab[i2] = carve(1024, BF16)
        nwab[0] = 2
        S.op("dve", "memset", writes=["ps7"], ap=ps[7][:, 0:64], constant=0.0)
        for nb in range(1, 6):
            for hf in range(2):
                mod_halfblock(0, nb, hf)
        mod_finish(0)
        xbf = carve(16 * 2048, BF16, "p (t d) -> p t d", t=16)
        p1b = carve(8 * 512, BF16, "p (b n) -> p b n", b=8)
        p2b = carve(NP2 * 1024, BF16, "p (b n) -> p b n", b=NP2)
        wpb = carve(8 * 512, BF16, "p (b n) -> p b n", b=8)
        mixT = carve(8 * 2048, BF16, "p (c n) -> p c n", c=8)
        invbc = carve(4 * 768 * 4, F32, "p (g n) -> p g n", g=4)
        psbc = carve(4096, F32)
        for t in range(4):
            S.dma("pool", xbf[:, t, :], xin[t * 128:(t + 1) * 128, :], writes=[f"xbf{t}"])
        for t in range(12):
            S.dma("pool", xbf[:, 4 + t, :], xext[t * 128:(t + 1) * 128, :], writes=[f"xbf{4 + t}"])
        S.dma("pool", p1b[:], p1.rearrange("b p n -> p b n"), writes=["p1b"])
        S.dma("pool", p2b[:], p2.rearrange("b p n -> p b n"), writes=["p2b"])
        S.dma("pool", wpb[:], w_pool[0].rearrange("g (c p) n -> p (g c) n", p=128), writes=["wpb"])
        S.dma("sp", invbc[:].rearrange("p g n -> p (g n)"), invc[0, :].partition_broadcast(128), writes=["invbc"])
        S.dma("sp", psbc, pool_scale[0, :].partition_broadcast(128), writes=["psbc"])
        for c in range(2):
            S.op("dve", "tensor_tensor", reads=["gate0%d" % c, "psbc"], writes=["gate0%d" % c],
                 out=gate_bc[:, 0, c, :], in0=gate_bc[:, 0, c, :], in1=psbc, op=ALU.mult)
        sc1p = sb("sc1p", [128, 8, 2])
        S.op("dve", "tensor_scalar", reads=["modc0"], writes=["sc1p"], out=sc1p[:], in0=modc[:, 0, 1, :, :],
             scalar1=1.0, scalar2=None, op0=ALU.add)
        make_AB(0, 1, 0, 2, 3)
        load_lnbc(0, 1)
        pbank = [0]
        for si, (tiles, ext) in enumerate(SEQS):
            c = 1 if ext else 0
            nout = 512 if ext else 256
            o0 = tiles[0] * 128
            for g in range(4):
                if ext:
                    lo, hi = P2_TILES[g]
                    srcs = [(4 + tin, p2b[:, P2_BLK0[g] + tin - lo, :], "p2b") for tin in range(lo, hi)]
                    inv = invbc[:, g, 256:768]
                else:
                    srcs = [(tiles[0] + j, p1b[:, g * 2 + j, :], "p1b") for j in range(2)]
                    inv = invbc[:, g, 0:256]
                for cc in range(2):
                    ch = g * 2 + cc
                    bk = pbank[0] % 2
                    pbank[0] += 1
                    for n, (xt, blk, bn) in enumerate(srcs):
                        S.op("pe", "matmul", reads=[f"xbf{xt}", bn], writes=[f"ps{bk}"],
                             out=ps[bk][:, 0:nout], lhsT=xbf[:, xt, ch * 128:(ch + 1) * 128], rhs=blk,
                             start=(n == 0), stop=(n == len(srcs) - 1))
                    S.op("dve", "scalar_tensor_tensor", reads=[f"ps{bk}", "sc1p", "invbc"],
                         writes=[f"mixT{ch}_{si}"], out=mixT[:, ch, o0:o0 + nout], in0=ps[bk][:, 0:nout],
                         scalar=sc1p[:, ch, c:c + 1], in1=inv, op0=ALU.mult, op1=ALU.mult)
        for t in range(NT):
            si = 0 if t < 2 else (1 if t < 4 else 2)
            b0 = 2 + 2 * (t % 2)
            for g in range(4):
                for cc in range(2):
                    ch = g * 2 + cc
                    bk = b0 + (g // 2)
                    S.op("pe", "matmul", reads=[f"mixT{ch}_{si}", "wpb"], writes=[f"ps{bk}"],
                         out=ps[bk][:, (g % 2) * 256:(g % 2) * 256 + 256], lhsT=mixT[:, ch, t * 128:(t + 1) * 128],
                         rhs=wpb[:, ch, :], start=(cc == 0), stop=(cc == 1))
            ln_epilogue(t, [f"ps{b0}", f"ps{b0 + 1}"], [(ps[b0][:, :], 0, 512), (ps[b0 + 1][:, :], 512, 1024)],
                        0, True, 6)

        if stop == 1:
            for t_ in range(NT):
                S.dma("sp", y_out[t_ * 128:(t_ + 1) * 128, :], x[:, t_, :], reads=[f"x{t_}"], writes=[f"yout{t_}"])
            S.emit(es)
            return nc
        def ffn(l, mid_hook, post_AB, final, pre_hook=None):
            S.fence()
            apos[0] = 0
            hid = carve(NFC * 2048, BF16, "p (f n) -> p f n", f=NFC)
            wo = carve(2 * NFC * 1024, BF16, "p (h f n) -> p h f n", h=2, f=NFC)
            wi = [carve(8 * 512, BF16, "p (k n) -> p k n", k=8) for _ in range(3)]
            sg = [carve(2048, F32) for _ in range(2)]
            for i2 in range(2):
                wab[i2] = carve(8 * 1024, BF16, "p (k n) -> p k n", k=8)
                bab[i2] = carve(1024, BF16)
            nwab[0] = 2
            if pre_hook is not None:
                pre_hook()
            load_lnbc(l, 2)
            for fc in range(NFC):
                i = fc % 3
                S.dma("pool", wi[i][:, :, 0:128],
                      ffn_w_in[l, :, fc * 128:(fc + 1) * 128].rearrange("(kc p) n -> p kc n", p=128),
                      writes=[f"wi{i}"])
                S.dma("pool", wi[i][:, :, 128:256],
                      ffn_w_in[l, :, FF + fc * 128:FF + (fc + 1) * 128].rearrange("(kc p) n -> p kc n", p=128),
                      writes=[f"wi{i}"])
                S.dma("pool", wo[:, :, fc, :],
                      ffn_w_out[l, fc * 128:(fc + 1) * 128, :].rearrange("p (h n) -> p h n", h=2),
                      writes=[f"wo_{fc}"])
                for th in range(2):
                    pg = (fc * 2 + th) % 2
                    bg, bu = pg * 2, pg * 2 + 1
                    hreads = [f"hT{t}" for t in range(th * 4, th * 4 + 4)]
                    for kc in range(8):
                        S.op("pe", "matmul", reads=hreads + [f"wi{i}"], writes=[f"ps{bg}"],
                             out=ps[bg][:, :], lhsT=wi[i][:, kc, 0:128], rhs=hT[:, kc, th * 512:(th + 1) * 512],
                             start=(kc == 0), stop=(kc == 7))
                    for kc in range(8):
                        S.op("pe", "matmul", reads=hreads + [f"wi{i}"], writes=[f"ps{bu}"],
                             out=ps[bu][:, :], lhsT=wi[i][:, kc, 128:256], rhs=hT[:, kc, th * 512:(th + 1) * 512],
                             start=(kc == 0), stop=(kc == 7))
                    S.op("act", "activation", reads=[f"ps{bg}"], writes=[f"sg{pg}"], out=sg[pg], in_=ps[bg][:, :],
                         func=AF.Silu)
                    S.op("dve", "tensor_tensor", reads=[f"sg{pg}", f"ps{bu}"], writes=[f"hid{fc}_{th}"],
                         out=hid[:, fc, th * 512:(th + 1) * 512], in0=sg[pg], in1=ps[bu][:, :], op=ALU.mult)
                if mid_hook is not None:
                    mid_hook(fc)
            if post_AB is not None:
                post_AB()
            for t in range(NT):
                th = t // 4
                b0 = (t % 2) * 2
                for dh in range(2):
                    for fc in range(NFC):
                        S.op("pe", "matmul", reads=[f"hid{fc}_{th}", f"wo_{fc}"], writes=[f"ps{b0 + dh}"],
                             out=ps[b0 + dh][:, :], lhsT=hid[:, fc, t * 128:(t + 1) * 128], rhs=wo[:, dh, fc, :],
                             start=(fc == 0), stop=(fc == NFC - 1))
                ln_epilogue(t, [f"ps{b0}", f"ps{b0 + 1}"], [(ps[b0][:, :], 0, 512), (ps[b0 + 1][:, :], 512, 1024)],
                            1, not final, 4 + (t % 2),
                            out_dram=(y_out[t * 128:(t + 1) * 128, :] if final else None))

        l1_blocks = [(nb, hf) for nb in range(5) for hf in range(2)]

        def hook0(fc):
            if fc < 10:
                nb, hf = l1_blocks[fc]
                mod_halfblock(1, nb, hf)
            if fc == 10:
                mod_finish(1)

        ffn(0, hook0, lambda: make_AB(0, 2, 1, 0, 1), False)

        if stop == 2:
            for t_ in range(NT):
                S.dma("sp", y_out[t_ * 128:(t_ + 1) * 128, :], x[:, t_, :], reads=[f"x{t_}"], writes=[f"yout{t_}"])
            S.emit(es)
            return nc
        S.fence()
        apos[0] = 0
        ynT = carve(16 * 2048, BF16, "p (c n) -> p c n", c=16)
        mark_grp = apos[0]
        wob_early = arena[:, mark_grp:mark_grp + 8192].bitcast(BF16).rearrange("p (c n) -> p c n", c=16)
        W = carve(8 * 776 * 2, BF16, "p (k n) -> p k n", k=8)
        wname = "wg"
        xTg = carve(2 * 2048, BF16, "p (c n) -> p c n", c=2)
        BCT = carve(2 * 2048, BF16, "p (c n) -> p c n", c=2)
        xs_tm = carve(NT * 512, BF16, "p (t n) -> p t n", t=NT)
        B_tm = carve(NT * 256, BF16, "p (t n) -> p t n", t=NT)
        siluz = carve(NT * 512, BF16, "p (t n) -> p t n", t=NT)
        cacc = carve(2048, F32)
        hTh = carve(8 * 4 * 2, BF16, "p (k n) -> p k n", k=8)
        x1g = pre[0:12, :]
        selb = sb("selb", [12, 3])
        cwb = carve(32 * 5 * 4, F32, "p (c k) -> p c k", c=32)
        dtb = carve(64 * 4, F32)
        abc = carve(64 * 4, F32)
        dbc = carve(32 * 4, F32)
        nwg = carve(256 * 4, F32)
        dtt = carve(NT * 8 * 4, F32, "p (t n) -> p t n", t=NT)
        smA = carve(NT * 16 * 4, F32, "p (t n) -> p t n", t=NT)
        nacsA, eaA, dteA, cdecA, w2A, daA = [carve(NT * 8 * 4, F32, "p (t n) -> p t n", t=NT) for _ in range(6)]
        dahl = carve(NT * 16 * 2, BF16, "p (t n) -> p t n", t=NT)
        spt = carve(4 * 64 * 4, F32)
        xsdt = [carve(512, BF16) for _ in range(2)]
        xsst = [carve(512, BF16) for _ in range(2)]
        GT = carve(512, F32)
        Lsum = [pre[:, 0:512], carve(2048, F32)]
        GL4 = [xnbf[:, 0:512], xnbf[:, 512:1024]]
        S_sb = carve(NT * 2 * 1024, F32, "p (t d n) -> p t d n", t=NT, d=2)
        ydiag = carve(NT * 1024, F32, "p (t n) -> p t n", t=NT)
        Hst = [carve(1024, F32) for _ in range(2)]
        Hbf = [carve(512, BF16) for _ in range(2)]
        Zst = carve(2 * 1040, F32, "p (d n) -> p d n", d=2)
        gat = carve(8 * 1040, F32, "p (j d n) -> p j d n", j=4, d=2)
        h0g = carve(2 * 1024, F32, "p (d n) -> p d n", d=2)
        aj = carve(32 * 4, F32, "p (j d r) -> p j d r", j=4, d=2)
        ysum = gate_bc[:, 1, 0, :]
        yz = gate_bc[:, 1, 1, :]
        ynb = lnbc[:, 0, :].bitcast(BF16)[:, 0:1024]
        sq = carve(1024, F32)
        ytmp = [carve(1024, F32) for _ in range(2)]
        lnbf = lnbc[:, 0, :].bitcast(BF16)
        GTb = [GT, sq[:, 0:128]]
        xsdtb = [xsdt, [carve(512, BF16) for _ in range(2)]]
        xsstb = [xsst, [carve(512, BF16) for _ in range(2)]]
        Lsumb = [Lsum, [lnbc[:, 1, 0:512], lnbc[:, 1, 512:1024]]]
        GL4b = [GL4, [lnbf[:, 1024:1536], lnbf[:, 1536:2048]]]

        print("SSD arena bytes used", apos[0] * 4, "of", ARENA * 4)
        S.dma("sp", cwb[:], cwl, writes=["cwb"])
        S.dma("sp", dtb, dt_bias[0].rearrange("d h -> (d h)").partition_broadcast(128), writes=["dtb"])
        S.dma("sp", abc, a_log[0].rearrange("d h -> (d h)").partition_broadcast(128), writes=["abc"])
        S.dma("sp", dbc, ssd_d[0, :].partition_broadcast(128), writes=["dbc"])
        S.dma("sp", selb[:], sel, writes=["selb"])
        S.op("act", "activation", reads=["abc"], writes=["abc"], out=abc, in_=abc, func=AF.Exp)
        S.op("dve", "tensor_scalar", reads=["abc"], writes=["abc"], out=abc, in0=abc, scalar1=-1.0, scalar2=None,
             op0=ALU.mult)

        def wsl(c0, n):
            return w_in[0, :, c0:c0 + n].rearrange("(kc p) n -> p kc n", p=128)

        def load_wg(g):
            S.dma("pool", W[:, :, 0:256], wsl(DI + g * 256, 256), writes=[wname])
            S.dma("pool", W[:, :, 256:384], wsl(2 * DI + g * 128, 128), writes=[wname])
            S.dma("pool", W[:, :, 384:512], wsl(2 * DI + 1024 + g * 128, 128), writes=[wname])
            S.dma("pool", W[:, :, 512:768], wsl(g * 256, 256), writes=[wname])
            for d in range(2):
                S.dma("pool", W[:, :, 768 + d * 4:772 + d * 4], wsl(DI + CONV_CH + d * 32 + g * 4, 4), writes=[wname])

        load_wg(0)
        S.dma("sp", ag1_src[0:1, :], x[0:1, 4, :], reads=["x4"], writes=["ag1_src"])
        S.dma("sp", ag1_src[1:3, :], x[126:128, 7, :], reads=["x7"], writes=["ag1_src"])
        S.cc(reads=["ag1_src"], writes=["ag1_dst"], kind="AllGather", op=ALU.bypass, replica_groups=GROUPS4,
             ins=[ag1_src.opt()], outs=[ag1_dst.opt()])
        def emit_halo():
            S.dma("sp", x1g, ag1_dst, reads=["ag1_dst"], writes=["pre"])
            sc1h = sb("sc1h", [128, 8])
            S.op("dve", "tensor_scalar", reads=["modc1"], writes=["sc1h"], out=sc1h[:], in0=modc[:, 1, 1, :, 1],
                 scalar1=1.0, scalar2=None, op0=ALU.add)
            for kc in range(8):
                S.op("pe", "matmul", reads=["pre", "selb"], writes=["ps7"], out=ps[7][:, 64 + kc * 4:64 + kc * 4 + 3],
                     lhsT=x1g[:, kc * 128:(kc + 1) * 128], rhs=selb[:], start=True, stop=True)
            for kc in range(8):
                S.op("dve", "tensor_scalar", reads=["ps7", "sc1h", "modc1"], writes=["hTh32"],
                     out=sq[:, kc * 4:kc * 4 + 3], in0=ps[7][:, 64 + kc * 4:64 + kc * 4 + 3],
                     scalar1=sc1h[:, kc:kc + 1], scalar2=modc[:, 1, 0, kc, 1:2], op0=ALU.mult, op1=ALU.add)
                S.op("dve", "tensor_tensor", reads=["hTh32", "maskbc"], writes=["hTh"], out=hTh[:, kc, 0:3],
                     in0=sq[:, kc * 4:kc * 4 + 3], in1=maskbc[:, 16:19], op=ALU.mult)

        import os as _os2
        KSUB = int(_os2.environ.get("KSUB", "0"))
        KSKIP = _os2.environ.get("KSKIP", "")
        if KSUB == 1:
            for t_ in range(NT):
                S.dma("sp", y_out[t_ * 128:(t_ + 1) * 128, :], x[:, t_, :], reads=[f"x{t_}"], writes=[f"yout{t_}"])
            S.emit(es)
            return nc
        make_AB(1, 1, 1, 2, 3)

        def bc4(ap4, n=64):
            return ap4.unsqueeze(2).to_broadcast([128, 4, n])

        def v3(ap, n=64):
            return ap.rearrange("p (r e) -> p r e", e=n)

        import os as _os
        NG = int(_os.environ.get("KGROUPS", "8"))
        for g in range(NG):
            S.dma("sp", nwg, norm_w[0, g * 256:(g + 1) * 256].partition_broadcast(128), writes=["nwg"])
            for th, ci in [(th_, ci_) for th_ in range(2) for ci_ in range(4)]:
                chg = (g * 2 + ci) if ci < 2 else (16 + g if ci == 2 else 24 + g)
                if g == 0 and th == 1 and ci == 0:
                    emit_halo()
                if True:
                    bk = ci % 2
                    hreads = [f"hT{t}" for t in range(th * 4, th * 4 + 4)]
                    for kc in range(8):
                        S.op("pe", "matmul", reads=hreads + [wname], writes=[f"ps{bk}"], out=ps[bk][:, :],
                             lhsT=W[:, kc, ci * 128:(ci + 1) * 128], rhs=hT[:, kc, th * 512:(th + 1) * 512],
                             start=(kc == 0), stop=(kc == 7))
                    if th == 1 and 'h' not in KSKIP:
                        for kc in range(8):
                            S.op("pe", "matmul", reads=["hTh", wname], writes=["ps7h"], out=ps[7][:, 128:131],
                                 lhsT=W[:, kc, ci * 128:(ci + 1) * 128], rhs=hTh[:, kc, 0:3],
                                 start=(kc == 0), stop=(kc == 7))
                    P = ps[bk]
                    cw = cwb[:, chg, :]
                    S.op("act", "activation", reads=[f"ps{bk}", "cwb"], writes=["cacc"], out=cacc, in_=P[:, :],
                         func=AF.Identity, scale=cw[:, 2:3])
                    segs = [(0, 256), (256, 256)] if th == 0 else [(0, 512)]
                    for (o, L) in segs:
                        for k, s_ in ((0, -2), (1, -1), (3, 1)):
                            lo = o + max(0, -s_)
                            hi = o + min(L, L - s_)
                            S.op("dve", "scalar_tensor_tensor", reads=[f"ps{bk}", "cwb", "cacc"], writes=["cacc"],
                                 out=cacc[:, lo:hi], in0=P[:, lo + s_:hi + s_], scalar=cw[:, k:k + 1], in1=cacc[:, lo:hi],
                                 op0=ALU.mult, op1=ALU.add)
                    if th == 1 and 'h' not in KSKIP:
                        Hh = ps[7]
                        for (dst, hc, k) in ((0, 128, 0), (1, 129, 0), (0, 129, 1), (511, 130, 3)):
                            S.op("dve", "scalar_tensor_tensor", reads=["ps7h", "cwb", "cacc"], writes=["cacc"],
                                 out=cacc[:, dst:dst + 1], in0=Hh[:, hc:hc + 1], scalar=cw[:, k:k + 1],
                                 in1=cacc[:, dst:dst + 1], op0=ALU.mult, op1=ALU.add)
                    if ci < 2:
                        dst, dname = xTg[:, ci, th * 512:(th + 1) * 512], f"xTg{ci}_{th}"
                    else:
                        dst, dname = BCT[:, ci - 2, th * 512:(th + 1) * 512], f"BCT{ci - 2}_{th}"
                    S.op("act", "activation", reads=["cacc", "cwb"], writes=[dname], out=dst, in_=cacc, func=AF.Silu,
                         bias=cw[:, 4:5], scale=1.0)
            for t in (range(NT) if 't' not in KSKIP else []):
                th = t // 4
                tb = 6 if t % 2 == 0 else 4
                pb = psbf(tb)
                for ci in range(2):
                    S.op("pe", "transpose", reads=[f"xTg{ci}_{th}", "cmb"], writes=[f"ps{tb}"],
                         out=pb[:, ci * 128:(ci + 1) * 128], in_=xTg[:, ci, t * 128:(t + 1) * 128], identity=ident_b)
                S.op("pe", "transpose", reads=[f"BCT0_{th}", "cmb"], writes=[f"ps{tb}"],
                     out=pb[:, 256:384], in_=BCT[:, 0, t * 128:(t + 1) * 128], identity=ident_b)
                S.op("act", "activation", reads=[f"ps{tb}"], writes=[f"xs_tm{t}"], out=xs_tm[:, t, :], in_=pb[:, 0:256],
                     func=AF.Identity)
                S.op("act", "activation", reads=[f"ps{tb}"], writes=[f"B_tm{t}"], out=B_tm[:, t, :], in_=pb[:, 256:384],
                     func=AF.Identity)
            vv, av, ev, dtraw = spt[:, 0:64], spt[:, 64:128], spt[:, 128:192], spt[:, 192:256]
            t8 = lambda ap: ap.rearrange("p (t n) -> p t n", t=NT)
            for t in range(NT):
                zb = 2 + t % 2
                for kc in range(8):
                    S.op("pe", "matmul", reads=[f"hT{t}", wname], writes=[f"ps{zb}"], out=ps[zb][:, 0:264],
                         lhsT=hT[:, kc, t * 128:(t + 1) * 128], rhs=W[:, kc, 512:776], start=(kc == 0), stop=(kc == 7))
                S.op("act", "activation", reads=[f"ps{zb}"], writes=[f"siluz{t}"], out=siluz[:, t, :],
                     in_=ps[zb][:, 0:256], func=AF.Silu)
                S.op("act", "activation", reads=[f"ps{zb}"], writes=["dtraw"], out=dtraw[:, t * 8:(t + 1) * 8],
                     in_=ps[zb][:, 256:264], func=AF.Identity)
            for d in range(2):
                S.op("dve", "tensor_tensor", reads=["dtraw", "dtb"], writes=["spt_v"], out=t8(vv)[:, :, d * 4:d * 4 + 4],
                     in0=t8(dtraw)[:, :, d * 4:d * 4 + 4],
                     in1=dtb[:, d * 32 + g * 4:d * 32 + g * 4 + 4].unsqueeze(1).to_broadcast([128, NT, 4]), op=ALU.add)
            S.op("dve", "scalar_tensor_tensor", reads=["spt_v"], writes=["spt_a"], out=av, in0=vv, scalar=-1.0,
                 in1=vv, op0=ALU.mult, op1=ALU.max)
            S.op("act", "activation", reads=["spt_a"], writes=["spt_e"], out=ev, in_=av, func=AF.Exp, scale=-1.0)
            S.op("act", "activation", reads=["spt_e"], writes=["spt_e"], out=ev, in_=ev, func=AF.Ln, bias=1.0, scale=1.0)
            S.op("dve", "scalar_tensor_tensor", reads=["spt_v", "spt_e"], writes=["dtt"],
                 out=dtt[:].rearrange("p t n -> p (t n)"), in0=vv, scalar=0.0, in1=ev, op0=ALU.max, op1=ALU.add)
            if KSUB == 2:
                for t_ in range(NT):
                    S.dma("sp", y_out[t_ * 128:(t_ + 1) * 128, :], x[:, t_, :], reads=[f"x{t_}"], writes=[f"yout{t_}"])
                S.emit(es)
                return nc
            if g + 1 < NG:
                load_wg(g + 1)
            else:
                S.dma("pool", wob_early[:, 0:8, :], w_out[0, 0:1024, :].rearrange("(c p) n -> p c n", p=128),
                      writes=["wg", "xTg0_0", "xTg0_1", "xTg1_0", "xTg1_1", "wob_lo"])
            fl = lambda ap: ap.rearrange("p t n -> p (t n)")
            for d in range(2):
                S.op("dve", "tensor_tensor", reads=["dtt", "abc"], writes=["da"], out=daA[:, :, d * 4:d * 4 + 4],
                     in0=dtt[:, :, d * 4:d * 4 + 4],
                     in1=abc[:, d * 32 + g * 4:d * 32 + g * 4 + 4].unsqueeze(1).to_broadcast([128, NT, 4]), op=ALU.mult)
            for t in range(NT):
                for d in range(2):
                    S.op("pe", "matmul", reads=["da", "cmf"], writes=["ps7a"],
                         out=ps[7][:, 256 + t * 16 + d * 4:256 + t * 16 + d * 4 + 4],
                         lhsT=cmf[:, 1 + d, :], rhs=daA[:, t, d * 4:d * 4 + 4], start=True, stop=True)
                S.op("pe", "matmul", reads=["da", "cmf"], writes=["ps7a"], out=ps[7][:, 256 + t * 16 + 8:256 + t * 16 + 16],
                     lhsT=ones_f, rhs=daA[:, t, :], start=True, stop=True)
            S.op("dve", "tensor_copy", reads=["ps7a"], writes=["smA"], out=fl(smA), in_=ps[7][:, 256:384])
            S.op("dve", "tensor_scalar", reads=["smA"], writes=["nacs"], out=nacsA[:], in0=smA[:, :, 0:8], scalar1=-1.0,
                 scalar2=None, op0=ALU.mult)
            S.op("act", "activation", reads=["smA"], writes=["ea"], out=eaA[:], in_=smA[:, :, 0:8], func=AF.Exp)
            S.op("dve", "tensor_tensor", reads=["smA"], writes=["dte"], out=dteA[:], in0=smA[:, :, 8:16], in1=smA[:, :, 0:8],
                 op=ALU.subtract)
            S.op("act", "activation", reads=["dte"], writes=["dte"], out=dteA[:], in_=dteA[:], func=AF.Exp)
            S.op("act", "activation", reads=["smA"], writes=["cdec"], out=cdecA[:], in_=smA[:, :, 8:16], func=AF.Exp)
            S.op("dve", "tensor_tensor", reads=["dte", "dtt"], writes=["w2"], out=w2A[:], in0=dteA[:], in1=dtt[:], op=ALU.mult)
            S.op("dve", "tensor_copy", reads=["da"], writes=["dahl"], out=dahl[:, :, 0:8], in_=daA[:])
            S.op("dve", "tensor_tensor", reads=["da", "dahl"], writes=["dahl"], out=dahl[:, :, 8:16], in0=daA[:],
                 in1=dahl[:, :, 0:8], op=ALU.subtract)
            def diag_steps(t, par):
                th = t // 4
                tsl = slice(t * 128, (t + 1) * 128)
                GTp, gtn = GTb[par], f"GT{par}"
                yb = 3 if par == 0 else 7
                gtps = ps[3][:, 256:384] if par == 0 else ps[7][:, 384:512]
                lbs = (4, 2) if par == 0 else (0, 6)
                xd, xs_ = xsdtb[par], xsstb[par]

                def stepA():
                    S.op("pe", "matmul", reads=[f"BCT0_{th}", f"BCT1_{th}"], writes=[f"ps{yb}"], out=gtps,
                         lhsT=BCT[:, 0, tsl], rhs=BCT[:, 1, tsl], start=True, stop=True)
                    S.op("act", "activation", reads=[f"ps{yb}"], writes=[gtn], out=GTp, in_=gtps, func=AF.Identity)
                    for d in range(2):
                        S.op("pool", "tensor_tensor", reads=[f"xs_tm{t}", "dtt"], writes=[f"xsdt{par}{d}"],
                             out=v3(xd[d]), in0=v3(xs_tm[:, t, :]), in1=bc4(dtt[:, t, d * 4:d * 4 + 4]), op=ALU.mult)
                        S.op("dve", "tensor_tensor", reads=[f"xs_tm{t}", "w2"], writes=[f"xsst{par}{d}"],
                             out=v3(xs_[d]), in0=v3(xs_tm[:, t, :]), in1=bc4(w2A[:, t, d * 4:d * 4 + 4]), op=ALU.mult)
                        sbk = 5 if d == 0 else 1
                        S.op("pe", "matmul", reads=[f"B_tm{t}", f"xsst{par}{d}"], writes=[f"ps{sbk}"],
                             out=ps[sbk][:, 0:256], lhsT=B_tm[:, t, :], rhs=xs_[d], start=True, stop=True)
                        S.op("act", "activation", reads=[f"ps{sbk}"], writes=[f"S{t}_{d}"], out=S_sb[:, t, d, :],
                             in_=ps[sbk][:, 0:256], func=AF.Identity)

                def stepB():
                    for d in range(2):
                        lb = lbs[d]
                        lpn = f"ps{lb}"
                        for r in range(4):
                            col = d * 4 + r
                            lp = ps[lb][:, r * 128:(r + 1) * 128]
                            S.op("pe", "matmul", reads=["dahl", "cmb"], writes=[lpn], out=lp,
                                 lhsT=dahl[:, t, col:col + 1].to_broadcast([128, 128]), rhs=cmb[:, 1 + d, :],
                                 start=True, stop=False)
                            S.op("pe", "matmul", reads=["dahl", "cmb"], writes=[lpn], out=lp,
                                 lhsT=dahl[:, t, 8 + col:9 + col].to_broadcast([128, 128]), rhs=cmb[:, 1 + d, :],
                                 start=False, stop=False)
                            S.op("pe", "matmul", reads=["cmb"], writes=[lpn], out=lp, lhsT=ident_b, rhs=cmb[:, 3 + d, :],
                                 start=False, stop=True)

                def stepC(d):
                    def f():
                        lb = lbs[d]
                        lpn = f"ps{lb}"
                        Ls, lsn = Lsumb[par][d], f"Lsum{par}{d}"
                        G4, gln = GL4b[par][d], f"GL{par}{d}"
                        S.op("dve", "tensor_tensor", reads=[lpn, "nacs"], writes=[lsn], out=v3(Ls, 128),
                             in0=v3(ps[lb][:, :], 128), in1=bc4(nacsA[:, t, d * 4:d * 4 + 4], 128), op=ALU.add)
                        S.op("act", "activation", reads=[lsn], writes=[lsn], out=Ls, in_=Ls, func=AF.Exp)
                        S.op("dve", "tensor_tensor", reads=[lsn, gtn], writes=[gln], out=v3(G4, 128),
                             in0=v3(Ls, 128), in1=GTp.unsqueeze(1).to_broadcast([128, 4, 128]), op=ALU.mult)
                        for r in range(4):
                            S.op("pe", "matmul", reads=[gln, f"xsdt{par}{d}"], writes=[f"ps{yb}"],
                                 out=ps[yb][:, r * 64:(r + 1) * 64], lhsT=G4[:, r * 128:(r + 1) * 128],
                                 rhs=xd[d][:, r * 64:(r + 1) * 64],
                                 start=(d == 0 and r == 0), stop=(d == 1 and r == 3), skip_group_check=True)
                    return f

                def stepE():
                    S.op("act", "activation", reads=[f"ps{yb}"], writes=[f"ydiag{t}"], out=ydiag[:, t, :],
                         in_=ps[yb][:, 0:256], func=AF.Identity)

                return [stepA, stepB, stepC(0), stepC(1), stepE]

            def diag_pair(ta, tb):
                sa, sb_ = diag_steps(ta, 0), diag_steps(tb, 1)
                for fa, fb in zip(sa, sb_):
                    fa()
                    fb()

            def chain_seq(si, tiles, ext):
                orders = [tiles, tiles[::-1]]
                have = [False, False]
                if ext:
                    for d in range(2):
                        S.op("dve", "tensor_copy", reads=["h0g"], writes=[f"Hst{d}"], out=Hst[d], in_=h0g[:, d, :])
                    for step in range(4):
                        for d in range(2):
                            j = step if d == 0 else 3 - step
                            Hs, hn = Hst[d], f"Hst{d}"
                            mcol = maskbc[:, d * 4 + j:d * 4 + j + 1]
                            S.op("dve", "tensor_tensor", reads=[hn, "aj"], writes=[hn], out=v3(Hs), in0=v3(Hs),
                                 in1=bc4(aj[:, j, d, :]), op=ALU.mult)
                            S.op("dve", "scalar_tensor_tensor", reads=["gat", "maskbc", hn], writes=[hn], out=Hs,
                                 in0=gat[:, j, d, 0:256], scalar=mcol, in1=Hs, op0=ALU.mult, op1=ALU.add)
                    have = [True, True]
                for n in range(len(tiles)):
                    for d in range(2):
                        t = orders[d][n]
                        th = t // 4
                        Hs, hn = Hst[d], f"Hst{d}"
                        if have[d]:
                            S.op("act", "activation", reads=[hn], writes=[f"Hbf{d}"], out=Hbf[d], in_=Hs, func=AF.Identity)
                            sbk = 5 if d == 0 else 1
                            S.op("pe", "matmul", reads=[f"BCT1_{th}", f"Hbf{d}"], writes=[f"ps{sbk}"],
                                 out=ps[sbk][:, 0:256], lhsT=BCT[:, 1, t * 128:(t + 1) * 128],
                                 rhs=Hbf[d], start=True, stop=True)
                            S.op("dve", "tensor_tensor", reads=[f"ps{sbk}", "ea"], writes=[f"ytmp{d}"], out=v3(ytmp[d]),
                                 in0=v3(ps[sbk][:, 0:256]), in1=bc4(eaA[:, t, d * 4:d * 4 + 4]),
                                 op=ALU.mult)
                            S.op("dve", "tensor_tensor", reads=[f"ytmp{d}", f"ydiag{t}"], writes=[f"ydiag{t}"],
                                 out=ydiag[:, t, :], in0=ydiag[:, t, :], in1=ytmp[d], op=ALU.add)
                            if n < len(tiles) - 1 or not ext:
                                S.op("dve", "tensor_tensor", reads=[hn, "cdec"], writes=[hn], out=v3(Hs), in0=v3(Hs),
                                     in1=bc4(cdecA[:, t, d * 4:d * 4 + 4]), op=ALU.mult)
                                S.op("dve", "tensor_tensor", reads=[hn, f"S{t}_{d}"], writes=[hn], out=Hs, in0=Hs,
                                     in1=S_sb[:, t, d, :], op=ALU.add)
                        else:
                            S.op("dve", "tensor_copy", reads=[f"S{t}_{d}"], writes=[hn], out=Hs, in_=S_sb[:, t, d, :])
                            have[d] = True
                if not ext:
                    for d in range(2):
                        S.dma("sp", st_out[si, d, :, g * 256:(g + 1) * 256], Hst[d], reads=[f"Hst{d}"],
                              writes=[f"st{si}{d}{g}"])
                nt = len(tiles)
                t0 = tiles[0]
                W_ = nt * 256
                yd = ydiag[:, t0:t0 + nt, :]
                ydn = [f"ydiag{t}" for t in tiles]
                ys_, yz_ = ysum[:, 0:W_], yz[:, 0:W_]
                r4 = lambda ap: ap.rearrange("p (t r e) -> p t r e", t=nt, e=64)
                S.op("dve", "tensor_tensor", reads=[f"xs_tm{t}" for t in tiles] + ["dbc"], writes=["ysum"],
                     out=r4(ys_), in0=xs_tm[:, t0:t0 + nt, :].rearrange("p t (r e) -> p t r e", e=64),
                     in1=dbc[:, g * 4:g * 4 + 4].unsqueeze(1).unsqueeze(3).to_broadcast([128, nt, 4, 64]), op=ALU.mult)
                S.op("dve", "tensor_tensor", reads=["ysum"] + ydn, writes=["ysum"], out=ys_, in0=ys_,
                     in1=yd.rearrange("p t n -> p (t n)"), op=ALU.add)
                S.op("dve", "tensor_tensor", reads=["ysum"] + [f"siluz{t}" for t in tiles], writes=["yz"], out=yz_, in0=ys_,
                     in1=siluz[:, t0:t0 + nt, :].rearrange("p t n -> p (t n)"), op=ALU.mult)
                for n in range(nt):
                    S.op("act", "activation", reads=["yz"], writes=["sq", "stat_ss"], out=sq[:, 0:256],
                         in_=yz[:, n * 256:(n + 1) * 256], func=AF.Square, accum_out=stat[:, 8 + n:9 + n])
                S.op("act", "activation", reads=["stat_ss", "epsc"], writes=["stat_rs"], out=stat[:, 12:12 + nt],
                     in_=stat[:, 8:8 + nt], func=AF.Sqrt, bias=epsc[:, 1:2], scale=1.0 / 256.0)
                S.op("dve", "reciprocal", reads=["stat_rs"], writes=["stat_rr"], out=stat[:, 16:16 + nt], in_=stat[:, 12:12 + nt])
                y3 = lambda ap: ap.rearrange("p (t n) -> p t n", t=nt)
                S.op("dve", "tensor_tensor", reads=["yz", "stat_rr"], writes=["yz"], out=y3(yz_), in0=y3(yz_),
                     in1=stat[:, 16:16 + nt].unsqueeze(2).to_broadcast([128, nt, 256]), op=ALU.mult)
                S.op("dve", "tensor_tensor", reads=["yz", "nwg"], writes=["ynb"], out=y3(ynb[:, 0:W_]), in0=y3(yz_),
                     in1=nwg.unsqueeze(1).to_broadcast([128, nt, 256]), op=ALU.mult)
                pb = psbf(6)
                for n in range(nt):
                    for cc in range(2):
                        S.op("pe", "transpose", reads=["ynb", "cmb"], writes=["ps6"],
                             out=pb[:, (n * 2 + cc) * 128:(n * 2 + cc + 1) * 128],
                             in_=ynb[:, n * 256 + cc * 128:n * 256 + (cc + 1) * 128], identity=ident_b)
                for cc in range(2):
                    S.op("act", "activation", reads=["ps6"], writes=[f"ynT{g}_{t}" for t in tiles],
                         out=ynT[:, 2 * g + cc, t0 * 128:(t0 + nt) * 128].rearrange("p (t n) -> p t n", t=nt),
                         in_=pb[:, 0:W_].rearrange("p (t c n) -> p t c n", t=nt, c=2)[:, :, cc, :], func=AF.Identity)

            stiles = SEQS[2][0]
            diag_pair(4, 5)
            diag_pair(6, 7)
            if KSUB == 3:
                for t_ in range(NT):
                    S.dma("sp", y_out[t_ * 128:(t_ + 1) * 128, :], x[:, t_, :], reads=[f"x{t_}"], writes=[f"yout{t_}"])
                S.emit(es)
                return nc
            for d in range(2):
                order = stiles if d == 0 else stiles[::-1]
                Z = Zst[:, d, 0:256]
                LD = Zst[:, d, 256:260]
                for n, t in enumerate(order):
                    if n == 0:
                        S.op("dve", "tensor_copy", reads=[f"S{t}_{d}"], writes=["Zst"], out=Z, in_=S_sb[:, t, d, :])
                        S.op("dve", "tensor_copy", reads=["smA"], writes=["Zst"], out=LD,
                             in_=smA[:, t, 8 + d * 4:12 + d * 4])
                    else:
                        S.op("dve", "tensor_tensor", reads=["Zst", "cdec"], writes=["Zst"], out=v3(Z), in0=v3(Z),
                             in1=bc4(cdecA[:, t, d * 4:d * 4 + 4]), op=ALU.mult)
                        S.op("dve", "tensor_tensor", reads=["Zst", f"S{t}_{d}"], writes=["Zst"], out=Z, in0=Z,
                             in1=S_sb[:, t, d, :], op=ALU.add)
                        S.op("dve", "tensor_tensor", reads=["Zst", "smA"], writes=["Zst"], out=LD, in0=LD,
                             in1=smA[:, t, 8 + d * 4:12 + d * 4], op=ALU.add)
            S.dma("sp", ag2_src[g].rearrange("(d n) c -> n d c", d=2), Zst, reads=["Zst"], writes=[f"ag2s{g}"])
            S.cc(reads=[f"ag2s{g}"], writes=[f"ag2d{g}"], kind="AllGather", op=ALU.bypass, replica_groups=GROUPS4,
                 ins=[ag2_src[g].opt()], outs=[ag2_dst[g].opt()])
            S.dma("sp", gat, ag2_dst[g].rearrange("(j d n) c -> n j d c", j=4, d=2), reads=[f"ag2d{g}"], writes=["gat"])
            S.dma("sp", h0g, h0T[:, :, g * 256:(g + 1) * 256].rearrange("d n c -> n d c"), writes=["h0g"])
            for si in range(2):
                diag_pair(*SEQS[si][0])
                chain_seq(si, SEQS[si][0], False)
            if KSUB == 5:
                for t_ in range(NT):
                    S.dma("sp", y_out[t_ * 128:(t_ + 1) * 128, :], x[:, t_, :], reads=[f"x{t_}"], writes=[f"yout{t_}"])
                S.emit(es)
                return nc
            mjd = lambda o: maskbc[:, o:o + 8].rearrange("p (d j) -> p j d", d=2).unsqueeze(3).to_broadcast([128, 4, 2, 4])
            S.op("act", "activation", reads=["gat"], writes=["aj"], out=aj, in_=gat[:, :, :, 256:260], func=AF.Exp)
            S.op("dve", "tensor_tensor", reads=["aj", "maskbc"], writes=["aj"], out=aj, in0=aj, in1=mjd(0), op=ALU.mult)
            S.op("dve", "tensor_tensor", reads=["aj", "maskbc"], writes=["aj"], out=aj, in0=aj, in1=mjd(8), op=ALU.add)
            chain_seq(2, stiles, True)

        S.fence()
        apos[0] = mark_grp
        wob = carve(16 * 2048, BF16, "p (c n) -> p c n", c=16)
        S.dma("pool", wob[:, 8:16, :], w_out[0, 1024:2048, :].rearrange("(c p) n -> p c n", p=128), writes=["wob_hi"])
        load_lnbc(1, 1)
        for t in range(NT):
            b0 = (t % 2) * 2
            for dh in range(2):
                for c16 in range(16):
                    S.op("pe", "matmul", reads=[f"ynT{c16 // 2}_{t}", "wob_lo" if c16 < 8 else "wob_hi"], writes=[f"ps{b0 + dh}"],
                         out=ps[b0 + dh][:, :], lhsT=ynT[:, c16, t * 128:(t + 1) * 128],
                         rhs=wob[:, c16, dh * 512:(dh + 1) * 512], start=(c16 == 0), stop=(c16 == 15))
            ln_epilogue(t, [f"ps{b0}", f"ps{b0 + 1}"], [(ps[b0][:, :], 0, 512), (ps[b0 + 1][:, :], 512, 1024)],
                        0, True, 4 + (t % 2))

        if stop == 3:
            for t_ in range(NT):
                S.dma("sp", y_out[t_ * 128:(t_ + 1) * 128, :], x[:, t_, :], reads=[f"x{t_}"], writes=[f"yout{t_}"])
            S.emit(es)
            return nc
        def hook1(fc):
            if fc in (2, 3):
                mod_halfblock(1, 5, fc - 2)

        ffn(1, hook1, None, True)

        S.emit(es)
    return nc


def _win(n, k):
    left = k // 2
    right = k - 1 - left
    pos = np.arange(n)
    lo = np.clip(pos - left, 0, n)
    hi = np.clip(pos + right + 1, 0, n)
    return lo, hi


def _host_consts():
    ar = np.arange(128)
    ident = np.eye(128, dtype=np.float32)
    tri_f = (ar[:, None] <= ar[None, :]).astype(np.float32)
    tri_b = (ar[:, None] >= ar[None, :]).astype(np.float32)
    mneg_f = np.where(ar[:, None] <= ar[None, :], 0.0, NEG).astype(np.float32)
    mneg_b = np.where(ar[:, None] >= ar[None, :], 0.0, NEG).astype(np.float32)
    ones = np.ones((128, 128), np.float32)
    cm = np.stack([ident, tri_f, tri_b, mneg_f, mneg_b, ones]).astype(np.float32)
    p1 = np.zeros((4, 256, 256), np.float32)
    inv1 = np.zeros((4, 256), np.float32)
    for g, k in enumerate(POOL_K):
        lo, hi = _win(256, k)
        for o in range(256):
            p1[g, lo[o]:hi[o], o] = 1.0
            p1[g, o, o] -= float(hi[o] - lo[o])
            inv1[g, o] = 1.0 / float(hi[o] - lo[o])
    p1 = p1.reshape(4, 2, 128, 256).reshape(8, 128, 256)
    return cm, p1, inv1


def _host_pool2d(q):
    r0 = q * 8 - 8
    blocks = np.zeros((NP2, 128, 512), np.float32)
    inv2 = np.zeros((4, 512), np.float32)
    for g, k in enumerate(POOL_K):
        rlo, rhi = _win(32, k)
        clo, chi = _win(64, k)
        P = np.zeros((1536, 512), np.float32)
        for rr in range(8):
            r = q * 8 + rr
            for c in range(64):
                o = rr * 64 + c
                cnt = float((rhi[r] - rlo[r]) * (chi[c] - clo[c]))
                for r2 in range(rlo[r], rhi[r]):
                    e0 = (r2 - r0) * 64
                    P[e0 + clo[c]:e0 + chi[c], o] = 1.0
                P[(r - r0) * 64 + c, o] -= cnt
                inv2[g, o] = 1.0 / cnt
        lo, hi = P2_TILES[g]
        nz = np.abs(P).reshape(12, 128, 512).sum(axis=(1, 2))
        for tin in range(12):
            if tin < lo or tin >= hi:
                assert nz[tin] == 0.0
        blocks[P2_BLK0[g]:P2_BLK0[g] + hi - lo] = P.reshape(12, 128, 512)[lo:hi]
    return blocks, inv2


_NC_CACHE = {}


def kernel(x_prompt, x_sample, c, state_ssm, c_ctx, w_ada, b_ada, w_pool, pool_scale, ssd_w_in, ssd_conv_w,
           ssd_conv_b, ssd_dt_bias, ssd_a_log, ssd_d, ssd_norm_w, ssd_w_out, ffn_w_in, ffn_w_out,
           ln1_g, ln1_b, ln2_g, ln2_b):
    f = lambda a: np.ascontiguousarray(np.asarray(a, dtype=np.float32))
    x_prompt, x_sample, c, state_ssm, c_ctx = map(f, (x_prompt, x_sample, c, state_ssm, c_ctx))
    shared = dict(w_ada=f(w_ada), b_ada=f(b_ada), w_pool=f(w_pool), pool_scale=f(pool_scale), ssd_w_in=f(ssd_w_in),
                  ssd_dt_bias=f(ssd_dt_bias),
                  ssd_a_log=f(ssd_a_log), ssd_d=f(ssd_d), ssd_norm_w=f(ssd_norm_w), ssd_w_out=f(ssd_w_out),
                  ffn_w_in=f(ffn_w_in), ffn_w_out=f(ffn_w_out), ln1_g=f(ln1_g), ln1_b=f(ln1_b), ln2_g=f(ln2_g),
                  ln2_b=f(ln2_b))
    lnl = [shared[k][l] for l in range(2) for k in ("ln1_g", "ln1_b", "ln2_g", "ln2_b")]
    shared["lncT"] = np.ascontiguousarray(np.stack(lnl).reshape(8, 8, 128).transpose(2, 0, 1))
    cw5 = np.concatenate([f(ssd_conv_w)[0], f(ssd_conv_b)], axis=0)
    shared["cwl"] = np.ascontiguousarray(cw5.reshape(5, 32, 128).transpose(2, 1, 0))
    cm, p1, inv1 = _host_consts()
    pool2 = [_host_pool2d(q) for q in range(4)]
    in_maps = []
    for core in range(8):
        b, q = core // 4, core % 4
        xin = np.concatenate([x_prompt[2 * core].reshape(256, D), x_prompt[2 * core + 1].reshape(256, D),
                              x_sample[b, q * 512:(q + 1) * 512]], axis=0)
        xext = np.zeros((1536, D), np.float32)
        lo = q * 512 - 512
        hi = q * 512 + 1024
        slo, shi = max(lo, 0), min(hi, 2048)
        xext[slo - lo:shi - lo] = x_sample[b, slo:shi]
        cvec = np.ascontiguousarray(np.stack([c_ctx, c[b]]).astype(np.float32).reshape(2, 8, 128).transpose(2, 1, 0))
        h0T = np.ascontiguousarray(state_ssm[b, 0].transpose(0, 3, 1, 2).reshape(2, 128, DI))
        blocks, inv2 = pool2[q]
        invc = np.concatenate([inv1, inv2], axis=1).reshape(1, 4 * 768).astype(np.float32)
        mf = np.array([1.0 if j < q else 0.0 for j in range(4)], np.float32)
        mb = np.array([1.0 if j > q else 0.0 for j in range(4)], np.float32)
        valid = np.array([1.0 if q > 0 else 0.0, 1.0 if q > 0 else 0.0, 1.0 if q < 3 else 0.0], np.float32)
        cmask = np.concatenate([mf, mb, 1 - mf, 1 - mb, valid]).reshape(1, 19).astype(np.float32)
        sel = np.zeros((12, 3), np.float32)
        if q > 0:
            sel[3 * (q - 1) + 1, 0] = 1.0
            sel[3 * (q - 1) + 2, 1] = 1.0
        if q < 3:
            sel[3 * (q + 1) + 0, 2] = 1.0
        m = dict(shared)
        m.update(xin=np.ascontiguousarray(xin), xext=xext, cvecT=cvec, h0T=h0T, cmats=cm, p1=p1, p2=blocks, invc=invc,
                 cmask=cmask, sel=sel)
        in_maps.append(m)
    import os
    stop = int(os.environ.get("KSTOP", "99"))
    if stop not in _NC_CACHE:
        _NC_CACHE[stop] = build_program(stop)
    nc = _NC_CACHE[stop]
    res = run_bass_kernel_spmd(nc, in_maps, core_ids=list(range(8)))
    y_prompt = np.zeros((16, 256, D), np.float32)
    y_sample = np.zeros((2, 2048, D), np.float32)
    new_state = np.zeros((16, 1, 2, 32, 64, 128), np.float32)
    for core in range(8):
        r = res.results[core]
        b, q = core // 4, core % 4
        yo = np.asarray(r["y_out"])
        y_prompt[2 * core] = yo[0:256]
        y_prompt[2 * core + 1] = yo[256:512]
        y_sample[b, q * 512:(q + 1) * 512] = yo[512:1024]
        st = np.asarray(r["st_out"]).reshape(2, 2, 128, 32, 64)
        new_state[2 * core:2 * core + 2, 0] = st.transpose(0, 1, 3, 4, 2)
    return (y_prompt, y_sample, new_state)
```
